# Optimizing a Trainium2 kernel written in Bass

```python
import jax, jax.numpy as jnp
from jax import lax
import numpy as np

D_MODEL = 2048
BATCH = 4
SEQ = 4096
DEPTH = 4

MEM_LEN = 256
N_MIXERS = 2
N_GLA_LAYERS = (DEPTH + 1) // 2
N_SWA_LAYERS = DEPTH // 2
RMS_EPS = 1e-5
NEG_INF = -1e30

X_HEADS = 4
X_HEAD_DIM = D_MODEL // 16
X_WIDTH = X_HEADS * X_HEAD_DIM

MIX_WIDTH = D_MODEL

GLA_HEADS = 4
GLA_DK = D_MODEL // 8
GLA_DV = (MIX_WIDTH - X_WIDTH) // GLA_HEADS
GLA_K = GLA_HEADS * GLA_DK
GLA_V = GLA_HEADS * GLA_DV
GLA_GATE_RANK = 16
GLA_TAU = 16.0
GLA_CHUNK = 64
GLA_IN = 2 * GLA_K + 2 * GLA_V + GLA_GATE_RANK + X_WIDTH

SWA_HEAD_DIM = 64
SWA_Q_HEADS = (MIX_WIDTH - X_WIDTH) // SWA_HEAD_DIM
SWA_GROUP = 8
SWA_KV_HEADS = SWA_Q_HEADS // SWA_GROUP
SWA_WINDOW = 128
SWA_BLOCK = SWA_WINDOW
SWA_IN = SWA_Q_HEADS * SWA_HEAD_DIM + 2 * SWA_KV_HEADS * SWA_HEAD_DIM + X_WIDTH

D_FF = 4 * D_MODEL

kernel_name = 'hybrid_gla_swa_sink_alibi_memxattn_sqrelu'


def rmsnorm(x, g):
    xf = x.astype(jnp.float32)
    y = xf * lax.rsqrt(jnp.mean(xf * xf, axis=-1, keepdims=True) + RMS_EPS)
    return (y * g.astype(jnp.float32)).astype(x.dtype)


def alibi_slopes(n):
    return jnp.exp2(-8.0 * jnp.arange(1, n + 1, dtype=jnp.float32) / n)


def gla_mixer(q, k, v, gate_lr, w_gate_up, b_gate):
    dt = v.dtype
    B, S = q.shape[0], q.shape[1]
    C = GLA_CHUNK
    n = S // C
    pre = (gate_lr @ w_gate_up + b_gate).astype(jnp.float32)
    log_a = jax.nn.log_sigmoid(pre) / GLA_TAU
    log_a = log_a.reshape(B, n, C, GLA_HEADS, GLA_DK)
    qc = q.astype(jnp.float32).reshape(B, n, C, GLA_HEADS, GLA_DK) * (GLA_DK ** -0.5)
    kc = k.astype(jnp.float32).reshape(B, n, C, GLA_HEADS, GLA_DK)
    vc = v.astype(jnp.float32).reshape(B, n, C, GLA_HEADS, GLA_DV)
    b = jnp.cumsum(log_a, axis=2)
    b_last = b[:, :, -1:]
    q_in = qc * jnp.exp(b)
    k_in = kc * jnp.exp(-b)
    k_out = kc * jnp.exp(b_last - b)
    causal = jnp.tril(jnp.ones((C, C), dtype=bool))
    A = jnp.einsum('bnihd,bnjhd->bnhij', q_in, k_in)
    A = jnp.where(causal, A, 0.0)
    o_intra = jnp.einsum('bnhij,bnjhv->bnihv', A, vc)

    def step(state, xs):
        q_n, k_n, v_n, decay_n = xs
        o_n = jnp.einsum('bihd,bhdv->bihv', q_n, state)
        state = decay_n[..., None] * state + jnp.einsum('bjhd,bjhv->bhdv', k_n, v_n)
        return state, o_n

    xs = (jnp.moveaxis(q_in, 1, 0), jnp.moveaxis(k_out, 1, 0), jnp.moveaxis(vc, 1, 0),
          jnp.moveaxis(jnp.exp(b_last[:, :, 0]), 1, 0))
    state0 = jnp.zeros((B, GLA_HEADS, GLA_DK, GLA_DV), jnp.float32)
    _, o_inter = lax.scan(step, state0, xs)
    o = o_intra + jnp.moveaxis(o_inter, 0, 1)
    return o.reshape(B, S, GLA_HEADS, GLA_DV).astype(dt)


def swa_sink_mixer(q, k, v, sinks):
    B, S = q.shape[0], q.shape[1]
    L = SWA_BLOCK
    n = S // L
    qb = q.reshape(B, n, L, SWA_KV_HEADS, SWA_GROUP, SWA_HEAD_DIM)
    pad = ((0, 0), (L, 0), (0, 0), (0, 0))
    kp = jnp.pad(k, pad).reshape(B, n + 1, L, SWA_KV_HEADS, SWA_HEAD_DIM)
    vp = jnp.pad(v, pad).reshape(B, n + 1, L, SWA_KV_HEADS, SWA_HEAD_DIM)
    kb = jnp.concatenate([kp[:, :-1], kp[:, 1:]], axis=2)
    vb = jnp.concatenate([vp[:, :-1], vp[:, 1:]], axis=2)
    s = jnp.einsum('bnqkgd,bnckd->bnkgqc', qb, kb).astype(jnp.float32) * (SWA_HEAD_DIM ** -0.5)
    qpos = jnp.arange(L)[:, None] + L
    kpos = jnp.arange(2 * L)[None, :]
    dist = (qpos - kpos)
    key_abs = (jnp.arange(n)[:, None] - 1) * L + jnp.arange(2 * L)[None, :]
    mask = ((dist >= 0) & (dist < SWA_WINDOW))[None] & (key_abs >= 0)[:, None, :]
    slopes = alibi_slopes(SWA_Q_HEADS).reshape(SWA_KV_HEADS, SWA_GROUP)
    s = s - slopes[:, :, None, None] * dist.astype(jnp.float32)
    s = jnp.where(mask[:, None, None], s, NEG_INF)
    sink = sinks.astype(jnp.float32).reshape(SWA_KV_HEADS, SWA_GROUP)[:, :, None, None]
    m = jnp.maximum(jnp.max(s, axis=-1, keepdims=True), sink)
    p = jnp.exp(s - m)
    probs = p / (jnp.sum(p, axis=-1, keepdims=True) + jnp.exp(sink - m))
    o = jnp.einsum('bnkgqc,bnckd->bnqkgd', probs.astype(v.dtype), vb)
    return o.reshape(B, S, SWA_Q_HEADS * SWA_HEAD_DIM)


def memory_attention(xq, mk, mv):
    s = jnp.einsum('bshd,bmhd->bhsm', xq, mk).astype(jnp.float32) * (X_HEAD_DIM ** -0.5)
    p = jax.nn.softmax(s, axis=-1)
    o = jnp.einsum('bhsm,bmhd->bshd', p.astype(mv.dtype), mv)
    return o.reshape(xq.shape[0], xq.shape[1], X_WIDTH)


def squared_relu_mlp(h, w_up, w_down):
    return jnp.square(jax.nn.relu(h @ w_up)) @ w_down


def setup_inputs(seed: int = 0) -> dict:
    key = jax.random.key(seed)
    ks = jax.random.split(key, 16)
    f32 = jnp.float32

    def w(k, shape, fan_in):
        return jax.random.normal(k, shape, f32) * (fan_in ** -0.5)

    def gain(k, shape):
        return 1.0 + 0.02 * jax.random.normal(k, shape, f32)

    return {
        'x': jax.random.normal(ks[0], (BATCH, SEQ, D_MODEL), f32),
        'mem': jax.random.normal(ks[1], (BATCH, MEM_LEN, D_MODEL), f32),
        'mem_norm_g': gain(ks[2], (D_MODEL,)),
        'attn_norm_g': gain(ks[3], (DEPTH, D_MODEL)),
        'w_in_gla': w(ks[4], (N_GLA_LAYERS, D_MODEL, GLA_IN), D_MODEL),
        'w_gate_up': w(ks[5], (N_GLA_LAYERS, GLA_GATE_RANK, GLA_K), GLA_GATE_RANK),
        'b_gate': 0.1 * jax.random.normal(ks[6], (N_GLA_LAYERS, GLA_K), f32),
        'gla_out_norm_g': gain(ks[7], (N_GLA_LAYERS, GLA_DV)),
        'w_in_swa': w(ks[8], (N_SWA_LAYERS, D_MODEL, SWA_IN), D_MODEL),
        'sinks': 0.5 * jax.random.normal(ks[9], (N_SWA_LAYERS, SWA_Q_HEADS), f32),
        'w_mem_kv': w(ks[10], (DEPTH, D_MODEL, 2 * X_WIDTH), D_MODEL),
        'w_out': w(ks[11], (DEPTH, MIX_WIDTH, D_MODEL), MIX_WIDTH),
        'mlp_norm_g': gain(ks[12], (DEPTH, D_MODEL)),
        'w_up': w(ks[13], (DEPTH, D_MODEL, D_FF), D_MODEL),
        'w_down': w(ks[14], (DEPTH, D_FF, D_MODEL), D_FF),
        'final_norm_g': gain(ks[15], (D_MODEL,)),
    }


def reference(x, mem, mem_norm_g, attn_norm_g, w_in_gla, w_gate_up, b_gate, gla_out_norm_g,
              w_in_swa, sinks, w_mem_kv, w_out, mlp_norm_g, w_up, w_down, final_norm_g):
    B, S = x.shape[0], x.shape[1]
    M = mem.shape[1]
    mem_n = rmsnorm(mem, mem_norm_g)
    gla_split = [GLA_K, 2 * GLA_K, 2 * GLA_K + GLA_V, 2 * GLA_K + 2 * GLA_V,
                 2 * GLA_K + 2 * GLA_V + GLA_GATE_RANK]
    q_w = SWA_Q_HEADS * SWA_HEAD_DIM
    kv_w = SWA_KV_HEADS * SWA_HEAD_DIM
    swa_split = [q_w, q_w + kv_w, q_w + 2 * kv_w]
    for i in range(DEPTH):
        h = rmsnorm(x, attn_norm_g[i])
        mkv = mem_n @ w_mem_kv[i]
        mk = mkv[..., :X_WIDTH].reshape(B, M, X_HEADS, X_HEAD_DIM)
        mv = mkv[..., X_WIDTH:].reshape(B, M, X_HEADS, X_HEAD_DIM)
        if i % N_MIXERS == 0:
            j = i // N_MIXERS
            proj = h @ w_in_gla[j]
            q, k, v, g_out, g_lr, xq = jnp.split(proj, gla_split, axis=-1)
            o = gla_mixer(q.reshape(B, S, GLA_HEADS, GLA_DK), k.reshape(B, S, GLA_HEADS, GLA_DK),
                          v.reshape(B, S, GLA_HEADS, GLA_DV), g_lr, w_gate_up[j], b_gate[j])
            o = rmsnorm(o, gla_out_norm_g[j]) * jax.nn.silu(g_out.reshape(B, S, GLA_HEADS, GLA_DV))
            o_mix = o.reshape(B, S, GLA_V)
        else:
            j = i // N_MIXERS
            proj = h @ w_in_swa[j]
            q, k, v, xq = jnp.split(proj, swa_split, axis=-1)
            o_mix = swa_sink_mixer(q.reshape(B, S, SWA_KV_HEADS, SWA_GROUP, SWA_HEAD_DIM),
                                   k.reshape(B, S, SWA_KV_HEADS, SWA_HEAD_DIM),
                                   v.reshape(B, S, SWA_KV_HEADS, SWA_HEAD_DIM), sinks[j])
        o_mem = memory_attention(xq.reshape(B, S, X_HEADS, X_HEAD_DIM), mk, mv)
        x = x + jnp.concatenate([o_mix, o_mem], axis=-1) @ w_out[i]
        x = x + squared_relu_mlp(rmsnorm(x, mlp_norm_g[i]), w_up[i], w_down[i])
    return rmsnorm(x, final_norm_g)
```

```python
import numpy as np
from contextlib import ExitStack
import ml_dtypes
import concourse.bass as bass
import concourse.mybir as mybir
from concourse.bass_utils import run_bass_kernel_spmd

F32 = mybir.dt.float32
BF16 = mybir.dt.bfloat16
AF = mybir.ActivationFunctionType
ALU = mybir.AluOpType
AX = mybir.AxisListType

RMS_EPS = 1e-5
NEG_INF = -1e30
GLA_TAU = 16.0


class Cfg:
    def __init__(s, D=2048, T=2048, NL=4, MEM=256, XH=4, GH=4, DK=256, DV=384, R=16,
                 SKV=3, SG=8, DFF=8192, n_cores=8):
        s.D, s.T, s.NL, s.MEM, s.XH, s.GH, s.DK, s.DV, s.R = D, T, NL, MEM, XH, GH, DK, DV, R
        s.SKV, s.SG, s.DFF, s.n_cores = SKV, SG, DFF, n_cores
        s.KC = D // 128
        s.XW = XH * 128
        s.GK, s.GV = GH * DK, GH * DV
        s.DKC, s.DVC = DK // 128, DV // 128
        s.GIN = 2 * s.GK + 2 * s.GV + R + s.XW
        s.SQH = SKV * SG
        s.SQW, s.SKW = s.SQH * 64, SKV * 64
        s.SIN = s.SQW + 2 * s.SKW + s.XW
        s.MIXW = D
        assert s.GV + s.XW == D and s.SQW + s.XW == D
        s.NT = T // 128
        s.BS = min(512, T)
        s.NTB = T // s.BS
        s.NCH = T // 64
        s.TH = min(1024, T)
        s.FG = 512
        s.NG = (NL + 1) // 2
        s.NS = NL // 2
        c = 0
        s.c_memg = c; c += s.KC
        s.c_attng = c; c += NL * s.KC
        s.c_mlpg = c; c += NL * s.KC
        s.c_fing = c; c += s.KC
        s.c_bg = c; c += s.NG * (s.GK // 128)
        s.c_gon = c; c += s.NG * s.DVC
        s.c_flag = c; c += 1
        s.NPP = c


class Fw:
    def __init__(self, nc, es):
        self.nc, self.es = nc, es
        self.E = {'pe': nc.tensor, 'act': nc.scalar, 'dve': nc.vector, 'pool': nc.gpsimd, 'sp': nc.sync}
        self.cnt = {e: 0 for e in self.E}
        self.sems = {('E_' + e): es.enter_context(nc.semaphore('E_' + e)) for e in self.E}
        self.dcnt = {}
        self.known = {e: {} for e in self.E}
        self.lastw, self.readers = {}, {}
        self.psi = {}

    def _sem(self, sn):
        if sn not in self.sems:
            self.sems[sn] = self.es.enter_context(self.nc.semaphore(sn))
            self.dcnt[sn] = 0
        return self.sems[sn]

    def _wait(self, e, deps):
        for sn, v in deps.items():
            if v <= 0 or self.known[e].get(sn, 0) >= v:
                continue
            if sn == 'E_pe' and e == 'pe':
                continue
            self.E[e].wait_ge(self._sem(sn), v)
            self.known[e][sn] = v

    def _deps(self, reads, writes):
        d = {}
        for r in reads:
            ev = self.lastw.get(r)
            if ev:
                d[ev[0]] = max(d.get(ev[0], 0), ev[1])
        for w in writes:
            ev = self.lastw.get(w)
            if ev:
                d[ev[0]] = max(d.get(ev[0], 0), ev[1])
            for sn, v in self.readers.get(w, {}).items():
                d[sn] = max(d.get(sn, 0), v)
        return d

    def _record(self, ev, reads, writes):
        sn, v = ev
        for r in reads:
            self.readers.setdefault(r, {})[sn] = v
        for w in writes:
            self.lastw[w] = ev
            self.readers[w] = {}

    def op(self, e, fn, reads=(), writes=()):
        self._wait(e, self._deps(reads, writes))
        ins = fn(self.E[e])
        self.cnt[e] += 1
        ins.then_inc(self.sems['E_' + e], 1)
        self._record(('E_' + e, self.cnt[e]), reads, writes)

    def dma(self, q, out, in_, sem, reads=(), writes=()):
        self.dma_group(q, [(out, in_)], sem, reads, writes)

    def dma_group(self, q, pairs, sem, reads=(), writes=()):
        sn = 'D_' + sem
        s = self._sem(sn)
        self._wait(q, self._deps(reads, writes))
        for out, in_ in pairs:
            ins = self.E[q].dma_start(out=out, in_=in_)
            self.dcnt[sn] += 16
            ins.then_inc(s, 16)
        self._record((sn, self.dcnt[sn]), reads, writes)

    def collective(self, ins_ap, outs_ap, groups, sem, reads=(), writes=()):
        sn = 'C_' + sem
        s = self._sem(sn)
        self._wait('pool', self._deps(reads, writes))
        ins = self.nc.gpsimd.collective_compute("AllGather", ALU.bypass, replica_groups=groups,
                                                ins=[ins_ap], outs=[outs_ap])
        self.dcnt[sn] += 1
        ins.then_inc(s, 1)
        self._record((sn, self.dcnt[sn]), reads, writes)

    def barrier(self):
        allev = {('E_' + e): self.cnt[e] for e in self.E}
        allev.update(self.dcnt)
        for e in self.E:
            self._wait(e, allev)
        self.lastw, self.readers = {}, {}

    def rot(self, name, n):
        i = self.psi.get(name, 0)
        self.psi[name] = (i + 1) % n
        return i


def build_program(cfg):
    c = cfg
    D, T, KC, BS, NTB, NT = c.D, c.T, c.KC, c.BS, c.NTB, c.NT
    nc = bass.Bass("TRN2", target_bir_lowering=False)

    def din(name, shape, dt=F32):
        return nc.dram_tensor(name, list(shape), dt, kind="ExternalInput")

    xT_in = din("xT", [D, T])
    memT_in = din("memT", [D, c.MEM])
    w_in_gla = din("w_in_gla", [c.NG, D, c.GIN])
    w_gate_up = din("w_gate_up", [c.NG, c.R, c.GK])
    w_in_swa = din("w_in_swa", [max(c.NS, 1), D, c.SIN])
    w_mem_kv = din("w_mem_kv", [c.NL, D, 2 * c.XW])
    w_out = din("w_out", [c.NL, c.MIXW, D])
    w_up = din("w_up", [c.NL, D, c.DFF])
    w_down = din("w_down", [c.NL, c.DFF, D])
    pp_in = din("pp", [128, c.NPP])
    sinks_in = din("sinks_bc", [128, max(c.NS, 1) * c.SQH])
    ident_in = din("ident", [128, 128], BF16)
    onesb_in = din("onesb", [128, 128], BF16)
    onesf_in = din("onesf", [128, 128])
    maskbd_in = din("maskbd", [128, 128])
    cmask_in = din("cmask", [128, T])
    swab_in = din("swa_bias", [128, c.SQH, 256])
    premask_in = din("premask", [128, 256])
    yT_out = nc.dram_tensor("yT", [D, T], F32, kind="ExternalOutput")

    xs = nc.dram_tensor("xs", [D, T], F32)
    mixd = nc.dram_tensor("mixd", [c.MIXW, T], BF16)
    oTd = nc.dram_tensor("oTd", [c.GV, T], F32)
    qcd = nc.dram_tensor("qcd", [c.GK, T], BF16)
    cc_g_in = nc.dram_tensor("cc_g_in", [c.GK, c.DV], F32)
    cc_g_out = nc.dram_tensor("cc_g_out", [2 * c.GK, c.DV], F32)
    SXW = c.SKV * 128 + c.SKW
    cc_s_in = nc.dram_tensor("cc_s_in", [128, SXW], BF16)
    cc_s_out = nc.dram_tensor("cc_s_out", [256, SXW], BF16)
    groups = [[2 * i, 2 * i + 1] for i in range(c.n_cores // 2)]

    with ExitStack() as es:
        fw = Fw(nc, es)
        sb = lambda name, shape, dt=F32: es.enter_context(nc.sbuf_tensor(name, list(shape), dt))
        uid = [0]

        def mk_sbl(stack):
            def f(name, shape, dt=F32):
                uid[0] += 1
                return stack.enter_context(nc.sbuf_tensor(f"{name}_{uid[0]}", list(shape), dt))
            return f
        hT = sb("hT", [128, KC, T], BF16)
        wslot = [sb(f"wslot{i}", [128, 8192], BF16) for i in range(2)]
        pp = sb("pp_sb", [128, c.NPP])
        negbg = sb("negbg", [128, c.NG * (c.GK // 128)])
        ident = sb("ident_sb", [128, 128], BF16)
        onesb = sb("onesb_sb", [128, 128], BF16)
        onesf = sb("onesf_sb", [128, 128])
        maskbd = sb("maskbd_sb", [128, 128])
        sinks = sb("sinks_sb", [128, max(c.NS, 1) * c.SQH])
        memnT = sb("memnT", [128, KC, c.MEM], BF16)
        PS = es.enter_context(nc.psum_tensor("PS", [128, 6 * 512], F32))
        PT = es.enter_context(nc.psum_tensor("PT", [128, 2048], BF16))

        def bank(i):
            return PS[:, i * 512:(i + 1) * 512]

        def ptbank(i):
            return PT[:, i * 1024:(i + 1) * 1024]

        for dst, src, nm in ((pp, pp_in, 'c0'), (ident, ident_in, 'c1'), (onesb, onesb_in, 'c2'),
                             (onesf, onesf_in, 'c3'), (maskbd, maskbd_in, 'c4'), (sinks, sinks_in, 'c5')):
            fw.dma('sp', dst[:], src[:, :], nm, writes=[nm])
        nbg = c.NG * (c.GK // 128)
        fw.op('dve', lambda e: e.tensor_scalar_mul(out=negbg[:], in0=pp[:, c.c_bg:c.c_bg + nbg], scalar1=-1.0),
              reads=['c0'], writes=['negbg'])
        fw.barrier()

        wctr = [0]

        def wload(view, k, m):
            s = wctr[0] % 2
            wctr[0] += 1
            assert k * m <= 8192
            wv = wslot[s][:, 0:k * m].rearrange("p (k m) -> p k m", k=k)
            fw.dma('pool', wv, view, f'w{s}', writes=[('w', s)])
            return wv, ('w', s)

        def wcols(W2d, c0, ncols, kc=KC):
            return W2d[:, c0:c0 + ncols].rearrange("(k p) m -> p k m", p=128)

        def norm_phase(src, N, gcol, dst_tile=None, dst_dram=None, tokname='hT'):
            bs = min(512, N)
            with ExitStack() as ps_:
                sbl = mk_sbl(ps_)
                xblk = [sbl(f"n_xblk{i}", [128, KC, bs]) for i in range(2)]
                sq = [sbl(f"n_sq{i}", [128, 4, bs], BF16) for i in range(2)]
                rs = [sbl(f"n_rs{i}", [128, bs]) for i in range(2)]
                ot = [sbl(f"n_ot{i}", [128, 4, bs]) for i in range(2)] if dst_dram is not None else None
                for b in range(N // bs):
                    s = b % 2
                    fw.dma('sp', xblk[s][:], src[:, b * bs:(b + 1) * bs].rearrange("(k p) t -> p k t", p=128),
                           f'nx{s}', writes=[('nx', s)])
                    pb = 4 + fw.rot('nps', 2)
                    for g4 in range(KC // 4):
                        s2 = fw.rot('nsq', 2)
                        fw.op('act', lambda e, s=s, s2=s2, g4=g4: e.activation(
                            out=sq[s2][:], in_=xblk[s][:, g4 * 4:(g4 + 1) * 4, :], func=AF.Square),
                            reads=[('nx', s)], writes=[('nsq', s2)])
                        for k in range(4):
                            kk = g4 * 4 + k
                            fw.op('pe', lambda e, s2=s2, k=k, kk=kk, pb=pb: e.matmul(
                                bank(pb)[:, 0:bs], onesb[:], sq[s2][:, k, :], start=(kk == 0), stop=(kk == KC - 1)),
                                reads=[('nsq', s2)], writes=[('ps', pb)])
                    fw.op('act', lambda e, s=s, pb=pb: e.activation(out=rs[s][:], in_=bank(pb)[:, 0:bs], func=AF.Sqrt,
                                                                   scale=1.0 / D, bias=RMS_EPS),
                          reads=[('ps', pb)], writes=[('nrs', s)])
                    fw.op('dve', lambda e, s=s: e.reciprocal(out=rs[s][:], in_=rs[s][:]),
                          reads=[('nrs', s)], writes=[('nrs', s)])
                    if dst_tile is not None:
                        for kc in range(KC):
                            fw.op('dve', lambda e, s=s, kc=kc, b=b: e.scalar_tensor_tensor(
                                out=dst_tile[:, kc, b * bs:(b + 1) * bs], in0=xblk[s][:, kc, :],
                                scalar=pp[:, gcol + kc:gcol + kc + 1], in1=rs[s][:], op0=ALU.mult, op1=ALU.mult),
                                reads=[('nx', s), ('nrs', s)], writes=[(tokname, kc, b)])
                    else:
                        for g4 in range(KC // 4):
                            so = fw.rot('not', 2)
                            for k in range(4):
                                kc = g4 * 4 + k
                                fw.op('dve', lambda e, s=s, kc=kc, k=k, so=so: e.scalar_tensor_tensor(
                                    out=ot[so][:, k, :], in0=xblk[s][:, kc, :],
                                    scalar=pp[:, gcol + kc:gcol + kc + 1], in1=rs[s][:], op0=ALU.mult, op1=ALU.mult),
                                    reads=[('nx', s), ('nrs', s)], writes=[('not', so)])
                            fw.dma('sp', dst_dram[g4 * 512:(g4 + 1) * 512, b * bs:(b + 1) * bs].rearrange(
                                "(k p) t -> p k t", p=128), ot[so][:], f'not{so}', reads=[('not', so)],
                                writes=[('yT', g4, b)])
                fw.barrier()

        def hT_reads(b):
            return [('hT', kc, b) for kc in range(KC)]

        def lin_ws(W2d, c0, ncols, epi, banks=(0, 1, 2, 3), src=None, src_reads=None, kc_n=KC, wview=None,
                   tbs=None):
            src = hT if src is None else src
            src_reads = hT_reads if src_reads is None else src_reads
            tbs = range(NTB) if tbs is None else tbs
            gw = 8192 // kc_n
            gw = min(gw, 512)
            for g0 in range(0, ncols, gw):
                gsz = min(gw, ncols - g0)
                wv, wt = wload(wcols(W2d, c0 + g0, gsz) if wview is None else wview(g0, gsz), kc_n, gsz)
                for b in tbs:
                    for m0 in range(0, gsz, 128):
                        msz = min(128, gsz - m0)
                        pb = banks[fw.rot(('lin', banks), len(banks))]
                        for kc in range(kc_n):
                            fw.op('pe', lambda e, kc=kc, m0=m0, msz=msz, b=b, pb=pb, wv=wv: e.matmul(
                                bank(pb)[0:msz, 0:BS], wv[:, kc, m0:m0 + msz], src[:, kc, b * BS:(b + 1) * BS],
                                start=(kc == 0), stop=(kc == kc_n - 1)),
                                reads=[wt] + (src_reads(b) if kc == 0 else []), writes=[('ps', pb)])
                        epi(g0 + m0, msz, b, bank(pb)[0:msz, 0:BS], ('ps', pb))

        def mem_kv(li, mkT, mv_tm):
            W = w_mem_kv[li]
            MT = c.MEM // 128
            for g0 in range(0, c.XW, 512):
                gsz = min(512, c.XW - g0)
                wv, wt = wload(wcols(W, g0, gsz), KC, gsz)
                for m0 in range(0, gsz, 128):
                    h = (g0 + m0) // 128
                    pb = fw.rot('mk', 2)
                    for kc in range(KC):
                        fw.op('pe', lambda e, kc=kc, m0=m0, pb=pb, wv=wv: e.matmul(
                            bank(pb)[:, 0:c.MEM], wv[:, kc, m0:m0 + 128], memnT[:, kc, :],
                            start=(kc == 0), stop=(kc == KC - 1)), reads=[wt, 'memnT'], writes=[('ps', pb)])
                    fw.op('act', lambda e, h=h, pb=pb: e.copy(out=mkT[:, h, :], in_=bank(pb)[:, 0:c.MEM]),
                          reads=[('ps', pb)], writes=[('mkT', h)])
            for g0 in range(0, c.XW, 512):
                gsz = min(512, c.XW - g0)
                wv, wt = wload(wcols(W, c.XW + g0, gsz), KC, gsz)
                for mt in range(MT):
                    pb = fw.rot('mk', 2)
                    for kc in range(KC):
                        fw.op('pe', lambda e, kc=kc, mt=mt, pb=pb, wv=wv, gsz=gsz: e.matmul(
                            bank(pb)[:, 0:gsz], memnT[:, kc, mt * 128:(mt + 1) * 128], wv[:, kc, :],
                            start=(kc == 0), stop=(kc == KC - 1)), reads=[wt, 'memnT'], writes=[('ps', pb)])
                    fw.op('act', lambda e, mt=mt, pb=pb, g0=g0, gsz=gsz: e.copy(
                        out=mv_tm[:, mt, g0:g0 + gsz], in_=bank(pb)[:, 0:gsz]),
                        reads=[('ps', pb)], writes=[('mv', mt, g0)])

        def mem_attn(li, Win2d, cxq):
            MT = c.MEM // 128
            with ExitStack() as ps_:
                sbl = mk_sbl(ps_)
                mkT = sbl("ma_mkT", [128, c.XH, c.MEM], BF16)
                mv_tm = sbl("ma_mv", [128, MT, c.XW], BF16)
                xqT = sbl("ma_xqT", [128, c.XH, T], BF16)
                s_sb = [sbl(f"ma_s{i}", [128, c.MEM]) for i in range(4)]
                pn = [sbl(f"ma_pn{i}", [128, c.MEM], BF16) for i in range(4)]
                st = [sbl(f"ma_st{i}", [128, 4]) for i in range(4)]
                pT = [sbl(f"ma_pT{i}", [128, MT, 128], BF16) for i in range(4)]
                ob = [sbl(f"ma_ob{i}", [128, 128], BF16) for i in range(4)]
                mem_kv(li, mkT, mv_tm)

                def epi_xq(mg, msz, b, ps, pst):
                    h = mg // 128
                    fw.op('act', lambda e: e.activation(out=xqT[:, h, b * BS:(b + 1) * BS], in_=ps, func=AF.Copy,
                                                        scale=128.0 ** -0.5),
                          reads=[pst], writes=[('xq', h, b)])
                lin_ws(Win2d, cxq, c.XW, epi_xq, banks=(4, 5))
                mv_reads = [('mv', mt, g0) for mt in range(MT) for g0 in range(0, c.XW, 512)]
                units = [(tt, h) for tt in range(NT) for h in range(c.XH)]
                NU = 4

                def stA(n):
                    tt, h = units[n]
                    b = (tt * 128) // BS
                    u = n % NU
                    pb = fw.rot('ma_ps', 3)
                    fw.op('pe', lambda e: e.matmul(
                        bank(pb)[:, 0:c.MEM], xqT[:, h, tt * 128:(tt + 1) * 128], mkT[:, h, :], start=True, stop=True),
                        reads=[('xq', h, b), ('mkT', h)], writes=[('ps', pb)])
                    fw.op('dve', lambda e: e.reduce_max(out=st[u][:, 0:1], in_=bank(pb)[:, 0:c.MEM], axis=AX.X),
                          reads=[('ps', pb)], writes=[('ma_st', u)])
                    fw.op('dve', lambda e: e.tensor_scalar_mul(out=st[u][:, 1:2], in0=st[u][:, 0:1], scalar1=-1.0),
                          reads=[('ma_st', u)], writes=[('ma_st', u)])
                    fw.op('act', lambda e: e.activation(out=s_sb[u][:], in_=bank(pb)[:, 0:c.MEM], func=AF.Exp,
                                                        bias=st[u][:, 1:2], accum_out=st[u][:, 2:3]),
                          reads=[('ps', pb), ('ma_st', u)], writes=[('ma_s', u), ('ma_st2', u)])
                    fw.op('dve', lambda e: e.reciprocal(out=st[u][:, 3:4], in_=st[u][:, 2:3]),
                          reads=[('ma_st2', u)], writes=[('ma_st3', u)])
                    fw.op('dve', lambda e: e.tensor_scalar_mul(out=pn[u][:], in0=s_sb[u][:], scalar1=st[u][:, 3:4]),
                          reads=[('ma_s', u), ('ma_st3', u)], writes=[('ma_pn', u)])

                def stB(n):
                    u = n % NU
                    tb_ = fw.rot('pt', 2)
                    for mt in range(MT):
                        fw.op('pe', lambda e, mt=mt: e.transpose(
                            ptbank(tb_)[:, mt * 128:(mt + 1) * 128], pn[u][:, mt * 128:(mt + 1) * 128], ident[:]),
                            reads=[('ma_pn', u)], writes=[('pt', tb_)])
                    fw.op('act', lambda e: e.copy(
                        out=pT[u][:], in_=ptbank(tb_)[:, 0:MT * 128].rearrange("p (m t) -> p m t", m=MT)),
                        reads=[('pt', tb_)], writes=[('ma_pT', u)])

                def stC(n):
                    tt, h = units[n]
                    u = n % NU
                    pb2 = 3 + fw.rot('ma_ps2', 3)
                    for mt in range(MT):
                        fw.op('pe', lambda e, mt=mt: e.matmul(
                            bank(pb2)[:, 0:128], mv_tm[:, mt, h * 128:(h + 1) * 128], pT[u][:, mt, :],
                            start=(mt == 0), stop=(mt == MT - 1)),
                            reads=[('ma_pT', u)] + mv_reads, writes=[('ps', pb2)])
                    fw.op('act', lambda e: e.copy(out=ob[u][:], in_=bank(pb2)[:, 0:128]),
                          reads=[('ps', pb2)], writes=[('ma_ob', u)])
                    r0 = c.MIXW - c.XW + h * 128
                    fw.dma('sp', mixd[r0:r0 + 128, tt * 128:(tt + 1) * 128], ob[u][:], f'ma_ob{u}',
                           reads=[('ma_ob', u)], writes=[('mixd', r0, tt)])

                for n in range(len(units) + 2):
                    if n < len(units):
                        stA(n)
                    if 0 <= n - 1 < len(units):
                        stB(n - 1)
                    if 0 <= n - 2 < len(units):
                        stC(n - 2)
                fw.barrier()

        def gla_layer(li, j):
            W = w_in_gla[j]
            cq0, ck0, cv0, cgo0 = 0, c.GK, 2 * c.GK, 2 * c.GK + c.GV
            cglr = 2 * c.GK + 2 * c.GV
            cxq = cglr + c.R
            DKC, DVC, DK, DV, NCH = c.DKC, c.DVC, c.DK, c.DV, c.NCH
            i16 = 1.0 / GLA_TAU
            with ExitStack() as ps_:
                sbl = mk_sbl(ps_)
                glr = sbl("g_glr", [c.R, T], BF16)
                wgu = sbl("g_wgu", [c.R, c.GK], BF16)
                cmask = sbl("g_cmask", [128, BS])
                lbuf = [sbl(f"g_l{i}", [128, BS]) for i in range(2)]
                cs = [sbl(f"g_cs{i}", [128, T]) for i in range(DKC)]
                etmp = [sbl(f"g_et{i}", [128, BS]) for i in range(1)]
                ebt = [sbl(f"g_eb{i}", [128, BS]) for i in range(1)]
                enbt = [sbl(f"g_enb{i}", [128, BS]) for i in range(1)]
                ekt = [sbl(f"g_ek{i}", [128, BS]) for i in range(1)]
                kot = [sbl(f"g_kot{i}", [128, BS], BF16) for i in range(2)]
                qct = [sbl(f"g_qct{i}", [128, BS], BF16) for i in range(2)]
                dec = sbl("g_dec", [128, DKC, NCH])
                ncl = sbl("g_ncl", [128, DKC, NCH])
                pn_ = sbl("g_pn", [128, DKC, NCH])
                sc1 = sbl("g_sc1", [128, NCH])
                sc2 = sbl("g_sc2", [128, NCH])
                q_in = sbl("g_qin", [128, DKC, T], BF16)
                k_in = sbl("g_kin", [128, DKC, T], BF16)
                k_out = sbl("g_kout", [128, NT, DK], BF16)
                v_tm = sbl("g_v", [128, NT, DV], BF16)
                S_f = [sbl(f"g_Sf{i}", [128, DKC, DV]) for i in range(2)]
                S_b = [sbl(f"g_Sb{i}", [128, DKC, DV], BF16) for i in range(2)]
                AT = [sbl(f"g_AT{i}", [128, 128], BF16) for i in range(2)]
                o_sb = [sbl(f"g_o{i}", [128, DVC, 128]) for i in range(2)]

                fw.dma('sp', cmask[:], cmask_in[:, 0:BS], 'g_cm', writes=['cmask'])
                fw.dma('pool', wgu[:], w_gate_up[j], 'g_wgu', writes=['wgu'])

                def epi_glr(mg, msz, b, ps, pst):
                    fw.op('act', lambda e: e.copy(out=glr[0:c.R, b * BS:(b + 1) * BS], in_=ps),
                          reads=[pst], writes=[('glr', b)])
                lin_ws(W, cglr, c.R, epi_glr)

                for h in range(c.GH):
                    for jj in range(DKC):
                        cg = h * DKC + jj
                        for b in range(NTB):
                            pb = fw.rot('g_ps', 4)
                            fw.op('pe', lambda e, cg=cg, b=b, pb=pb: e.matmul(
                                bank(pb)[:, 0:BS], wgu[0:c.R, cg * 128:(cg + 1) * 128], glr[0:c.R, b * BS:(b + 1) * BS],
                                start=True, stop=True), reads=['wgu', ('glr', b)], writes=[('ps', pb)])
                            u = 0
                            lb_ = fw.rot('g_l', 2)
                            bcol = j * (c.GK // 128) + cg
                            fw.op('act', lambda e, u=u, pb=pb, bcol=bcol: e.activation(
                                out=etmp[u][:], in_=bank(pb)[:, 0:BS], func=AF.Exp, scale=-1.0,
                                bias=negbg[:, bcol:bcol + 1]), reads=[('ps', pb), 'negbg'], writes=[('g_et', u)])
                            fw.op('act', lambda e, u=u, lb_=lb_: e.activation(
                                out=lbuf[lb_][:], in_=etmp[u][:], func=AF.Ln, bias=1.0),
                                reads=[('g_et', u)], writes=[('g_l', lb_)])
                            fw.op('dve', lambda e, jj=jj, b=b, lb_=lb_: e.tensor_tensor_scan(
                                out=cs[jj][:, b * BS:(b + 1) * BS], data0=cmask[:], data1=lbuf[lb_][:], initial=0.0,
                                op0=ALU.mult, op1=ALU.add),
                                reads=['cmask', ('g_l', lb_)], writes=[('g_cs', jj)])
                        csl = cs[jj][:, :].rearrange("p (n q) -> p n q", q=64)[:, :, 63]
                        fw.op('act', lambda e, jj=jj, csl=csl: e.activation(out=dec[:, jj, :], in_=csl, func=AF.Exp, scale=-i16),
                              reads=[('g_cs', jj)], writes=[('g_dec', jj)])
                        fw.op('dve', lambda e, jj=jj, csl=csl: e.tensor_scalar_mul(out=ncl[:, jj, :], in0=csl, scalar1=-i16),
                              reads=[('g_cs', jj)], writes=[('g_ncl', jj)])
                        fw.op('dve', lambda e, csl=csl: e.tensor_tensor_scan(
                            out=sc1[:], data0=onesf[:, 0:NCH], data1=csl, initial=0.0, op0=ALU.mult, op1=ALU.add),
                            reads=[('g_cs', jj), 'c3'], writes=['g_sc1'])
                        fw.op('dve', lambda e, csl=csl: e.tensor_sub(out=sc2[:], in0=sc1[:], in1=csl),
                              reads=['g_sc1', ('g_cs', jj)], writes=['g_sc2'])
                        fw.op('act', lambda e, jj=jj: e.activation(out=pn_[:, jj, :], in_=sc2[:], func=AF.Exp, scale=-i16),
                              reads=['g_sc2'], writes=[('g_pn', jj)])

                    def decay_blocks(jj, b):
                        u = 0
                        blk = cs[jj][:, b * BS:(b + 1) * BS]
                        fw.op('act', lambda e: e.activation(out=ebt[u][:], in_=blk, func=AF.Exp, scale=-i16),
                              reads=[('g_cs', jj)], writes=[('g_eb', u)])
                        fw.op('act', lambda e: e.activation(out=enbt[u][:], in_=blk, func=AF.Exp, scale=i16),
                              reads=[('g_cs', jj)], writes=[('g_enb', u)])
                        for n in range(BS // 64):
                            ng = b * (BS // 64) + n
                            fw.op('act', lambda e, n=n, ng=ng: e.activation(
                                out=ekt[u][:, n * 64:(n + 1) * 64], in_=cs[jj][:, ng * 64:(ng + 1) * 64], func=AF.Exp,
                                scale=i16, bias=ncl[:, jj, ng:ng + 1]),
                                reads=[('g_cs', jj), ('g_ncl', jj)], writes=[('g_ek', u, n)])
                        return u

                    def epi_q(mg, msz, b, ps, pst):
                        jj = mg // 128
                        u = decay_blocks(jj, b)
                        fw.op('dve', lambda e: e.scalar_tensor_tensor(
                            out=q_in[:, jj, b * BS:(b + 1) * BS], in0=ps, scalar=float(DK) ** -0.5, in1=ebt[u][:],
                            op0=ALU.mult, op1=ALU.mult), reads=[pst, ('g_eb', u)], writes=[('g_qin', jj, b)])
                        nb = BS // 64
                        v = fw.rot('g_qct', 2)
                        fw.op('dve', lambda e: e.tensor_tensor(
                            out=qct[v][:, :].rearrange("p (n q) -> p n q", q=64),
                            in0=q_in[:, jj, b * BS:(b + 1) * BS].rearrange("p (n q) -> p n q", q=64),
                            in1=pn_[:, jj, b * nb:(b + 1) * nb].unsqueeze(2).to_broadcast([128, nb, 64]), op=ALU.mult),
                            reads=[('g_qin', jj, b), ('g_pn', jj)], writes=[('g_qct', v)])
                        r0 = (h * DKC + jj) * 128
                        fw.dma('sp', qcd[r0:r0 + 128, b * BS:(b + 1) * BS], qct[v][:], f'g_qct{v}',
                               reads=[('g_qct', v)], writes=[('qcd', r0, b)])
                    lin_ws(W, cq0 + h * DK, DK, epi_q)

                    def epi_k(mg, msz, b, ps, pst):
                        jj = mg // 128
                        u = decay_blocks(jj, b)
                        fw.op('dve', lambda e: e.tensor_tensor(out=k_in[:, jj, b * BS:(b + 1) * BS], in0=ps, in1=enbt[u][:],
                                                               op=ALU.mult),
                              reads=[pst, ('g_enb', u)], writes=[('g_kin', jj, b)])
                        v = fw.rot('g_kot', 2)
                        fw.op('dve', lambda e: e.tensor_tensor(out=kot[v][:], in0=ps, in1=ekt[u][:], op=ALU.mult),
                              reads=[pst] + [('g_ek', u, n) for n in range(BS // 64)], writes=[('g_kot', v)])
                        for t4 in range(BS // 128):
                            tt = b * (BS // 128) + t4
                            tb_ = fw.rot('pt', 2)
                            fw.op('pe', lambda e, t4=t4, tb_=tb_: e.transpose(
                                ptbank(tb_)[:, 0:128], kot[v][:, t4 * 128:(t4 + 1) * 128], ident[:]),
                                reads=[('g_kot', v)], writes=[('pt', tb_)])
                            fw.op('act', lambda e, tt=tt, tb_=tb_: e.copy(out=k_out[:, tt, jj * 128:(jj + 1) * 128],
                                                                         in_=ptbank(tb_)[:, 0:128]),
                                  reads=[('pt', tb_)], writes=[('g_kout', tt, jj)])
                    lin_ws(W, ck0 + h * DK, DK, epi_k)

                    wv, wt = wload(wcols(W, cv0 + h * DV, DV), KC, DV)
                    for tt in range(NT):
                        b = (tt * 128) // BS
                        pb = fw.rot('g_ps', 4)
                        for kc in range(KC):
                            fw.op('pe', lambda e, kc=kc, tt=tt, pb=pb: e.matmul(
                                bank(pb)[:, 0:DV], hT[:, kc, tt * 128:(tt + 1) * 128], wv[:, kc, :],
                                start=(kc == 0), stop=(kc == KC - 1)),
                                reads=[wt] + (hT_reads(b) if kc == 0 else []), writes=[('ps', pb)])
                        fw.op('act', lambda e, tt=tt, pb=pb: e.copy(out=v_tm[:, tt, :], in_=bank(pb)[:, 0:DV]),
                              reads=[('ps', pb)], writes=[('g_v', tt)])

                    fw.op('dve', lambda e: e.memset(S_f[0][:], 0.0), writes=[('g_Sf', 0, jj) for jj in range(DKC)])
                    fw.op('dve', lambda e: e.memset(S_b[0][:], 0.0), writes=[('g_Sb', 0, jj) for jj in range(DKC)])
                    for tt in range(NT):
                        b = (tt * 128) // BS
                        tsl = slice(tt * 128, (tt + 1) * 128)
                        pa = fw.rot('g_ps', 4)
                        for jj in range(DKC):
                            fw.op('pe', lambda e, jj=jj, pa=pa, tsl=tsl: e.matmul(
                                bank(pa)[:, 0:128], k_in[:, jj, tsl], q_in[:, jj, tsl], start=(jj == 0), stop=(jj == DKC - 1)),
                                reads=[('g_kin', jj, b), ('g_qin', jj, b)], writes=[('ps', pa)])
                        a = fw.rot('g_AT', 2)
                        fw.op('dve', lambda e, a=a, pa=pa: e.tensor_tensor(out=AT[a][:], in0=bank(pa)[:, 0:128], in1=maskbd[:],
                                                                        op=ALU.mult),
                              reads=[('ps', pa), 'c4'], writes=[('g_AT', a)])
                        ob_ = fw.rot('g_o', 2)
                        po = 4 + fw.rot('g_po', 2)
                        for cc in range(2):
                            n = tt * 2 + cc
                            cur, nxt = n % 2, (n + 1) % 2
                            rows = slice(cc * 64, (cc + 1) * 64)
                            csl_ = slice(tt * 128 + cc * 64, tt * 128 + (cc + 1) * 64)
                            for dvc in range(DVC):
                                ocol = slice(dvc * 128 + cc * 64, dvc * 128 + (cc + 1) * 64)
                                for jj in range(DKC):
                                    fw.op('pe', lambda e, jj=jj, dvc=dvc, ocol=ocol, csl_=csl_, po=po: e.matmul(
                                        bank(po)[:, ocol], S_b[cur][:, jj, dvc * 128:(dvc + 1) * 128], q_in[:, jj, csl_],
                                        start=(jj == 0), stop=False),
                                        reads=[('g_Sb', cur, jj), ('g_qin', jj, b)], writes=[('ps', po)])
                                fw.op('pe', lambda e, dvc=dvc, ocol=ocol, rows=rows, a=a, tt=tt, cc=cc, po=po: e.matmul(
                                    bank(po)[:, ocol], v_tm[rows, tt, dvc * 128:(dvc + 1) * 128],
                                    AT[a][rows, cc * 64:(cc + 1) * 64], start=False, stop=True),
                                    reads=[('g_v', tt), ('g_AT', a)], writes=[('ps', po)])
                            for jj in range(DKC):
                                pu = fw.rot('g_ps', 4)
                                fw.op('pe', lambda e, jj=jj, rows=rows, tt=tt, pu=pu: e.matmul(
                                    bank(pu)[:, 0:DV], k_out[rows, tt, jj * 128:(jj + 1) * 128], v_tm[rows, tt, :],
                                    start=True, stop=True),
                                    reads=[('g_kout', tt, jj), ('g_v', tt)], writes=[('ps', pu)])
                                fw.op('dve', lambda e, jj=jj, n=n, pu=pu: e.scalar_tensor_tensor(
                                    out=S_f[nxt][:, jj, :], in0=S_f[cur][:, jj, :], scalar=dec[:, jj, n:n + 1], in1=bank(pu)[:, 0:DV],
                                    op0=ALU.mult, op1=ALU.add),
                                    reads=[('g_Sf', cur, jj), ('g_dec', jj), ('ps', pu)], writes=[('g_Sf', nxt, jj)])
                                fw.op('dve', lambda e, jj=jj: e.tensor_copy(out=S_b[nxt][:, jj, :], in_=S_f[nxt][:, jj, :]),
                                      reads=[('g_Sf', nxt, jj)], writes=[('g_Sb', nxt, jj)])
                        fw.op('act', lambda e, ob_=ob_, po=po: e.copy(
                            out=o_sb[ob_][:], in_=bank(po)[:, 0:DVC * 128].rearrange("p (d t) -> p d t", d=DVC)),
                            reads=[('ps', po)], writes=[('g_o', ob_)])
                        fw.dma('sp', oTd[h * DV:(h + 1) * DV, tsl].rearrange("(d p) t -> p d t", p=128), o_sb[ob_][:],
                               f'g_o{ob_}', reads=[('g_o', ob_)], writes=[('oTd', h, tt)])
                    fin = NCH % 2
                    fw.dma('sp', cc_g_in[h * DK:(h + 1) * DK, :].rearrange("(j p) v -> p j v", p=128), S_f[fin][:],
                           'g_Sst', reads=[('g_Sf', fin, jj) for jj in range(DKC)], writes=[('ccg', h)])
                fw.collective(cc_g_in.ap().opt(), cc_g_out.ap().opt(), groups, 'g',
                              reads=[('ccg', h) for h in range(c.GH)], writes=['ccg_out'])
                fw.barrier()
            return W, cxq, cgo0

        def gla_finish(li, j, W, cgo0):
            DKC, DVC, DK, DV = c.DKC, c.DVC, c.DK, c.DV
            with ExitStack() as ps_:
                sbl = mk_sbl(ps_)
                S0f = sbl("f_S0f", [128, DKC, DV])
                S0b = sbl("f_S0b", [128, DKC, DV], BF16)
                qc = [sbl(f"f_qc{i}", [128, DKC, BS], BF16) for i in range(2)]
                ob = [sbl(f"f_ob{i}", [128, DVC, BS]) for i in range(2)]
                sq = [sbl(f"f_sq{i}", [128, DVC, BS], BF16) for i in range(2)]
                rs = [sbl(f"f_rs{i}", [128, BS]) for i in range(2)]
                sg = [sbl(f"f_sg{i}", [128, BS]) for i in range(2)]
                tm = [sbl(f"f_tm{i}", [128, BS]) for i in range(2)]
                mx = [sbl(f"f_mx{i}", [128, BS], BF16) for i in range(2)]
                units = [(h, b) for h in range(c.GH) for b in range(NTB)]
                wts = {}

                def stA(n):
                    h, b = units[n]
                    u = n % 2
                    bsl = slice(b * BS, (b + 1) * BS)
                    if b == 0:
                        fw.dma('sp', S0f[:], cc_g_out[h * DK:(h + 1) * DK, :].rearrange("(j p) v -> p j v", p=128), 'f_S0',
                               writes=['f_S0f'])
                        fw.op('dve', lambda e: e.tensor_scalar_mul(out=S0b[:], in0=S0f[:], scalar1=pp[:, c.c_flag:c.c_flag + 1]),
                              reads=['f_S0f'], writes=['f_S0b'])
                    fw.dma('sp', qc[u][:], qcd[h * DK:(h + 1) * DK, bsl].rearrange("(j p) t -> p j t", p=128), f'f_qc{u}',
                           writes=[('f_qc', u)])
                    fw.dma('sp', ob[u][:], oTd[h * DV:(h + 1) * DV, bsl].rearrange("(d p) t -> p d t", p=128), f'f_ob{u}',
                           writes=[('f_ob', u)])
                    for dvc in range(DVC):
                        pb = fw.rot('f_ps', 2)
                        for jj in range(DKC):
                            fw.op('pe', lambda e, jj=jj: e.matmul(
                                bank(pb)[:, 0:BS], S0b[:, jj, dvc * 128:(dvc + 1) * 128], qc[u][:, jj, :],
                                start=(jj == 0), stop=(jj == DKC - 1)),
                                reads=['f_S0b', ('f_qc', u)], writes=[('ps', pb)])
                        fw.op('dve', lambda e: e.tensor_tensor(
                            out=ob[u][:, dvc, :], in0=bank(pb)[:, 0:BS], in1=ob[u][:, dvc, :], op=ALU.add),
                            reads=[('ps', pb), ('f_ob', u)], writes=[('f_ob', u)])
                    fw.op('act', lambda e: e.activation(out=sq[u][:], in_=ob[u][:], func=AF.Square),
                          reads=[('f_ob', u)], writes=[('f_sq', u)])
                    pn = 2 + fw.rot('f_pn', 2)
                    for dvc in range(DVC):
                        fw.op('pe', lambda e: e.matmul(
                            bank(pn)[:, 0:BS], onesb[:], sq[u][:, dvc, :], start=(dvc == 0), stop=(dvc == DVC - 1)),
                            reads=[('f_sq', u)], writes=[('ps', pn)])
                    fw.op('act', lambda e: e.activation(out=rs[u][:], in_=bank(pn)[:, 0:BS], func=AF.Sqrt,
                                                        scale=1.0 / DV, bias=RMS_EPS),
                          reads=[('ps', pn)], writes=[('f_rs', u)])
                    fw.op('dve', lambda e: e.reciprocal(out=rs[u][:], in_=rs[u][:]),
                          reads=[('f_rs', u)], writes=[('f_rs', u)])

                def stB(n):
                    h, b = units[n]
                    u = n % 2
                    bsl = slice(b * BS, (b + 1) * BS)
                    if b == 0 and h + 1 < c.GH:
                        wts[h + 1] = wload(wcols(W, cgo0 + (h + 1) * DV, DV), KC, DV)
                    wv, wt = wts[h]
                    for dvc in range(DVC):
                        pg = 4 + fw.rot('f_pg', 2)
                        for kc in range(KC):
                            fw.op('pe', lambda e, kc=kc: e.matmul(
                                bank(pg)[:, 0:BS], wv[:, kc, dvc * 128:(dvc + 1) * 128], hT[:, kc, b * BS:(b + 1) * BS],
                                start=(kc == 0), stop=(kc == KC - 1)),
                                reads=[wt] + (hT_reads(b) if kc == 0 else []), writes=[('ps', pg)])
                        v = fw.rot('f_sg', 2)
                        fw.op('act', lambda e: e.activation(out=sg[v][:], in_=bank(pg)[:, 0:BS], func=AF.Silu),
                              reads=[('ps', pg)], writes=[('f_sg', v)])
                        gcol = c.c_gon + j * DVC + dvc
                        fw.op('dve', lambda e: e.scalar_tensor_tensor(
                            out=tm[v][:], in0=ob[u][:, dvc, :], scalar=pp[:, gcol:gcol + 1], in1=rs[u][:],
                            op0=ALU.mult, op1=ALU.mult), reads=[('f_ob', u), ('f_rs', u)], writes=[('f_tm', v)])
                        fw.op('dve', lambda e: e.tensor_tensor(out=mx[v][:], in0=tm[v][:], in1=sg[v][:], op=ALU.mult),
                              reads=[('f_tm', v), ('f_sg', v)], writes=[('f_mx', v)])
                        r0 = h * DV + dvc * 128
                        fw.dma('sp', mixd[r0:r0 + 128, bsl], mx[v][:], f'f_mx{v}', reads=[('f_mx', v)],
                               writes=[('mixd', r0, b)])

                wts[0] = wload(wcols(W, cgo0, DV), KC, DV)
                stA(0)
                for n in range(len(units)):
                    if n + 1 < len(units):
                        stA(n + 1)
                    stB(n)
                fw.barrier()

        def swa_layer(li, j):
            W = w_in_swa[j]
            cq0, ck0, cv0, cxq = 0, c.SQW, c.SQW + c.SKW, c.SQW + 2 * c.SKW
            SG, SKV, SKW = c.SG, c.SKV, c.SKW
            NP = SG // 2
            with ExitStack() as ps_:
                sbl = mk_sbl(ps_)
                kT2 = sbl("s_kT2", [128, SKV, 128 + T], BF16)
                v_tm = sbl("s_v", [128, NT + 1, SKW], BF16)
                q2T = sbl("s_q2T", [128, NP, T], BF16)
                bias = sbl("s_bias", [128, SG, 256])
                prem = sbl("s_prem", [128, 256])
                s_sb = [sbl(f"s_s{i}", [128, SG, 256]) for i in range(2)]
                pnb = [sbl(f"s_pn{i}", [128, SG, 256], BF16) for i in range(2)]
                st = [sbl(f"s_st{i}", [128, 6, SG]) for i in range(2)]
                pT = [sbl(f"s_pT{i}", [128, SG * 2, 128], BF16) for i in range(2)]
                ob = [sbl(f"s_ob{i}", [128, NP, 128], BF16) for i in range(2)]
                fw.dma('sp', prem[:], premask_in[:, :], 's_prem', writes=['s_prem'])

                for kh in range(SKV):
                    s = wctr[0] % 2
                    wctr[0] += 1
                    wv = wslot[s][:, 0:KC * 128].rearrange("p (k m) -> p k m", k=KC)
                    src_ = wcols(W, ck0 + kh * 64, 64)
                    fw.dma_group('pool', [(wv[:, :, 0:64], src_), (wv[:, :, 64:128], src_)], f'w{s}', writes=[('w', s)])
                    for b in range(NTB):
                        pb = fw.rot('s_ps', 2)
                        for kc in range(KC):
                            fw.op('pe', lambda e, kc=kc, b=b, pb=pb, wv=wv: e.matmul(
                                bank(pb)[:, 0:BS], wv[:, kc, :], hT[:, kc, b * BS:(b + 1) * BS],
                                start=(kc == 0), stop=(kc == KC - 1)),
                                reads=[('w', s)] + (hT_reads(b) if kc == 0 else []), writes=[('ps', pb)])
                        fw.op('act', lambda e, kh=kh, b=b, pb=pb: e.copy(out=kT2[:, kh, 128 + b * BS:128 + (b + 1) * BS],
                                                                        in_=bank(pb)[:, 0:BS]),
                              reads=[('ps', pb)], writes=[('s_k', kh, b)])
                wv, wt = wload(wcols(W, cv0, SKW), KC, SKW)
                for tt in range(NT):
                    b = (tt * 128) // BS
                    pb = fw.rot('s_ps', 2)
                    for kc in range(KC):
                        fw.op('pe', lambda e, kc=kc, tt=tt, pb=pb: e.matmul(
                            bank(pb)[:, 0:SKW], hT[:, kc, tt * 128:(tt + 1) * 128], wv[:, kc, :],
                            start=(kc == 0), stop=(kc == KC - 1)),
                            reads=[wt] + (hT_reads(b) if kc == 0 else []), writes=[('ps', pb)])
                    fw.op('act', lambda e, tt=tt, pb=pb: e.copy(out=v_tm[:, tt + 1, :], in_=bank(pb)[:, 0:SKW]),
                          reads=[('ps', pb)], writes=[('s_v', tt + 1)])
                lb = NTB - 1
                fw.dma('sp', cc_s_in[:, 0:SKV * 128].rearrange("p (k t) -> p k t", k=SKV), kT2[:, :, T:T + 128], 's_x0',
                       reads=[('s_k', kh, lb) for kh in range(SKV)], writes=['ccs0'])
                fw.dma('sp', cc_s_in[:, SKV * 128:SKV * 128 + SKW], v_tm[:, NT, :], 's_x1',
                       reads=[('s_v', NT)], writes=['ccs1'])
                fw.collective(cc_s_in.ap().opt(), cc_s_out.ap().opt(), groups, 's', reads=['ccs0', 'ccs1'],
                              writes=['ccs_out'])
                fw.dma('sp', kT2[:, :, 0:128], cc_s_out[0:128, 0:SKV * 128].rearrange("p (k t) -> p k t", k=SKV), 's_x2',
                       reads=['ccs_out'], writes=[('s_k', kh, -1) for kh in range(SKV)])
                fw.dma('sp', v_tm[:, 0, :], cc_s_out[0:128, SKV * 128:SKV * 128 + SKW], 's_x3',
                       reads=['ccs_out'], writes=[('s_v', 0)])

                for kh in range(SKV):
                    fw.dma('sp', bias[:], swab_in[:, kh * SG:(kh + 1) * SG, :], 's_bias', writes=['s_bias'])

                    def epi_q(mg, msz, b, ps, pst):
                        p = mg // 128
                        fw.op('act', lambda e: e.activation(out=q2T[:, p, b * BS:(b + 1) * BS], in_=ps, func=AF.Copy,
                                                            scale=64.0 ** -0.5), reads=[pst], writes=[('s_q', p, b)])
                    lin_ws(W, cq0 + kh * SG * 64, SG * 64, epi_q, banks=(0, 1))

                    gs = 256 if SG >= 4 else 512
                    slot = lambda g_: (g_ % 2) * (SG // 2) + g_ // 2
                    half_major = [g_ for hf_ in (0, 1) for g_ in range(hf_, SG, 2)]
                    sk = sinks[:, j * c.SQH + kh * SG:j * c.SQH + (kh + 1) * SG]
                    sc_ps = PS[:, 2 * 512:2 * 512 + SG * gs].rearrange("p (g k) -> p g k", g=SG)[:, :, 0:256]
                    psr = [('ps', 2 + i) for i in range((SG * gs + 511) // 512)]

                    def stA(tt):
                        b = (tt * 128) // BS
                        u = tt % 2
                        kreads = [('s_k', kh, -1 if tt == 0 else ((tt - 1) * 128) // BS), ('s_k', kh, b)]
                        for g in half_major:
                            p, half = g // 2, g % 2
                            hs = slice(half * 64, (half + 1) * 64)
                            pb = 2 + (slot(g) * gs) // 512
                            off = (slot(g) * gs) % 512
                            fw.op('pe', lambda e: e.matmul(
                                bank(pb)[:, off:off + 256], q2T[hs, p, tt * 128:(tt + 1) * 128],
                                kT2[hs, kh, tt * 128:tt * 128 + 256], start=True, stop=True),
                                reads=[('s_q', p, b)] + kreads, writes=[('ps', pb)])
                        fw.op('dve', lambda e: e.tensor_tensor(out=s_sb[u][:], in0=sc_ps, in1=bias[:], op=ALU.add),
                              reads=psr + ['s_bias'], writes=[('s_s', u)])
                        if tt == 0:
                            fw.op('dve', lambda e: e.tensor_tensor(
                                out=s_sb[u][:], in0=s_sb[u][:], in1=prem[:, :].unsqueeze(1).to_broadcast([128, SG, 256]),
                                op=ALU.add), reads=[('s_s', u), 's_prem'], writes=[('s_s', u)])
                        fw.op('dve', lambda e: e.tensor_reduce(out=st[u][:, 0, :], in_=s_sb[u][:], axis=AX.X, op=ALU.max),
                              reads=[('s_s', u)], writes=[('s_st', u)])
                        fw.op('dve', lambda e: e.tensor_tensor(out=st[u][:, 0, :], in0=st[u][:, 0, :], in1=sk, op=ALU.max),
                              reads=[('s_st', u), 'c5'], writes=[('s_st', u)])
                        fw.op('dve', lambda e: e.tensor_scalar_mul(out=st[u][:, 1, :], in0=st[u][:, 0, :], scalar1=-1.0),
                              reads=[('s_st', u)], writes=[('s_st', u)])
                        fw.op('dve', lambda e: e.tensor_sub(out=st[u][:, 2, :], in0=sk, in1=st[u][:, 0, :]),
                              reads=[('s_st', u)], writes=[('s_st', u)])
                        fw.op('act', lambda e: e.activation(out=st[u][:, 3, :], in_=st[u][:, 2, :], func=AF.Exp),
                              reads=[('s_st', u)], writes=[('s_st', u)])
                        for g in range(SG):
                            fw.op('act', lambda e, g=g: e.activation(
                                out=s_sb[u][:, g, :], in_=s_sb[u][:, g, :], func=AF.Exp, bias=st[u][:, 1, g:g + 1],
                                accum_out=st[u][:, 4, g:g + 1]), reads=[('s_s', u), ('s_st', u)], writes=[('s_s', u), ('s_st', u)])
                        fw.op('dve', lambda e: e.tensor_add(out=st[u][:, 5, :], in0=st[u][:, 4, :], in1=st[u][:, 3, :]),
                              reads=[('s_st', u)], writes=[('s_st', u)])
                        fw.op('dve', lambda e: e.reciprocal(out=st[u][:, 5, :], in_=st[u][:, 5, :]),
                              reads=[('s_st', u)], writes=[('s_st', u)])
                        fw.op('dve', lambda e: e.tensor_tensor(
                            out=pnb[u][:], in0=s_sb[u][:], in1=st[u][:, 5, :].unsqueeze(2).to_broadcast([128, SG, 256]),
                            op=ALU.mult), reads=[('s_s', u), ('s_st', u)], writes=[('s_pn', u)])

                    def stB(tt):
                        u = tt % 2
                        for i in range(SG * 2):
                            g, kt = i // 2, i % 2
                            tb_ = (i * 128) // 1024
                            off = (i * 128) % 1024
                            fw.op('pe', lambda e: e.transpose(
                                ptbank(tb_)[:, off:off + 128], pnb[u][:, g, kt * 128:(kt + 1) * 128], ident[:]),
                                reads=[('s_pn', u)], writes=[('pt', tb_)])
                        ntb_ = (SG * 2 * 128 + 1023) // 1024
                        for tb_ in range(ntb_):
                            n_i = min(8, SG * 2 - tb_ * 8)
                            fw.op('act' if tb_ == 0 else 'dve', lambda e: (e.copy if tb_ == 0 else e.tensor_copy)(
                                out=pT[u][:, tb_ * 8:tb_ * 8 + n_i, :],
                                in_=ptbank(tb_)[:, 0:n_i * 128].rearrange("p (i t) -> p i t", i=n_i)),
                                reads=[('pt', tb_)], writes=[('s_pT', u, tb_)])
                        for g in half_major:
                            p, half = g // 2, g % 2
                            for kt in range(2):
                                i = slot(g) * 2 + kt
                                fw.op('pe', lambda e: e.matmul(
                                    bank(half)[half * 64:(half + 1) * 64, p * 128:(p + 1) * 128],
                                    v_tm[:, tt + kt, kh * 64:(kh + 1) * 64], pT[u][:, i, :], start=(kt == 0), stop=(kt == 1)),
                                    reads=[('s_pT', u, i // 8), ('s_v', tt + kt)], writes=[('ps', half)])
                        for half in range(2):
                            hs = slice(half * 64, (half + 1) * 64)
                            fw.op('act', lambda e: e.copy(
                                out=ob[u][hs, :, :], in_=bank(half)[hs, 0:NP * 128].rearrange("p (n t) -> p n t", n=NP)),
                                reads=[('ps', half)], writes=[('s_ob', u, half)])
                        r0 = kh * SG * 64
                        fw.dma('sp', mixd[r0:r0 + NP * 128, tt * 128:(tt + 1) * 128].rearrange("(n p) t -> p n t", p=128),
                               ob[u][:], f's_ob{u}', reads=[('s_ob', u, 0), ('s_ob', u, 1)], writes=[('mixd', r0, tt)])

                    stA(0)
                    for tt in range(NT):
                        if tt + 1 < NT:
                            stA(tt + 1)
                        stB(tt)
                fw.barrier()
            return W, cxq, 0

        def out_proj(li, xsrc):
            GW = min(512, D)
            MC = GW // 128
            with ExitStack() as ps_:
                sbl = mk_sbl(ps_)
                xt = [sbl(f"o_x{i}", [128, MC, BS]) for i in range(3)]
                for b in range(NTB):
                    fw.dma('sp', hT[:, :, b * BS:(b + 1) * BS], mixd[:, b * BS:(b + 1) * BS].rearrange("(k p) t -> p k t", p=128),
                           f'o_mix{b}', writes=[('mix', b)])
                items = [(g, b) for g in range(D // GW) for b in range(NTB)]

                def xload(idx):
                    g, b = items[idx]
                    u = idx % 3
                    fw.dma('sp', xt[u][:], xsrc[g * GW:(g + 1) * GW, b * BS:(b + 1) * BS].rearrange("(k p) t -> p k t", p=128),
                           f'o_x{u}', writes=[('o_x', u, m) for m in range(MC)])
                xload(0)
                if len(items) > 1:
                    xload(1)
                W2d = w_out[li]
                wts = {0: wload(wcols(W2d, 0, GW), KC, GW)}
                for idx, (g, b) in enumerate(items):
                    u = idx % 3
                    if b == 0 and (g + 1) * GW < D:
                        wts[g + 1] = wload(wcols(W2d, (g + 1) * GW, GW), KC, GW)
                    wv, wt = wts[g]
                    for m in range(MC):
                        pb = fw.rot('o_ps', 4)
                        for kc in range(KC):
                            fw.op('pe', lambda e, kc=kc: e.matmul(
                                bank(pb)[:, 0:BS], wv[:, kc, m * 128:(m + 1) * 128], hT[:, kc, b * BS:(b + 1) * BS],
                                start=(kc == 0), stop=(kc == KC - 1)),
                                reads=[wt] + ([('mix', b)] if kc == 0 else []), writes=[('ps', pb)])
                        fw.op('dve', lambda e: e.tensor_tensor(out=xt[u][:, m, :], in0=bank(pb)[:, 0:BS], in1=xt[u][:, m, :],
                                                               op=ALU.add),
                              reads=[('ps', pb), ('o_x', u, m)], writes=[('o_x', u, m)])
                    fw.dma('sp', xs[g * GW:(g + 1) * GW, b * BS:(b + 1) * BS].rearrange("(k p) t -> p k t", p=128), xt[u][:],
                           f'o_x{u}', reads=[('o_x', u, m) for m in range(MC)], writes=[('o_x', u, m) for m in range(MC)])
                    if idx + 2 < len(items):
                        xload(idx + 2)
                fw.barrier()

        def mlp(li):
            TH, FG = c.TH, c.FG
            NBH = TH // BS
            FC = FG // 128
            with ExitStack() as ps_:
                sbl = mk_sbl(ps_)
                acc = sbl("m_acc", [128, KC, TH])
                uT = [sbl(f"m_uT{i}", [128, FC, TH], BF16) for i in range(2)]
                rt = [sbl(f"m_rt{i}", [128, BS]) for i in range(2)]
                for hf in range(T // TH):
                    t0 = hf * TH
                    fw.dma_group('sp', [(acc[:, kc, :], xs[kc * 128:(kc + 1) * 128, t0:t0 + TH]) for kc in range(KC)], 'm_acc',
                                 writes=[('m_acc', kc, bb) for kc in range(KC) for bb in range(NBH)])
                    for g in range(c.DFF // FG):
                        uu = g % 2
                        wv, wt = wload(wcols(w_up[li], g * FG, FG), KC, FG)
                        wv2, wt2 = wload(w_down[li][g * FG:(g + 1) * FG, :].rearrange("(k p) m -> p k m", p=128), FC, D)
                        for bb in range(NBH):
                            b = (t0 // BS) + bb
                            for fc in range(FC):
                                pb = fw.rot('m_pu', 3)
                                for kc in range(KC):
                                    fw.op('pe', lambda e, kc=kc, fc=fc, b=b, pb=pb, wv=wv: e.matmul(
                                        bank(pb)[:, 0:BS], wv[:, kc, fc * 128:(fc + 1) * 128], hT[:, kc, b * BS:(b + 1) * BS],
                                        start=(kc == 0), stop=(kc == KC - 1)),
                                        reads=[wt] + (hT_reads(b) if kc == 0 else []), writes=[('ps', pb)])
                                r = fw.rot('m_rt', 2)
                                fw.op('act', lambda e, r=r, pb=pb: e.activation(out=rt[r][:], in_=bank(pb)[:, 0:BS], func=AF.Relu),
                                      reads=[('ps', pb)], writes=[('m_rt', r)])
                                fw.op('act', lambda e, r=r, fc=fc, bb=bb, uu=uu: e.activation(
                                    out=uT[uu][:, fc, bb * BS:(bb + 1) * BS], in_=rt[r][:], func=AF.Square),
                                    reads=[('m_rt', r)], writes=[('m_uT', uu, fc, bb)])
                        for bb in range(NBH):
                            for dc in range(KC):
                                pb = 3 + fw.rot('m_pd', 3)
                                for fc in range(FC):
                                    fw.op('pe', lambda e, fc=fc, dc=dc, bb=bb, pb=pb, wv2=wv2, uu=uu: e.matmul(
                                        bank(pb)[:, 0:BS], wv2[:, fc, dc * 128:(dc + 1) * 128], uT[uu][:, fc, bb * BS:(bb + 1) * BS],
                                        start=(fc == 0), stop=(fc == FC - 1)),
                                        reads=[wt2, ('m_uT', uu, fc, bb)], writes=[('ps', pb)])
                                fw.op('dve', lambda e, dc=dc, bb=bb, pb=pb: e.tensor_tensor(
                                    out=acc[:, dc, bb * BS:(bb + 1) * BS], in0=bank(pb)[:, 0:BS], in1=acc[:, dc, bb * BS:(bb + 1) * BS],
                                    op=ALU.add), reads=[('ps', pb), ('m_acc', dc, bb)], writes=[('m_acc', dc, bb)])
                    fw.dma_group('sp', [(xs[kc * 128:(kc + 1) * 128, t0:t0 + TH], acc[:, kc, :]) for kc in range(KC)], 'm_acc',
                                 reads=[('m_acc', kc, bb) for kc in range(KC) for bb in range(NBH)],
                                 writes=[('m_acc', kc, bb) for kc in range(KC) for bb in range(NBH)])
                fw.barrier()

        stop = getattr(cfg, 'stop_after', None)
        pc = [0]

        def gate(fn):
            def g(*a, **k):
                pc[0] += 1
                if stop is not None and pc[0] > stop:
                    return (None, 0, 0)
                if getattr(cfg, 'scopes', False):
                    with nc.named_scope(f"P{pc[0]:02d}_{fn.__name__}"):
                        return fn(*a, **k)
                return fn(*a, **k)
            return g
        norm_phase, gla_layer, mem_attn, gla_finish, swa_layer, out_proj, mlp = map(
            gate, (norm_phase, gla_layer, mem_attn, gla_finish, swa_layer, out_proj, mlp))
        norm_phase(memT_in, c.MEM, c.c_memg, dst_tile=memnT, tokname='memn')
        for li in range(c.NL):
            xsrc = xT_in if li == 0 else xs
            norm_phase(xsrc, T, c.c_attng + li * KC, dst_tile=hT)
            j = li // 2
            if li % 2 == 0:
                W, cxq, cgo0 = gla_layer(li, j)
                mem_attn(li, W, cxq)
                gla_finish(li, j, W, cgo0)
            else:
                W, cxq = swa_layer(li, j)[:2]
                mem_attn(li, W, cxq)
            out_proj(li, xsrc)
            norm_phase(xs, T, c.c_mlpg + li * KC, dst_tile=hT)
            mlp(li)
        norm_phase(xs, T, c.c_fing, dst_dram=yT_out)
        fw.barrier()
    return nc


def _cols(v):
    return np.ascontiguousarray(np.asarray(v, np.float32).reshape(-1, 128).T)


def make_in_maps(cfg, inputs, seq_halves=2):
    c = cfg
    x = np.asarray(inputs['x'], np.float32)
    mem = np.asarray(inputs['mem'], np.float32)
    B, S = x.shape[0], x.shape[1]
    f = lambda k: np.ascontiguousarray(np.asarray(inputs[k], np.float32))
    bf = ml_dtypes.bfloat16
    ident = np.eye(128, dtype=np.float32).astype(bf)
    onesb = np.ones((128, 128), np.float32).astype(bf)
    onesf = np.ones((128, 128), np.float32)
    jj, ii = np.meshgrid(np.arange(128), np.arange(128), indexing='ij')
    maskbd = ((jj // 64 == ii // 64) & (jj <= ii)).astype(np.float32)
    cmask = np.broadcast_to((np.arange(c.T) % 64 != 0).astype(np.float32)[None, :], (128, c.T)).copy()
    slopes = np.exp2(-8.0 * np.arange(1, c.SQH + 1, dtype=np.float32) / c.SQH).astype(np.float32)
    q = np.arange(128)[:, None]
    kk = np.arange(256)[None, :]
    dist = (q + 128 - kk).astype(np.float32)
    valid = (dist >= 0) & (dist < 128)
    swab = np.where(valid[:, None, :], -slopes[None, :, None] * dist[:, None, :], np.float32(NEG_INF)).astype(np.float32)
    perm = np.array([kh * c.SG + (s_ % (c.SG // 2)) * 2 + s_ // (c.SG // 2) for kh in range(c.SKV) for s_ in range(c.SG)])
    swab = swab[:, perm, :]
    sinks_bc = np.ascontiguousarray(np.broadcast_to(f('sinks')[:, perm].reshape(1, -1), (128, f('sinks').size))) \
        if c.NS > 0 else np.zeros((128, c.SQH), np.float32)
    w_in_swa = f('w_in_swa') if c.NS > 0 else np.zeros((1, c.D, c.SIN), np.float32)
    shared = dict(w_in_gla=f('w_in_gla'), w_gate_up=f('w_gate_up'), w_in_swa=w_in_swa, w_mem_kv=f('w_mem_kv'),
                  w_out=f('w_out'), w_up=f('w_up'), w_down=f('w_down'), sinks_bc=sinks_bc, ident=ident, onesb=onesb,
                  onesf=onesf, maskbd=maskbd, cmask=cmask, swa_bias=np.ascontiguousarray(swab))
    ppb = np.zeros((128, c.NPP), np.float32)
    ppb[:, c.c_memg:c.c_memg + c.KC] = _cols(inputs['mem_norm_g'])
    for l in range(c.NL):
        ppb[:, c.c_attng + l * c.KC:c.c_attng + (l + 1) * c.KC] = _cols(inputs['attn_norm_g'][l])
        ppb[:, c.c_mlpg + l * c.KC:c.c_mlpg + (l + 1) * c.KC] = _cols(inputs['mlp_norm_g'][l])
    ppb[:, c.c_fing:c.c_fing + c.KC] = _cols(inputs['final_norm_g'])
    nb = c.GK // 128
    for g in range(c.NG):
        ppb[:, c.c_bg + g * nb:c.c_bg + (g + 1) * nb] = _cols(inputs['b_gate'][g])
        ppb[:, c.c_gon + g * c.DVC:c.c_gon + (g + 1) * c.DVC] = _cols(inputs['gla_out_norm_g'][g])
    in_maps = []
    for core in range(c.n_cores):
        b, half = core // seq_halves, core % seq_halves
        p = ppb.copy()
        p[:, c.c_flag] = float(half)
        prem = np.zeros((128, 256), np.float32)
        if half == 0:
            prem[:, 0:128] = NEG_INF
        m = dict(shared)
        m['xT'] = np.ascontiguousarray(x[b, half * c.T:(half + 1) * c.T, :].T)
        m['memT'] = np.ascontiguousarray(mem[b].T)
        m['pp'] = p
        m['premask'] = prem
        in_maps.append(m)
    return in_maps


_NC_CACHE = {}


def kernel(**inputs):
    cfg = Cfg()
    if 'nc' not in _NC_CACHE:
        _NC_CACHE['nc'] = build_program(cfg)
    nc = _NC_CACHE['nc']
    in_maps = make_in_maps(cfg, inputs)
    res = run_bass_kernel_spmd(nc, in_maps, core_ids=list(range(cfg.n_cores)))
    B, S = inputs['x'].shape[0], inputs['x'].shape[1]
    out = np.empty((B, S, cfg.D), np.float32)
    for core in range(cfg.n_cores):
        b, half = core // 2, core % 2
        out[b, half * cfg.T:(half + 1) * cfg.T, :] = res.results[core]['yT'].T
    return out
```

```python
import numpy as np
from contextlib import ExitStack
import ml_dtypes
import concourse.bass as bass
import concourse.mybir as mybir
from concourse.bass_utils import run_bass_kernel_spmd

F32 = mybir.dt.float32
BF16 = mybir.dt.bfloat16
AF = mybir.ActivationFunctionType
ALU = mybir.AluOpType
AX = mybir.AxisListType

RMS_EPS = 1e-5
NEG_INF = -1e30
GLA_TAU = 16.0


class Cfg:
    def __init__(s, D=2048, T=2048, NL=4, MEM=256, XH=4, GH=4, DK=256, DV=384, R=16,
                 SKV=3, SG=8, DFF=8192, n_cores=8):
        s.D, s.T, s.NL, s.MEM, s.XH, s.GH, s.DK, s.DV, s.R = D, T, NL, MEM, XH, GH, DK, DV, R
        s.SKV, s.SG, s.DFF, s.n_cores = SKV, SG, DFF, n_cores
        s.KC = D // 128
        s.XW = XH * 128
        s.GK, s.GV = GH * DK, GH * DV
        s.DKC, s.DVC = DK // 128, DV // 128
        s.GIN = 2 * s.GK + 2 * s.GV + R + s.XW
        s.SQH = SKV * SG
        s.SQW, s.SKW = s.SQH * 64, SKV * 64
        s.SIN = s.SQW + 2 * s.SKW + s.XW
        s.MIXW = D
        assert s.GV + s.XW == D and s.SQW + s.XW == D
        s.NT = T // 128
        s.BS = min(512, T)
        s.NTB = T // s.BS
        s.NCH = T // 64
        s.TH = min(1024, T)
        s.FG = 512
        s.NG = (NL + 1) // 2
        s.NS = NL // 2
        c = 0
        s.c_memg = c; c += s.KC
        s.c_attng = c; c += NL * s.KC
        s.c_mlpg = c; c += NL * s.KC
        s.c_fing = c; c += s.KC
        s.c_bg = c; c += s.NG * (s.GK // 128)
        s.c_gon = c; c += s.NG * s.DVC
        s.c_flag = c; c += 1
        s.NPP = c


class Fw:
    def __init__(self, nc, es):
        self.nc, self.es = nc, es
        self.E = {'pe': nc.tensor, 'act': nc.scalar, 'dve': nc.vector, 'pool': nc.gpsimd, 'sp': nc.sync}
        self.cnt = {e: 0 for e in self.E}
        self.sems = {('E_' + e): es.enter_context(nc.semaphore('E_' + e)) for e in self.E}
        self.dcnt = {}
        self.known = {e: {} for e in self.E}
        self.lastw, self.readers = {}, {}
        self.psi = {}

    def _sem(self, sn):
        if sn not in self.sems:
            self.sems[sn] = self.es.enter_context(self.nc.semaphore(sn))
            self.dcnt[sn] = 0
        return self.sems[sn]

    def _wait(self, e, deps):
        for sn, v in deps.items():
            if v <= 0 or self.known[e].get(sn, 0) >= v:
                continue
            if sn == 'E_pe' and e == 'pe':
                continue
            self.E[e].wait_ge(self._sem(sn), v)
            self.known[e][sn] = v

    def _deps(self, reads, writes):
        d = {}
        for r in reads:
            ev = self.lastw.get(r)
            if ev:
                d[ev[0]] = max(d.get(ev[0], 0), ev[1])
        for w in writes:
            ev = self.lastw.get(w)
            if ev:
                d[ev[0]] = max(d.get(ev[0], 0), ev[1])
            for sn, v in self.readers.get(w, {}).items():
                d[sn] = max(d.get(sn, 0), v)
        return d

    def _record(self, ev, reads, writes):
        sn, v = ev
        for r in reads:
            self.readers.setdefault(r, {})[sn] = v
        for w in writes:
            self.lastw[w] = ev
            self.readers[w] = {}

    def op(self, e, fn, reads=(), writes=()):
        self._wait(e, self._deps(reads, writes))
        ins = fn(self.E[e])
        self.cnt[e] += 1
        ins.then_inc(self.sems['E_' + e], 1)
        self._record(('E_' + e, self.cnt[e]), reads, writes)

    def dma(self, q, out, in_, sem, reads=(), writes=()):
        self.dma_group(q, [(out, in_)], sem, reads, writes)

    def dma_group(self, q, pairs, sem, reads=(), writes=()):
        sn = 'D_' + sem
        s = self._sem(sn)
        self._wait(q, self._deps(reads, writes))
        for out, in_ in pairs:
            ins = self.E[q].dma_start(out=out, in_=in_)
            self.dcnt[sn] += 16
            ins.then_inc(s, 16)
        self._record((sn, self.dcnt[sn]), reads, writes)

    def collective(self, ins_ap, outs_ap, groups, sem, reads=(), writes=()):
        sn = 'C_' + sem
        s = self._sem(sn)
        self._wait('pool', self._deps(reads, writes))
        ins = self.nc.gpsimd.collective_compute("AllGather", ALU.bypass, replica_groups=groups,
                                                ins=[ins_ap], outs=[outs_ap])
        self.dcnt[sn] += 1
        ins.then_inc(s, 1)
        self._record((sn, self.dcnt[sn]), reads, writes)

    def barrier(self):
        allev = {('E_' + e): self.cnt[e] for e in self.E}
        allev.update(self.dcnt)
        for e in self.E:
            self._wait(e, allev)
        self.lastw, self.readers = {}, {}

    def rot(self, name, n):
        i = self.psi.get(name, 0)
        self.psi[name] = (i + 1) % n
        return i


def build_program(cfg):
    c = cfg
    D, T, KC, BS, NTB, NT = c.D, c.T, c.KC, c.BS, c.NTB, c.NT
    nc = bass.Bass("TRN2", target_bir_lowering=False)

    def din(name, shape, dt=F32):
        return nc.dram_tensor(name, list(shape), dt, kind="ExternalInput")

    xT_in = din("xT", [D, T])
    memT_in = din("memT", [D, c.MEM])
    w_in_gla = din("w_in_gla", [c.NG, D, c.GIN])
    w_gate_up = din("w_gate_up", [c.NG, c.R, c.GK])
    w_in_swa = din("w_in_swa", [max(c.NS, 1), D, c.SIN])
    w_mem_kv = din("w_mem_kv", [c.NL, D, 2 * c.XW])
    w_out = din("w_out", [c.NL, c.MIXW, D])
    w_up = din("w_up", [c.NL, D, c.DFF])
    w_down = din("w_down", [c.NL, c.DFF, D])
    pp_in = din("pp", [128, c.NPP])
    sinks_in = din("sinks_bc", [128, max(c.NS, 1) * c.SQH])
    ident_in = din("ident", [128, 128], BF16)
    onesb_in = din("onesb", [128, 128], BF16)
    onesf_in = din("onesf", [128, 128])
    maskbd_in = din("maskbd", [128, 128])
    cmask_in = din("cmask", [128, T])
    swab_in = din("swa_bias", [128, c.SQH, 256])
    premask_in = din("premask", [128, 256])
    yT_out = nc.dram_tensor("yT", [D, T], F32, kind="ExternalOutput")

    xs = nc.dram_tensor("xs", [D, T], F32)
    mixd = nc.dram_tensor("mixd", [c.MIXW, T], BF16)
    oTd = nc.dram_tensor("oTd", [c.GV, T], F32)
    qcd = nc.dram_tensor("qcd", [c.GK, T], BF16)
    cc_g_in = nc.dram_tensor("cc_g_in", [c.GK, c.DV], F32)
    cc_g_out = nc.dram_tensor("cc_g_out", [2 * c.GK, c.DV], F32)
    SXW = c.SKV * 128 + c.SKW
    cc_s_in = nc.dram_tensor("cc_s_in", [128, SXW], BF16)
    cc_s_out = nc.dram_tensor("cc_s_out", [256, SXW], BF16)
    groups = [[2 * i, 2 * i + 1] for i in range(c.n_cores // 2)]

    with ExitStack() as es:
        fw = Fw(nc, es)
        sb = lambda name, shape, dt=F32: es.enter_context(nc.sbuf_tensor(name, list(shape), dt))
        uid = [0]

        def mk_sbl(stack):
            def f(name, shape, dt=F32):
                uid[0] += 1
                return stack.enter_context(nc.sbuf_tensor(f"{name}_{uid[0]}", list(shape), dt))
            return f
        hT = sb("hT", [128, KC, T], BF16)
        wslot = [sb(f"wslot{i}", [128, 8192], BF16) for i in range(2)]
        pp = sb("pp_sb", [128, c.NPP])
        negbg = sb("negbg", [128, c.NG * (c.GK // 128)])
        ident = sb("ident_sb", [128, 128], BF16)
        onesb = sb("onesb_sb", [128, 128], BF16)
        onesf = sb("onesf_sb", [128, 128])
        maskbd = sb("maskbd_sb", [128, 128])
        sinks = sb("sinks_sb", [128, max(c.NS, 1) * c.SQH])
        memnT = sb("memnT", [128, KC, c.MEM], BF16)
        PS = es.enter_context(nc.psum_tensor("PS", [128, 6 * 512], F32))
        PT = es.enter_context(nc.psum_tensor("PT", [128, 2048], BF16))

        def bank(i):
            return PS[:, i * 512:(i + 1) * 512]

        def ptbank(i):
            return PT[:, i * 1024:(i + 1) * 1024]

        for dst, src, nm in ((pp, pp_in, 'c0'), (ident, ident_in, 'c1'), (onesb, onesb_in, 'c2'),
                             (onesf, onesf_in, 'c3'), (maskbd, maskbd_in, 'c4'), (sinks, sinks_in, 'c5')):
            fw.dma('sp', dst[:], src[:, :], nm, writes=[nm])
        nbg = c.NG * (c.GK // 128)
        fw.op('dve', lambda e: e.tensor_scalar_mul(out=negbg[:], in0=pp[:, c.c_bg:c.c_bg + nbg], scalar1=-1.0),
              reads=['c0'], writes=['negbg'])
        fw.barrier()

        wctr = [0]

        def wload(view, k, m):
            s = wctr[0] % 2
            wctr[0] += 1
            assert k * m <= 8192
            wv = wslot[s][:, 0:k * m].rearrange("p (k m) -> p k m", k=k)
            fw.dma('pool', wv, view, f'w{s}', writes=[('w', s)])
            return wv, ('w', s)

        def wcols(W2d, c0, ncols, kc=KC):
            return W2d[:, c0:c0 + ncols].rearrange("(k p) m -> p k m", p=128)

        def norm_phase(src, N, gcol, dst_tile=None, dst_dram=None, tokname='hT'):
            bs = min(512, N)
            with ExitStack() as ps_:
                sbl = mk_sbl(ps_)
                xblk = [sbl(f"n_xblk{i}", [128, KC, bs]) for i in range(2)]
                sq = [sbl(f"n_sq{i}", [128, 4, bs], BF16) for i in range(2)]
                rs = [sbl(f"n_rs{i}", [128, bs]) for i in range(2)]
                ot = [sbl(f"n_ot{i}", [128, 4, bs]) for i in range(2)] if dst_dram is not None else None
                for b in range(N // bs):
                    s = b % 2
                    fw.dma('sp', xblk[s][:], src[:, b * bs:(b + 1) * bs].rearrange("(k p) t -> p k t", p=128),
                           f'nx{s}', writes=[('nx', s)])
                    pb = 4 + fw.rot('nps', 2)
                    for g4 in range(KC // 4):
                        s2 = fw.rot('nsq', 2)
                        fw.op('act', lambda e, s=s, s2=s2, g4=g4: e.activation(
                            out=sq[s2][:], in_=xblk[s][:, g4 * 4:(g4 + 1) * 4, :], func=AF.Square),
                            reads=[('nx', s)], writes=[('nsq', s2)])
                        for k in range(4):
                            kk = g4 * 4 + k
                            fw.op('pe', lambda e, s2=s2, k=k, kk=kk, pb=pb: e.matmul(
                                bank(pb)[:, 0:bs], onesb[:], sq[s2][:, k, :], start=(kk == 0), stop=(kk == KC - 1)),
                                reads=[('nsq', s2)], writes=[('ps', pb)])
                    fw.op('act', lambda e, s=s, pb=pb: e.activation(out=rs[s][:], in_=bank(pb)[:, 0:bs], func=AF.Sqrt,
                                                                   scale=1.0 / D, bias=RMS_EPS),
                          reads=[('ps', pb)], writes=[('nrs', s)])
                    fw.op('dve', lambda e, s=s: e.reciprocal(out=rs[s][:], in_=rs[s][:]),
                          reads=[('nrs', s)], writes=[('nrs', s)])
                    if dst_tile is not None:
                        for kc in range(KC):
                            fw.op('dve', lambda e, s=s, kc=kc, b=b: e.scalar_tensor_tensor(
                                out=dst_tile[:, kc, b * bs:(b + 1) * bs], in0=xblk[s][:, kc, :],
                                scalar=pp[:, gcol + kc:gcol + kc + 1], in1=rs[s][:], op0=ALU.mult, op1=ALU.mult),
                                reads=[('nx', s), ('nrs', s)], writes=[(tokname, kc, b)])
                    else:
                        for g4 in range(KC // 4):
                            so = fw.rot('not', 2)
                            for k in range(4):
                                kc = g4 * 4 + k
                                fw.op('dve', lambda e, s=s, kc=kc, k=k, so=so: e.scalar_tensor_tensor(
                                    out=ot[so][:, k, :], in0=xblk[s][:, kc, :],
                                    scalar=pp[:, gcol + kc:gcol + kc + 1], in1=rs[s][:], op0=ALU.mult, op1=ALU.mult),
                                    reads=[('nx', s), ('nrs', s)], writes=[('not', so)])
                            fw.dma('sp', dst_dram[g4 * 512:(g4 + 1) * 512, b * bs:(b + 1) * bs].rearrange(
                                "(k p) t -> p k t", p=128), ot[so][:], f'not{so}', reads=[('not', so)],
                                writes=[('yT', g4, b)])
                fw.barrier()

        def hT_reads(b):
            return [('hT', kc, b) for kc in range(KC)]

        def lin_ws(W2d, c0, ncols, epi, banks=(0, 1, 2, 3), src=None, src_reads=None, kc_n=KC, wview=None,
                   tbs=None):
            src = hT if src is None else src
            src_reads = hT_reads if src_reads is None else src_reads
            tbs = range(NTB) if tbs is None else tbs
            gw = 8192 // kc_n
            gw = min(gw, 512)
            for g0 in range(0, ncols, gw):
                gsz = min(gw, ncols - g0)
                wv, wt = wload(wcols(W2d, c0 + g0, gsz) if wview is None else wview(g0, gsz), kc_n, gsz)
                for b in tbs:
                    for m0 in range(0, gsz, 128):
                        msz = min(128, gsz - m0)
                        pb = banks[fw.rot(('lin', banks), len(banks))]
                        for kc in range(kc_n):
                            fw.op('pe', lambda e, kc=kc, m0=m0, msz=msz, b=b, pb=pb, wv=wv: e.matmul(
                                bank(pb)[0:msz, 0:BS], wv[:, kc, m0:m0 + msz], src[:, kc, b * BS:(b + 1) * BS],
                                start=(kc == 0), stop=(kc == kc_n - 1)),
                                reads=[wt] + (src_reads(b) if kc == 0 else []), writes=[('ps', pb)])
                        epi(g0 + m0, msz, b, bank(pb)[0:msz, 0:BS], ('ps', pb))

        def mem_kv(li, mkT, mv_tm):
            W = w_mem_kv[li]
            MT = c.MEM // 128
            for g0 in range(0, c.XW, 512):
                gsz = min(512, c.XW - g0)
                wv, wt = wload(wcols(W, g0, gsz), KC, gsz)
                for m0 in range(0, gsz, 128):
                    h = (g0 + m0) // 128
                    pb = fw.rot('mk', 2)
                    for kc in range(KC):
                        fw.op('pe', lambda e, kc=kc, m0=m0, pb=pb, wv=wv: e.matmul(
                            bank(pb)[:, 0:c.MEM], wv[:, kc, m0:m0 + 128], memnT[:, kc, :],
                            start=(kc == 0), stop=(kc == KC - 1)), reads=[wt, 'memnT'], writes=[('ps', pb)])
                    fw.op('act', lambda e, h=h, pb=pb: e.copy(out=mkT[:, h, :], in_=bank(pb)[:, 0:c.MEM]),
                          reads=[('ps', pb)], writes=[('mkT', h)])
            for g0 in range(0, c.XW, 512):
                gsz = min(512, c.XW - g0)
                wv, wt = wload(wcols(W, c.XW + g0, gsz), KC, gsz)
                for mt in range(MT):
                    pb = fw.rot('mk', 2)
                    for kc in range(KC):
                        fw.op('pe', lambda e, kc=kc, mt=mt, pb=pb, wv=wv, gsz=gsz: e.matmul(
                            bank(pb)[:, 0:gsz], memnT[:, kc, mt * 128:(mt + 1) * 128], wv[:, kc, :],
                            start=(kc == 0), stop=(kc == KC - 1)), reads=[wt, 'memnT'], writes=[('ps', pb)])
                    fw.op('act', lambda e, mt=mt, pb=pb, g0=g0, gsz=gsz: e.copy(
                        out=mv_tm[:, mt, g0:g0 + gsz], in_=bank(pb)[:, 0:gsz]),
                        reads=[('ps', pb)], writes=[('mv', mt, g0)])

        def mem_attn(li, Win2d, cxq):
            MT = c.MEM // 128
            with ExitStack() as ps_:
                sbl = mk_sbl(ps_)
                mkT = sbl("ma_mkT", [128, c.XH, c.MEM], BF16)
                mv_tm = sbl("ma_mv", [128, MT, c.XW], BF16)
                xqT = sbl("ma_xqT", [128, c.XH, T], BF16)
                s_sb = [sbl(f"ma_s{i}", [128, c.MEM]) for i in range(4)]
                pn = [sbl(f"ma_pn{i}", [128, c.MEM], BF16) for i in range(4)]
                st = [sbl(f"ma_st{i}", [128, 4]) for i in range(4)]
                pT = [sbl(f"ma_pT{i}", [128, MT, 128], BF16) for i in range(4)]
                ob = [sbl(f"ma_ob{i}", [128, 128], BF16) for i in range(4)]
                mem_kv(li, mkT, mv_tm)

                def epi_xq(mg, msz, b, ps, pst):
                    h = mg // 128
                    fw.op('act', lambda e: e.activation(out=xqT[:, h, b * BS:(b + 1) * BS], in_=ps, func=AF.Copy,
                                                        scale=128.0 ** -0.5),
                          reads=[pst], writes=[('xq', h, b)])
                lin_ws(Win2d, cxq, c.XW, epi_xq, banks=(4, 5))
                mv_reads = [('mv', mt, g0) for mt in range(MT) for g0 in range(0, c.XW, 512)]
                units = [(tt, h) for tt in range(NT) for h in range(c.XH)]
                NU = 4

                def stA(n):
                    tt, h = units[n]
                    b = (tt * 128) // BS
                    u = n % NU
                    pb = fw.rot('ma_ps', 3)
                    fw.op('pe', lambda e: e.matmul(
                        bank(pb)[:, 0:c.MEM], xqT[:, h, tt * 128:(tt + 1) * 128], mkT[:, h, :], start=True, stop=True),
                        reads=[('xq', h, b), ('mkT', h)], writes=[('ps', pb)])
                    fw.op('dve', lambda e: e.reduce_max(out=st[u][:, 0:1], in_=bank(pb)[:, 0:c.MEM], axis=AX.X),
                          reads=[('ps', pb)], writes=[('ma_st', u)])
                    fw.op('dve', lambda e: e.tensor_scalar_mul(out=st[u][:, 1:2], in0=st[u][:, 0:1], scalar1=-1.0),
                          reads=[('ma_st', u)], writes=[('ma_st', u)])
                    fw.op('act', lambda e: e.activation(out=s_sb[u][:], in_=bank(pb)[:, 0:c.MEM], func=AF.Exp,
                                                        bias=st[u][:, 1:2], accum_out=st[u][:, 2:3]),
                          reads=[('ps', pb), ('ma_st', u)], writes=[('ma_s', u), ('ma_st2', u)])
                    fw.op('dve', lambda e: e.reciprocal(out=st[u][:, 3:4], in_=st[u][:, 2:3]),
                          reads=[('ma_st2', u)], writes=[('ma_st3', u)])
                    fw.op('dve', lambda e: e.tensor_scalar_mul(out=pn[u][:], in0=s_sb[u][:], scalar1=st[u][:, 3:4]),
                          reads=[('ma_s', u), ('ma_st3', u)], writes=[('ma_pn', u)])

                def stB(n):
                    u = n % NU
                    tb_ = fw.rot('pt', 2)
                    for mt in range(MT):
                        fw.op('pe', lambda e, mt=mt: e.transpose(
                            ptbank(tb_)[:, mt * 128:(mt + 1) * 128], pn[u][:, mt * 128:(mt + 1) * 128], ident[:]),
                            reads=[('ma_pn', u)], writes=[('pt', tb_)])
                    fw.op('act', lambda e: e.copy(
                        out=pT[u][:], in_=ptbank(tb_)[:, 0:MT * 128].rearrange("p (m t) -> p m t", m=MT)),
                        reads=[('pt', tb_)], writes=[('ma_pT', u)])

                def stC(n):
                    tt, h = units[n]
                    u = n % NU
                    pb2 = 3 + fw.rot('ma_ps2', 3)
                    for mt in range(MT):
                        fw.op('pe', lambda e, mt=mt: e.matmul(
                            bank(pb2)[:, 0:128], mv_tm[:, mt, h * 128:(h + 1) * 128], pT[u][:, mt, :],
                            start=(mt == 0), stop=(mt == MT - 1)),
                            reads=[('ma_pT', u)] + mv_reads, writes=[('ps', pb2)])
                    fw.op('act', lambda e: e.copy(out=ob[u][:], in_=bank(pb2)[:, 0:128]),
                          reads=[('ps', pb2)], writes=[('ma_ob', u)])
                    r0 = c.MIXW - c.XW + h * 128
                    fw.dma('sp', mixd[r0:r0 + 128, tt * 128:(tt + 1) * 128], ob[u][:], f'ma_ob{u}',
                           reads=[('ma_ob', u)], writes=[('mixd', r0, tt)])

                for n in range(len(units) + 2):
                    if n < len(units):
                        stA(n)
                    if 0 <= n - 1 < len(units):
                        stB(n - 1)
                    if 0 <= n - 2 < len(units):
                        stC(n - 2)
                fw.barrier()

        def gla_layer(li, j):
            W = w_in_gla[j]
            cq0, ck0, cv0, cgo0 = 0, c.GK, 2 * c.GK, 2 * c.GK + c.GV
            cglr = 2 * c.GK + 2 * c.GV
            cxq = cglr + c.R
            DKC, DVC, DK, DV, NCH = c.DKC, c.DVC, c.DK, c.DV, c.NCH
            i16 = 1.0 / GLA_TAU
            with ExitStack() as ps_:
                sbl = mk_sbl(ps_)
                glr = sbl("g_glr", [c.R, T], BF16)
                wgu = sbl("g_wgu", [c.R, c.GK], BF16)
                cmask = sbl("g_cmask", [128, BS])
                lbuf = [sbl(f"g_l{i}", [128, BS]) for i in range(2)]
                cs = [sbl(f"g_cs{i}", [128, T]) for i in range(DKC)]
                etmp = [sbl(f"g_et{i}", [128, BS]) for i in range(1)]
                ebt = [sbl(f"g_eb{i}", [128, BS]) for i in range(1)]
                enbt = [sbl(f"g_enb{i}", [128, BS]) for i in range(1)]
                ekt = [sbl(f"g_ek{i}", [128, BS]) for i in range(1)]
                kot = [sbl(f"g_kot{i}", [128, BS], BF16) for i in range(2)]
                qct = [sbl(f"g_qct{i}", [128, BS], BF16) for i in range(2)]
                dec = sbl("g_dec", [128, DKC, NCH])
                ncl = sbl("g_ncl", [128, DKC, NCH])
                pn_ = sbl("g_pn", [128, DKC, NCH])
                sc1 = sbl("g_sc1", [128, NCH])
                sc2 = sbl("g_sc2", [128, NCH])
                q_in = sbl("g_qin", [128, DKC, T], BF16)
                k_in = sbl("g_kin", [128, DKC, T], BF16)
                k_out = sbl("g_kout", [128, NT, DK], BF16)
                v_tm = sbl("g_v", [128, NT, DV], BF16)
                S_f = [sbl(f"g_Sf{i}", [128, DKC, DV]) for i in range(2)]
                S_b = [sbl(f"g_Sb{i}", [128, DKC, DV], BF16) for i in range(2)]
                AT = [sbl(f"g_AT{i}", [128, 128], BF16) for i in range(2)]
                o_sb = [sbl(f"g_o{i}", [128, DVC, 128]) for i in range(2)]

                fw.dma('sp', cmask[:], cmask_in[:, 0:BS], 'g_cm', writes=['cmask'])
                fw.dma('pool', wgu[:], w_gate_up[j], 'g_wgu', writes=['wgu'])

                def epi_glr(mg, msz, b, ps, pst):
                    fw.op('act', lambda e: e.copy(out=glr[0:c.R, b * BS:(b + 1) * BS], in_=ps),
                          reads=[pst], writes=[('glr', b)])
                lin_ws(W, cglr, c.R, epi_glr)

                for h in range(c.GH):
                    for jj in range(DKC):
                        cg = h * DKC + jj
                        for b in range(NTB):
                            pb = fw.rot('g_ps', 4)
                            fw.op('pe', lambda e, cg=cg, b=b, pb=pb: e.matmul(
                                bank(pb)[:, 0:BS], wgu[0:c.R, cg * 128:(cg + 1) * 128], glr[0:c.R, b * BS:(b + 1) * BS],
                                start=True, stop=True), reads=['wgu', ('glr', b)], writes=[('ps', pb)])
                            u = 0
                            lb_ = fw.rot('g_l', 2)
                            bcol = j * (c.GK // 128) + cg
                            fw.op('act', lambda e, u=u, pb=pb, bcol=bcol: e.activation(
                                out=etmp[u][:], in_=bank(pb)[:, 0:BS], func=AF.Exp, scale=-1.0,
                                bias=negbg[:, bcol:bcol + 1]), reads=[('ps', pb), 'negbg'], writes=[('g_et', u)])
                            fw.op('act', lambda e, u=u, lb_=lb_: e.activation(
                                out=lbuf[lb_][:], in_=etmp[u][:], func=AF.Ln, bias=1.0),
                                reads=[('g_et', u)], writes=[('g_l', lb_)])
                            fw.op('dve', lambda e, jj=jj, b=b, lb_=lb_: e.tensor_tensor_scan(
                                out=cs[jj][:, b * BS:(b + 1) * BS], data0=cmask[:], data1=lbuf[lb_][:], initial=0.0,
                                op0=ALU.mult, op1=ALU.add),
                                reads=['cmask', ('g_l', lb_)], writes=[('g_cs', jj)])
                        csl = cs[jj][:, :].rearrange("p (n q) -> p n q", q=64)[:, :, 63]
                        fw.op('act', lambda e, jj=jj, csl=csl: e.activation(out=dec[:, jj, :], in_=csl, func=AF.Exp, scale=-i16),
                              reads=[('g_cs', jj)], writes=[('g_dec', jj)])
                        fw.op('dve', lambda e, jj=jj, csl=csl: e.tensor_scalar_mul(out=ncl[:, jj, :], in0=csl, scalar1=-i16),
                              reads=[('g_cs', jj)], writes=[('g_ncl', jj)])
                        fw.op('dve', lambda e, csl=csl: e.tensor_tensor_scan(
                            out=sc1[:], data0=onesf[:, 0:NCH], data1=csl, initial=0.0, op0=ALU.mult, op1=ALU.add),
                            reads=[('g_cs', jj), 'c3'], writes=['g_sc1'])
                        fw.op('dve', lambda e, csl=csl: e.tensor_sub(out=sc2[:], in0=sc1[:], in1=csl),
                              reads=['g_sc1', ('g_cs', jj)], writes=['g_sc2'])
                        fw.op('act', lambda e, jj=jj: e.activation(out=pn_[:, jj, :], in_=sc2[:], func=AF.Exp, scale=-i16),
                              reads=['g_sc2'], writes=[('g_pn', jj)])

                    def decay_blocks(jj, b):
                        u = 0
                        blk = cs[jj][:, b * BS:(b + 1) * BS]
                        fw.op('act', lambda e: e.activation(out=ebt[u][:], in_=blk, func=AF.Exp, scale=-i16),
                              reads=[('g_cs', jj)], writes=[('g_eb', u)])
                        fw.op('act', lambda e: e.activation(out=enbt[u][:], in_=blk, func=AF.Exp, scale=i16),
                              reads=[('g_cs', jj)], writes=[('g_enb', u)])
                        for n in range(BS // 64):
                            ng = b * (BS // 64) + n
                            fw.op('act', lambda e, n=n, ng=ng: e.activation(
                                out=ekt[u][:, n * 64:(n + 1) * 64], in_=cs[jj][:, ng * 64:(ng + 1) * 64], func=AF.Exp,
                                scale=i16, bias=ncl[:, jj, ng:ng + 1]),
                                reads=[('g_cs', jj), ('g_ncl', jj)], writes=[('g_ek', u, n)])
                        return u

                    def epi_q(mg, msz, b, ps, pst):
                        jj = mg // 128
                        u = decay_blocks(jj, b)
                        fw.op('dve', lambda e: e.scalar_tensor_tensor(
                            out=q_in[:, jj, b * BS:(b + 1) * BS], in0=ps, scalar=float(DK) ** -0.5, in1=ebt[u][:],
                            op0=ALU.mult, op1=ALU.mult), reads=[pst, ('g_eb', u)], writes=[('g_qin', jj, b)])
                        nb = BS // 64
                        v = fw.rot('g_qct', 2)
                        fw.op('dve', lambda e: e.tensor_tensor(
                            out=qct[v][:, :].rearrange("p (n q) -> p n q", q=64),
                            in0=q_in[:, jj, b * BS:(b + 1) * BS].rearrange("p (n q) -> p n q", q=64),
                            in1=pn_[:, jj, b * nb:(b + 1) * nb].unsqueeze(2).to_broadcast([128, nb, 64]), op=ALU.mult),
                            reads=[('g_qin', jj, b), ('g_pn', jj)], writes=[('g_qct', v)])
                        r0 = (h * DKC + jj) * 128
                        fw.dma('sp', qcd[r0:r0 + 128, b * BS:(b + 1) * BS], qct[v][:], f'g_qct{v}',
                               reads=[('g_qct', v)], writes=[('qcd', r0, b)])
                    lin_ws(W, cq0 + h * DK, DK, epi_q)

                    def epi_k(mg, msz, b, ps, pst):
                        jj = mg // 128
                        u = decay_blocks(jj, b)
                        fw.op('dve', lambda e: e.tensor_tensor(out=k_in[:, jj, b * BS:(b + 1) * BS], in0=ps, in1=enbt[u][:],
                                                               op=ALU.mult),
                              reads=[pst, ('g_enb', u)], writes=[('g_kin', jj, b)])
                        v = fw.rot('g_kot', 2)
                        fw.op('dve', lambda e: e.tensor_tensor(out=kot[v][:], in0=ps, in1=ekt[u][:], op=ALU.mult),
                              reads=[pst] + [('g_ek', u, n) for n in range(BS // 64)], writes=[('g_kot', v)])
                        for t4 in range(BS // 128):
                            tt = b * (BS // 128) + t4
                            tb_ = fw.rot('pt', 2)
                            fw.op('pe', lambda e, t4=t4, tb_=tb_: e.transpose(
                                ptbank(tb_)[:, 0:128], kot[v][:, t4 * 128:(t4 + 1) * 128], ident[:]),
                                reads=[('g_kot', v)], writes=[('pt', tb_)])
                            fw.op('act', lambda e, tt=tt, tb_=tb_: e.copy(out=k_out[:, tt, jj * 128:(jj + 1) * 128],
                                                                         in_=ptbank(tb_)[:, 0:128]),
                                  reads=[('pt', tb_)], writes=[('g_kout', tt, jj)])
                    lin_ws(W, ck0 + h * DK, DK, epi_k)

                    wv, wt = wload(wcols(W, cv0 + h * DV, DV), KC, DV)
                    for tt in range(NT):
                        b = (tt * 128) // BS
                        pb = fw.rot('g_ps', 4)
                        for kc in range(KC):
                            fw.op('pe', lambda e, kc=kc, tt=tt, pb=pb: e.matmul(
                                bank(pb)[:, 0:DV], hT[:, kc, tt * 128:(tt + 1) * 128], wv[:, kc, :],
                                start=(kc == 0), stop=(kc == KC - 1)),
                                reads=[wt] + (hT_reads(b) if kc == 0 else []), writes=[('ps', pb)])
                        fw.op('act', lambda e, tt=tt, pb=pb: e.copy(out=v_tm[:, tt, :], in_=bank(pb)[:, 0:DV]),
                              reads=[('ps', pb)], writes=[('g_v', tt)])

                    fw.op('dve', lambda e: e.memset(S_f[0][:], 0.0), writes=[('g_Sf', 0, jj) for jj in range(DKC)])
                    fw.op('dve', lambda e: e.memset(S_b[0][:], 0.0), writes=[('g_Sb', 0, jj) for jj in range(DKC)])
                    def emit_AT(tt):
                        b = (tt * 128) // BS
                        tsl = slice(tt * 128, (tt + 1) * 128)
                        pa = fw.rot('g_ps', 4)
                        for jj in range(DKC):
                            fw.op('pe', lambda e, jj=jj: e.matmul(
                                bank(pa)[:, 0:128], k_in[:, jj, tsl], q_in[:, jj, tsl], start=(jj == 0), stop=(jj == DKC - 1)),
                                reads=[('g_kin', jj, b), ('g_qin', jj, b)], writes=[('ps', pa)])
                        a = tt % 2
                        fw.op('dve', lambda e: e.tensor_tensor(out=AT[a][:], in0=bank(pa)[:, 0:128], in1=maskbd[:], op=ALU.mult),
                              reads=[('ps', pa), 'c4'], writes=[('g_AT', a)])

                    emit_AT(0)
                    for tt in range(NT):
                        b = (tt * 128) // BS
                        tsl = slice(tt * 128, (tt + 1) * 128)
                        if tt + 1 < NT:
                            emit_AT(tt + 1)
                        a = tt % 2
                        ob_ = fw.rot('g_o', 2)
                        po = 4 + fw.rot('g_po', 2)
                        for cc in range(2):
                            n = tt * 2 + cc
                            cur, nxt = n % 2, (n + 1) % 2
                            rows = slice(cc * 64, (cc + 1) * 64)
                            csl_ = slice(tt * 128 + cc * 64, tt * 128 + (cc + 1) * 64)
                            for jj in range(DKC):
                                pu = fw.rot('g_ps', 4)
                                fw.op('pe', lambda e, jj=jj: e.matmul(
                                    bank(pu)[:, 0:DV], k_out[rows, tt, jj * 128:(jj + 1) * 128], v_tm[rows, tt, :],
                                    start=True, stop=True),
                                    reads=[('g_kout', tt, jj), ('g_v', tt)], writes=[('ps', pu)])
                                fw.op('dve', lambda e, jj=jj: e.scalar_tensor_tensor(
                                    out=S_f[nxt][:, jj, :], in0=S_f[cur][:, jj, :], scalar=dec[:, jj, n:n + 1], in1=bank(pu)[:, 0:DV],
                                    op0=ALU.mult, op1=ALU.add),
                                    reads=[('g_Sf', cur, jj), ('g_dec', jj), ('ps', pu)], writes=[('g_Sf', nxt, jj)])
                                fw.op('act', lambda e, jj=jj: e.copy(out=S_b[nxt][:, jj, :], in_=S_f[nxt][:, jj, :]),
                                      reads=[('g_Sf', nxt, jj)], writes=[('g_Sb', nxt, jj)])
                            for dvc in range(DVC):
                                ocol = slice(dvc * 128 + cc * 64, dvc * 128 + (cc + 1) * 64)
                                for jj in range(DKC):
                                    fw.op('pe', lambda e, jj=jj: e.matmul(
                                        bank(po)[:, ocol], S_b[cur][:, jj, dvc * 128:(dvc + 1) * 128], q_in[:, jj, csl_],
                                        start=(jj == 0), stop=False),
                                        reads=[('g_Sb', cur, jj), ('g_qin', jj, b)], writes=[('ps', po)])
                                fw.op('pe', lambda e: e.matmul(
                                    bank(po)[:, ocol], v_tm[rows, tt, dvc * 128:(dvc + 1) * 128],
                                    AT[a][rows, cc * 64:(cc + 1) * 64], start=False, stop=True),
                                    reads=[('g_v', tt), ('g_AT', a)], writes=[('ps', po)])
                        fw.op('act', lambda e: e.copy(
                            out=o_sb[ob_][:], in_=bank(po)[:, 0:DVC * 128].rearrange("p (d t) -> p d t", d=DVC)),
                            reads=[('ps', po)], writes=[('g_o', ob_)])
                        fw.dma('sp', oTd[h * DV:(h + 1) * DV, tsl].rearrange("(d p) t -> p d t", p=128), o_sb[ob_][:],
                               f'g_o{ob_}', reads=[('g_o', ob_)], writes=[('oTd', h, tt)])
                    fin = NCH % 2
                    fw.dma('sp', cc_g_in[h * DK:(h + 1) * DK, :].rearrange("(j p) v -> p j v", p=128), S_f[fin][:],
                           'g_Sst', reads=[('g_Sf', fin, jj) for jj in range(DKC)], writes=[('ccg', h)])
                fw.collective(cc_g_in.ap().opt(), cc_g_out.ap().opt(), groups, 'g',
                              reads=[('ccg', h) for h in range(c.GH)], writes=['ccg_out'])
                fw.barrier()
            return W, cxq, cgo0

        def gla_finish(li, j, W, cgo0):
            DKC, DVC, DK, DV = c.DKC, c.DVC, c.DK, c.DV
            with ExitStack() as ps_:
                sbl = mk_sbl(ps_)
                S0f = sbl("f_S0f", [128, DKC, DV])
                S0b = sbl("f_S0b", [128, DKC, DV], BF16)
                qc = [sbl(f"f_qc{i}", [128, DKC, BS], BF16) for i in range(2)]
                ob = [sbl(f"f_ob{i}", [128, DVC, BS]) for i in range(2)]
                sq = [sbl(f"f_sq{i}", [128, DVC, BS], BF16) for i in range(2)]
                rs = [sbl(f"f_rs{i}", [128, BS]) for i in range(2)]
                sg = [sbl(f"f_sg{i}", [128, BS]) for i in range(2)]
                tm = [sbl(f"f_tm{i}", [128, BS]) for i in range(2)]
                mx = [sbl(f"f_mx{i}", [128, BS], BF16) for i in range(2)]
                units = [(h, b) for h in range(c.GH) for b in range(NTB)]
                wts = {}

                def stA(n):
                    h, b = units[n]
                    u = n % 2
                    bsl = slice(b * BS, (b + 1) * BS)
                    if b == 0:
                        fw.dma('sp', S0f[:], cc_g_out[h * DK:(h + 1) * DK, :].rearrange("(j p) v -> p j v", p=128), 'f_S0',
                               writes=['f_S0f'])
                        fw.op('dve', lambda e: e.tensor_scalar_mul(out=S0b[:], in0=S0f[:], scalar1=pp[:, c.c_flag:c.c_flag + 1]),
                              reads=['f_S0f'], writes=['f_S0b'])
                    fw.dma('sp', qc[u][:], qcd[h * DK:(h + 1) * DK, bsl].rearrange("(j p) t -> p j t", p=128), f'f_qc{u}',
                           writes=[('f_qc', u)])
                    fw.dma('sp', ob[u][:], oTd[h * DV:(h + 1) * DV, bsl].rearrange("(d p) t -> p d t", p=128), f'f_ob{u}',
                           writes=[('f_ob', u)])
                    for dvc in range(DVC):
                        pb = fw.rot('f_ps', 2)
                        for jj in range(DKC):
                            fw.op('pe', lambda e, jj=jj: e.matmul(
                                bank(pb)[:, 0:BS], S0b[:, jj, dvc * 128:(dvc + 1) * 128], qc[u][:, jj, :],
                                start=(jj == 0), stop=(jj == DKC - 1)),
                                reads=['f_S0b', ('f_qc', u)], writes=[('ps', pb)])
                        fw.op('dve', lambda e: e.tensor_tensor(
                            out=ob[u][:, dvc, :], in0=bank(pb)[:, 0:BS], in1=ob[u][:, dvc, :], op=ALU.add),
                            reads=[('ps', pb), ('f_ob', u)], writes=[('f_ob', u)])
                    fw.op('act', lambda e: e.activation(out=sq[u][:], in_=ob[u][:], func=AF.Square),
                          reads=[('f_ob', u)], writes=[('f_sq', u)])
                    pn = 2 + fw.rot('f_pn', 2)
                    for dvc in range(DVC):
                        fw.op('pe', lambda e: e.matmul(
                            bank(pn)[:, 0:BS], onesb[:], sq[u][:, dvc, :], start=(dvc == 0), stop=(dvc == DVC - 1)),
                            reads=[('f_sq', u)], writes=[('ps', pn)])
                    fw.op('act', lambda e: e.activation(out=rs[u][:], in_=bank(pn)[:, 0:BS], func=AF.Sqrt,
                                                        scale=1.0 / DV, bias=RMS_EPS),
                          reads=[('ps', pn)], writes=[('f_rs', u)])
                    fw.op('dve', lambda e: e.reciprocal(out=rs[u][:], in_=rs[u][:]),
                          reads=[('f_rs', u)], writes=[('f_rs', u)])

                def stB(n):
                    h, b = units[n]
                    u = n % 2
                    bsl = slice(b * BS, (b + 1) * BS)
                    if b == 0 and h + 1 < c.GH:
                        wts[h + 1] = wload(wcols(W, cgo0 + (h + 1) * DV, DV), KC, DV)
                    wv, wt = wts[h]
                    for dvc in range(DVC):
                        pg = 4 + fw.rot('f_pg', 2)
                        for kc in range(KC):
                            fw.op('pe', lambda e, kc=kc: e.matmul(
                                bank(pg)[:, 0:BS], wv[:, kc, dvc * 128:(dvc + 1) * 128], hT[:, kc, b * BS:(b + 1) * BS],
                                start=(kc == 0), stop=(kc == KC - 1)),
                                reads=[wt] + (hT_reads(b) if kc == 0 else []), writes=[('ps', pg)])
                        v = fw.rot('f_sg', 2)
                        fw.op('act', lambda e: e.activation(out=sg[v][:], in_=bank(pg)[:, 0:BS], func=AF.Silu),
                              reads=[('ps', pg)], writes=[('f_sg', v)])
                        gcol = c.c_gon + j * DVC + dvc
                        fw.op('dve', lambda e: e.scalar_tensor_tensor(
                            out=tm[v][:], in0=ob[u][:, dvc, :], scalar=pp[:, gcol:gcol + 1], in1=rs[u][:],
                            op0=ALU.mult, op1=ALU.mult), reads=[('f_ob', u), ('f_rs', u)], writes=[('f_tm', v)])
                        fw.op('dve', lambda e: e.tensor_tensor(out=mx[v][:], in0=tm[v][:], in1=sg[v][:], op=ALU.mult),
                              reads=[('f_tm', v), ('f_sg', v)], writes=[('f_mx', v)])
                        r0 = h * DV + dvc * 128
                        fw.dma('sp', mixd[r0:r0 + 128, bsl], mx[v][:], f'f_mx{v}', reads=[('f_mx', v)],
                               writes=[('mixd', r0, b)])

                wts[0] = wload(wcols(W, cgo0, DV), KC, DV)
                stA(0)
                for n in range(len(units)):
                    if n + 1 < len(units):
                        stA(n + 1)
                    stB(n)
                fw.barrier()

        def swa_layer(li, j):
            W = w_in_swa[j]
            cq0, ck0, cv0, cxq = 0, c.SQW, c.SQW + c.SKW, c.SQW + 2 * c.SKW
            SG, SKV, SKW = c.SG, c.SKV, c.SKW
            NP = SG // 2
            with ExitStack() as ps_:
                sbl = mk_sbl(ps_)
                kT2 = sbl("s_kT2", [128, SKV, 128 + T], BF16)
                v_tm = sbl("s_v", [128, NT + 1, SKW], BF16)
                q2T = sbl("s_q2T", [128, NP, T], BF16)
                bias = sbl("s_bias", [128, SG, 256])
                prem = sbl("s_prem", [128, 256])
                s_sb = [sbl(f"s_s{i}", [128, SG, 256]) for i in range(2)]
                pnb = [sbl(f"s_pn{i}", [128, SG, 256], BF16) for i in range(2)]
                st = [sbl(f"s_st{i}", [128, 6, SG]) for i in range(2)]
                pT = [sbl(f"s_pT{i}", [128, SG * 2, 128], BF16) for i in range(2)]
                ob = [sbl(f"s_ob{i}", [128, NP, 128], BF16) for i in range(2)]
                fw.dma('sp', prem[:], premask_in[:, :], 's_prem', writes=['s_prem'])

                for kh in range(SKV):
                    s = wctr[0] % 2
                    wctr[0] += 1
                    wv = wslot[s][:, 0:KC * 128].rearrange("p (k m) -> p k m", k=KC)
                    src_ = wcols(W, ck0 + kh * 64, 64)
                    fw.dma_group('pool', [(wv[:, :, 0:64], src_), (wv[:, :, 64:128], src_)], f'w{s}', writes=[('w', s)])
                    for b in range(NTB):
                        pb = fw.rot('s_ps', 2)
                        for kc in range(KC):
                            fw.op('pe', lambda e, kc=kc, b=b, pb=pb, wv=wv: e.matmul(
                                bank(pb)[:, 0:BS], wv[:, kc, :], hT[:, kc, b * BS:(b + 1) * BS],
                                start=(kc == 0), stop=(kc == KC - 1)),
                                reads=[('w', s)] + (hT_reads(b) if kc == 0 else []), writes=[('ps', pb)])
                        fw.op('act', lambda e, kh=kh, b=b, pb=pb: e.copy(out=kT2[:, kh, 128 + b * BS:128 + (b + 1) * BS],
                                                                        in_=bank(pb)[:, 0:BS]),
                              reads=[('ps', pb)], writes=[('s_k', kh, b)])
                wv, wt = wload(wcols(W, cv0, SKW), KC, SKW)
                for tt in range(NT):
                    b = (tt * 128) // BS
                    pb = fw.rot('s_ps', 2)
                    for kc in range(KC):
                        fw.op('pe', lambda e, kc=kc, tt=tt, pb=pb: e.matmul(
                            bank(pb)[:, 0:SKW], hT[:, kc, tt * 128:(tt + 1) * 128], wv[:, kc, :],
                            start=(kc == 0), stop=(kc == KC - 1)),
                            reads=[wt] + (hT_reads(b) if kc == 0 else []), writes=[('ps', pb)])
                    fw.op('act', lambda e, tt=tt, pb=pb: e.copy(out=v_tm[:, tt + 1, :], in_=bank(pb)[:, 0:SKW]),
                          reads=[('ps', pb)], writes=[('s_v', tt + 1)])
                lb = NTB - 1
                fw.dma('sp', cc_s_in[:, 0:SKV * 128].rearrange("p (k t) -> p k t", k=SKV), kT2[:, :, T:T + 128], 's_x0',
                       reads=[('s_k', kh, lb) for kh in range(SKV)], writes=['ccs0'])
                fw.dma('sp', cc_s_in[:, SKV * 128:SKV * 128 + SKW], v_tm[:, NT, :], 's_x1',
                       reads=[('s_v', NT)], writes=['ccs1'])
                fw.collective(cc_s_in.ap().opt(), cc_s_out.ap().opt(), groups, 's', reads=['ccs0', 'ccs1'],
                              writes=['ccs_out'])
                fw.dma('sp', kT2[:, :, 0:128], cc_s_out[0:128, 0:SKV * 128].rearrange("p (k t) -> p k t", k=SKV), 's_x2',
                       reads=['ccs_out'], writes=[('s_k', kh, -1) for kh in range(SKV)])
                fw.dma('sp', v_tm[:, 0, :], cc_s_out[0:128, SKV * 128:SKV * 128 + SKW], 's_x3',
                       reads=['ccs_out'], writes=[('s_v', 0)])

                for kh in range(SKV):
                    fw.dma('sp', bias[:], swab_in[:, kh * SG:(kh + 1) * SG, :], 's_bias', writes=['s_bias'])

                    def epi_q(mg, msz, b, ps, pst):
                        p = mg // 128
                        fw.op('act', lambda e: e.activation(out=q2T[:, p, b * BS:(b + 1) * BS], in_=ps, func=AF.Copy,
                                                            scale=64.0 ** -0.5), reads=[pst], writes=[('s_q', p, b)])
                    lin_ws(W, cq0 + kh * SG * 64, SG * 64, epi_q, banks=(0, 1))

                    gs = 256 if SG >= 4 else 512
                    slot = lambda g_: (g_ % 2) * (SG // 2) + g_ // 2
                    half_major = [g_ for hf_ in (0, 1) for g_ in range(hf_, SG, 2)]
                    sk = sinks[:, j * c.SQH + kh * SG:j * c.SQH + (kh + 1) * SG]
                    sc_ps = PS[:, 2 * 512:2 * 512 + SG * gs].rearrange("p (g k) -> p g k", g=SG)[:, :, 0:256]
                    psr = [('ps', 2 + i) for i in range((SG * gs + 511) // 512)]

                    def stA(tt):
                        b = (tt * 128) // BS
                        u = tt % 2
                        kreads = [('s_k', kh, -1 if tt == 0 else ((tt - 1) * 128) // BS), ('s_k', kh, b)]
                        for g in half_major:
                            p, half = g // 2, g % 2
                            hs = slice(half * 64, (half + 1) * 64)
                            pb = 2 + (slot(g) * gs) // 512
                            off = (slot(g) * gs) % 512
                            fw.op('pe', lambda e: e.matmul(
                                bank(pb)[:, off:off + 256], q2T[hs, p, tt * 128:(tt + 1) * 128],
                                kT2[hs, kh, tt * 128:tt * 128 + 256], start=True, stop=True),
                                reads=[('s_q', p, b)] + kreads, writes=[('ps', pb)])
                        fw.op('dve', lambda e: e.tensor_tensor(out=s_sb[u][:], in0=sc_ps, in1=bias[:], op=ALU.add),
                              reads=psr + ['s_bias'], writes=[('s_s', u)])
                        if tt == 0:
                            fw.op('dve', lambda e: e.tensor_tensor(
                                out=s_sb[u][:], in0=s_sb[u][:], in1=prem[:, :].unsqueeze(1).to_broadcast([128, SG, 256]),
                                op=ALU.add), reads=[('s_s', u), 's_prem'], writes=[('s_s', u)])
                        fw.op('dve', lambda e: e.tensor_reduce(out=st[u][:, 0, :], in_=s_sb[u][:], axis=AX.X, op=ALU.max),
                              reads=[('s_s', u)], writes=[('s_st', u)])
                        fw.op('dve', lambda e: e.tensor_tensor(out=st[u][:, 0, :], in0=st[u][:, 0, :], in1=sk, op=ALU.max),
                              reads=[('s_st', u), 'c5'], writes=[('s_st', u)])
                        fw.op('dve', lambda e: e.tensor_scalar_mul(out=st[u][:, 1, :], in0=st[u][:, 0, :], scalar1=-1.0),
                              reads=[('s_st', u)], writes=[('s_st', u)])
                        fw.op('dve', lambda e: e.tensor_sub(out=st[u][:, 2, :], in0=sk, in1=st[u][:, 0, :]),
                              reads=[('s_st', u)], writes=[('s_st', u)])
                        fw.op('act', lambda e: e.activation(out=st[u][:, 3, :], in_=st[u][:, 2, :], func=AF.Exp),
                              reads=[('s_st', u)], writes=[('s_st', u)])
                        for g in range(SG):
                            fw.op('act', lambda e, g=g: e.activation(
                                out=s_sb[u][:, g, :], in_=s_sb[u][:, g, :], func=AF.Exp, bias=st[u][:, 1, g:g + 1],
                                accum_out=st[u][:, 4, g:g + 1]), reads=[('s_s', u), ('s_st', u)], writes=[('s_s', u), ('s_st', u)])

                    def stA2(tt):
                        u = tt % 2
                        fw.op('dve', lambda e: e.tensor_add(out=st[u][:, 5, :], in0=st[u][:, 4, :], in1=st[u][:, 3, :]),
                              reads=[('s_st', u)], writes=[('s_st', u)])
                        fw.op('dve', lambda e: e.reciprocal(out=st[u][:, 5, :], in_=st[u][:, 5, :]),
                              reads=[('s_st', u)], writes=[('s_st', u)])
                        fw.op('dve', lambda e: e.tensor_tensor(
                            out=pnb[u][:], in0=s_sb[u][:], in1=st[u][:, 5, :].unsqueeze(2).to_broadcast([128, SG, 256]),
                            op=ALU.mult), reads=[('s_s', u), ('s_st', u)], writes=[('s_pn', u)])

                    def stB(tt):
                        u = tt % 2
                        for i in range(SG * 2):
                            g, kt = i // 2, i % 2
                            tb_ = (i * 128) // 1024
                            off = (i * 128) % 1024
                            fw.op('pe', lambda e: e.transpose(
                                ptbank(tb_)[:, off:off + 128], pnb[u][:, g, kt * 128:(kt + 1) * 128], ident[:]),
                                reads=[('s_pn', u)], writes=[('pt', tb_)])
                        ntb_ = (SG * 2 * 128 + 1023) // 1024
                        for tb_ in range(ntb_):
                            n_i = min(8, SG * 2 - tb_ * 8)
                            fw.op('act' if tb_ == 0 else 'dve', lambda e: (e.copy if tb_ == 0 else e.tensor_copy)(
                                out=pT[u][:, tb_ * 8:tb_ * 8 + n_i, :],
                                in_=ptbank(tb_)[:, 0:n_i * 128].rearrange("p (i t) -> p i t", i=n_i)),
                                reads=[('pt', tb_)], writes=[('s_pT', u, tb_)])
                        for g in half_major:
                            p, half = g // 2, g % 2
                            for kt in range(2):
                                i = slot(g) * 2 + kt
                                fw.op('pe', lambda e: e.matmul(
                                    bank(half)[half * 64:(half + 1) * 64, p * 128:(p + 1) * 128],
                                    v_tm[:, tt + kt, kh * 64:(kh + 1) * 64], pT[u][:, i, :], start=(kt == 0), stop=(kt == 1)),
                                    reads=[('s_pT', u, i // 8), ('s_v', tt + kt)], writes=[('ps', half)])
                        for half in range(2):
                            hs = slice(half * 64, (half + 1) * 64)
                            fw.op('act', lambda e: e.copy(
                                out=ob[u][hs, :, :], in_=bank(half)[hs, 0:NP * 128].rearrange("p (n t) -> p n t", n=NP)),
                                reads=[('ps', half)], writes=[('s_ob', u, half)])
                        r0 = kh * SG * 64
                        fw.dma('sp', mixd[r0:r0 + NP * 128, tt * 128:(tt + 1) * 128].rearrange("(n p) t -> p n t", p=128),
                               ob[u][:], f's_ob{u}', reads=[('s_ob', u, 0), ('s_ob', u, 1)], writes=[('mixd', r0, tt)])

                    stA(0)
                    for tt in range(NT):
                        if tt + 1 < NT:
                            stA(tt + 1)
                        stA2(tt)
                        stB(tt)
                fw.barrier()
            return W, cxq, 0

        def out_proj(li, xsrc):
            GW = min(512, D)
            MC = GW // 128
            with ExitStack() as ps_:
                sbl = mk_sbl(ps_)
                xt = [sbl(f"o_x{i}", [128, MC, BS]) for i in range(3)]
                for b in range(NTB):
                    fw.dma('sp', hT[:, :, b * BS:(b + 1) * BS], mixd[:, b * BS:(b + 1) * BS].rearrange("(k p) t -> p k t", p=128),
                           f'o_mix{b}', writes=[('mix', b)])
                items = [(g, b) for g in range(D // GW) for b in range(NTB)]

                def xload(idx):
                    g, b = items[idx]
                    u = idx % 3
                    fw.dma('sp', xt[u][:], xsrc[g * GW:(g + 1) * GW, b * BS:(b + 1) * BS].rearrange("(k p) t -> p k t", p=128),
                           f'o_x{u}', writes=[('o_x', u, m) for m in range(MC)])
                xload(0)
                if len(items) > 1:
                    xload(1)
                W2d = w_out[li]
                wts = {0: wload(wcols(W2d, 0, GW), KC, GW)}
                for idx, (g, b) in enumerate(items):
                    u = idx % 3
                    if b == 0 and (g + 1) * GW < D:
                        wts[g + 1] = wload(wcols(W2d, (g + 1) * GW, GW), KC, GW)
                    wv, wt = wts[g]
                    for m in range(MC):
                        pb = fw.rot('o_ps', 4)
                        for kc in range(KC):
                            fw.op('pe', lambda e, kc=kc: e.matmul(
                                bank(pb)[:, 0:BS], wv[:, kc, m * 128:(m + 1) * 128], hT[:, kc, b * BS:(b + 1) * BS],
                                start=(kc == 0), stop=(kc == KC - 1)),
                                reads=[wt] + ([('mix', b)] if kc == 0 else []), writes=[('ps', pb)])
                        fw.op('dve', lambda e: e.tensor_tensor(out=xt[u][:, m, :], in0=bank(pb)[:, 0:BS], in1=xt[u][:, m, :],
                                                               op=ALU.add),
                              reads=[('ps', pb), ('o_x', u, m)], writes=[('o_x', u, m)])
                    fw.dma('sp', xs[g * GW:(g + 1) * GW, b * BS:(b + 1) * BS].rearrange("(k p) t -> p k t", p=128), xt[u][:],
                           f'o_x{u}', reads=[('o_x', u, m) for m in range(MC)], writes=[('o_x', u, m) for m in range(MC)])
                    if idx + 2 < len(items):
                        xload(idx + 2)
                fw.barrier()

        def mlp(li):
            TH, FG = c.TH, c.FG
            NBH = TH // BS
            FC = FG // 128
            with ExitStack() as ps_:
                sbl = mk_sbl(ps_)
                acc = sbl("m_acc", [128, KC, TH])
                uT = [sbl(f"m_uT{i}", [128, FC, TH], BF16) for i in range(2)]
                rt = [sbl(f"m_rt{i}", [128, BS]) for i in range(2)]
                for hf in range(T // TH):
                    t0 = hf * TH
                    fw.dma_group('sp', [(acc[:, kc, :], xs[kc * 128:(kc + 1) * 128, t0:t0 + TH]) for kc in range(KC)], 'm_acc',
                                 writes=[('m_acc', kc, bb) for kc in range(KC) for bb in range(NBH)])
                    for g in range(c.DFF // FG):
                        uu = g % 2
                        wv, wt = wload(wcols(w_up[li], g * FG, FG), KC, FG)
                        wv2, wt2 = wload(w_down[li][g * FG:(g + 1) * FG, :].rearrange("(k p) m -> p k m", p=128), FC, D)
                        for bb in range(NBH):
                            b = (t0 // BS) + bb
                            for fc in range(FC):
                                pb = fw.rot('m_pu', 3)
                                for kc in range(KC):
                                    fw.op('pe', lambda e, kc=kc, fc=fc, b=b, pb=pb, wv=wv: e.matmul(
                                        bank(pb)[:, 0:BS], wv[:, kc, fc * 128:(fc + 1) * 128], hT[:, kc, b * BS:(b + 1) * BS],
                                        start=(kc == 0), stop=(kc == KC - 1)),
                                        reads=[wt] + (hT_reads(b) if kc == 0 else []), writes=[('ps', pb)])
                                r = fw.rot('m_rt', 2)
                                fw.op('act', lambda e, r=r, pb=pb: e.activation(out=rt[r][:], in_=bank(pb)[:, 0:BS], func=AF.Relu),
                                      reads=[('ps', pb)], writes=[('m_rt', r)])
                                fw.op('act', lambda e, r=r, fc=fc, bb=bb, uu=uu: e.activation(
                                    out=uT[uu][:, fc, bb * BS:(bb + 1) * BS], in_=rt[r][:], func=AF.Square),
                                    reads=[('m_rt', r)], writes=[('m_uT', uu, fc, bb)])
                        for bb in range(NBH):
                            for dc in range(KC):
                                pb = 3 + fw.rot('m_pd', 3)
                                for fc in range(FC):
                                    fw.op('pe', lambda e, fc=fc, dc=dc, bb=bb, pb=pb, wv2=wv2, uu=uu: e.matmul(
                                        bank(pb)[:, 0:BS], wv2[:, fc, dc * 128:(dc + 1) * 128], uT[uu][:, fc, bb * BS:(bb + 1) * BS],
                                        start=(fc == 0), stop=(fc == FC - 1)),
                                        reads=[wt2, ('m_uT', uu, fc, bb)], writes=[('ps', pb)])
                                fw.op('dve', lambda e, dc=dc, bb=bb, pb=pb: e.tensor_tensor(
                                    out=acc[:, dc, bb * BS:(bb + 1) * BS], in0=bank(pb)[:, 0:BS], in1=acc[:, dc, bb * BS:(bb + 1) * BS],
                                    op=ALU.add), reads=[('ps', pb), ('m_acc', dc, bb)], writes=[('m_acc', dc, bb)])
                    fw.dma_group('sp', [(xs[kc * 128:(kc + 1) * 128, t0:t0 + TH], acc[:, kc, :]) for kc in range(KC)], 'm_acc',
                                 reads=[('m_acc', kc, bb) for kc in range(KC) for bb in range(NBH)],
                                 writes=[('m_acc', kc, bb) for kc in range(KC) for bb in range(NBH)])
                fw.barrier()

        stop = getattr(cfg, 'stop_after', None)
        pc = [0]

        def gate(fn):
            def g(*a, **k):
                pc[0] += 1
                if stop is not None and pc[0] > stop:
                    return (None, 0, 0)
                if getattr(cfg, 'scopes', False):
                    with nc.named_scope(f"P{pc[0]:02d}_{fn.__name__}"):
                        return fn(*a, **k)
                return fn(*a, **k)
            return g
        norm_phase, gla_layer, mem_attn, gla_finish, swa_layer, out_proj, mlp = map(
            gate, (norm_phase, gla_layer, mem_attn, gla_finish, swa_layer, out_proj, mlp))
        norm_phase(memT_in, c.MEM, c.c_memg, dst_tile=memnT, tokname='memn')
        for li in range(c.NL):
            xsrc = xT_in if li == 0 else xs
            norm_phase(xsrc, T, c.c_attng + li * KC, dst_tile=hT)
            j = li // 2
            if li % 2 == 0:
                W, cxq, cgo0 = gla_layer(li, j)
                mem_attn(li, W, cxq)
                gla_finish(li, j, W, cgo0)
            else:
                W, cxq = swa_layer(li, j)[:2]
                mem_attn(li, W, cxq)
            out_proj(li, xsrc)
            norm_phase(xs, T, c.c_mlpg + li * KC, dst_tile=hT)
            mlp(li)
        norm_phase(xs, T, c.c_fing, dst_dram=yT_out)
        fw.barrier()
    return nc


def _cols(v):
    return np.ascontiguousarray(np.asarray(v, np.float32).reshape(-1, 128).T)


def make_in_maps(cfg, inputs, seq_halves=2):
    c = cfg
    x = np.asarray(inputs['x'], np.float32)
    mem = np.asarray(inputs['mem'], np.float32)
    B, S = x.shape[0], x.shape[1]
    f = lambda k: np.ascontiguousarray(np.asarray(inputs[k], np.float32))
    bf = ml_dtypes.bfloat16
    ident = np.eye(128, dtype=np.float32).astype(bf)
    onesb = np.ones((128, 128), np.float32).astype(bf)
    onesf = np.ones((128, 128), np.float32)
    jj, ii = np.meshgrid(np.arange(128), np.arange(128), indexing='ij')
    maskbd = ((jj // 64 == ii // 64) & (jj <= ii)).astype(np.float32)
    cmask = np.broadcast_to((np.arange(c.T) % 64 != 0).astype(np.float32)[None, :], (128, c.T)).copy()
    slopes = np.exp2(-8.0 * np.arange(1, c.SQH + 1, dtype=np.float32) / c.SQH).astype(np.float32)
    q = np.arange(128)[:, None]
    kk = np.arange(256)[None, :]
    dist = (q + 128 - kk).astype(np.float32)
    valid = (dist >= 0) & (dist < 128)
    swab = np.where(valid[:, None, :], -slopes[None, :, None] * dist[:, None, :], np.float32(NEG_INF)).astype(np.float32)
    perm = np.array([kh * c.SG + (s_ % (c.SG // 2)) * 2 + s_ // (c.SG // 2) for kh in range(c.SKV) for s_ in range(c.SG)])
    swab = swab[:, perm, :]
    sinks_bc = np.ascontiguousarray(np.broadcast_to(f('sinks')[:, perm].reshape(1, -1), (128, f('sinks').size))) \
        if c.NS > 0 else np.zeros((128, c.SQH), np.float32)
    w_in_swa = f('w_in_swa') if c.NS > 0 else np.zeros((1, c.D, c.SIN), np.float32)
    shared = dict(w_in_gla=f('w_in_gla'), w_gate_up=f('w_gate_up'), w_in_swa=w_in_swa, w_mem_kv=f('w_mem_kv'),
                  w_out=f('w_out'), w_up=f('w_up'), w_down=f('w_down'), sinks_bc=sinks_bc, ident=ident, onesb=onesb,
                  onesf=onesf, maskbd=maskbd, cmask=cmask, swa_bias=np.ascontiguousarray(swab))
    ppb = np.zeros((128, c.NPP), np.float32)
    ppb[:, c.c_memg:c.c_memg + c.KC] = _cols(inputs['mem_norm_g'])
    for l in range(c.NL):
        ppb[:, c.c_attng + l * c.KC:c.c_attng + (l + 1) * c.KC] = _cols(inputs['attn_norm_g'][l])
        ppb[:, c.c_mlpg + l * c.KC:c.c_mlpg + (l + 1) * c.KC] = _cols(inputs['mlp_norm_g'][l])
    ppb[:, c.c_fing:c.c_fing + c.KC] = _cols(inputs['final_norm_g'])
    nb = c.GK // 128
    for g in range(c.NG):
        ppb[:, c.c_bg + g * nb:c.c_bg + (g + 1) * nb] = _cols(inputs['b_gate'][g])
        ppb[:, c.c_gon + g * c.DVC:c.c_gon + (g + 1) * c.DVC] = _cols(inputs['gla_out_norm_g'][g])
    in_maps = []
    for core in range(c.n_cores):
        b, half = core // seq_halves, core % seq_halves
        p = ppb.copy()
        p[:, c.c_flag] = float(half)
        prem = np.zeros((128, 256), np.float32)
        if half == 0:
            prem[:, 0:128] = NEG_INF
        m = dict(shared)
        m['xT'] = np.ascontiguousarray(x[b, half * c.T:(half + 1) * c.T, :].T)
        m['memT'] = np.ascontiguousarray(mem[b].T)
        m['pp'] = p
        m['premask'] = prem
        in_maps.append(m)
    return in_maps


_NC_CACHE = {}


def kernel(**inputs):
    cfg = Cfg()
    if 'nc' not in _NC_CACHE:
        _NC_CACHE['nc'] = build_program(cfg)
    nc = _NC_CACHE['nc']
    in_maps = make_in_maps(cfg, inputs)
    res = run_bass_kernel_spmd(nc, in_maps, core_ids=list(range(cfg.n_cores)))
    B, S = inputs['x'].shape[0], inputs['x'].shape[1]
    out = np.empty((B, S, cfg.D), np.float32)
    for core in range(cfg.n_cores):
        b, half = core // 2, core % 2
        out[b, half * cfg.T:(half + 1) * cfg.T, :] = res.results[core]['yT'].T
    return out
```

```python
import numpy as np
from contextlib import ExitStack
import ml_dtypes
import concourse.bass as bass
import concourse.mybir as mybir
from concourse.bass_utils import run_bass_kernel_spmd

F32 = mybir.dt.float32
BF16 = mybir.dt.bfloat16
AF = mybir.ActivationFunctionType
ALU = mybir.AluOpType
AX = mybir.AxisListType

RMS_EPS = 1e-5
NEG_INF = -1e30
GLA_TAU = 16.0


class Cfg:
    def __init__(s, D=2048, T=2048, NL=4, MEM=256, XH=4, GH=4, DK=256, DV=384, R=16,
                 SKV=3, SG=8, DFF=8192, n_cores=8):
        s.D, s.T, s.NL, s.MEM, s.XH, s.GH, s.DK, s.DV, s.R = D, T, NL, MEM, XH, GH, DK, DV, R
        s.SKV, s.SG, s.DFF, s.n_cores = SKV, SG, DFF, n_cores
        s.KC = D // 128
        s.XW = XH * 128
        s.GK, s.GV = GH * DK, GH * DV
        s.DKC, s.DVC = DK // 128, DV // 128
        s.GIN = 2 * s.GK + 2 * s.GV + R + s.XW
        s.SQH = SKV * SG
        s.SQW, s.SKW = s.SQH * 64, SKV * 64
        s.SIN = s.SQW + 2 * s.SKW + s.XW
        s.MIXW = D
        assert s.GV + s.XW == D and s.SQW + s.XW == D
        s.NT = T // 128
        s.BS = min(512, T)
        s.NTB = T // s.BS
        s.NCH = T // 64
        s.TH = min(1024, T)
        s.FG = 512
        s.NG = (NL + 1) // 2
        s.NS = NL // 2
        c = 0
        s.c_memg = c; c += s.KC
        s.c_attng = c; c += NL * s.KC
        s.c_mlpg = c; c += NL * s.KC
        s.c_fing = c; c += s.KC
        s.c_bg = c; c += s.NG * (s.GK // 128)
        s.c_gon = c; c += s.NG * s.DVC
        s.c_flag = c; c += 1
        s.NPP = c


class Fw:
    def __init__(self, nc, es):
        self.nc, self.es = nc, es
        self.E = {'pe': nc.tensor, 'act': nc.scalar, 'dve': nc.vector, 'pool': nc.gpsimd, 'sp': nc.sync}
        self.cnt = {e: 0 for e in self.E}
        self.sems = {('E_' + e): es.enter_context(nc.semaphore('E_' + e)) for e in self.E}
        self.dcnt = {}
        self.known = {e: {} for e in self.E}
        self.lastw, self.readers = {}, {}
        self.psi = {}

    def _sem(self, sn):
        if sn not in self.sems:
            self.sems[sn] = self.es.enter_context(self.nc.semaphore(sn))
            self.dcnt[sn] = 0
        return self.sems[sn]

    def _wait(self, e, deps):
        for sn, v in deps.items():
            if v <= 0 or self.known[e].get(sn, 0) >= v:
                continue
            if sn == 'E_pe' and e == 'pe':
                continue
            self.E[e].wait_ge(self._sem(sn), v)
            self.known[e][sn] = v

    def _deps(self, reads, writes):
        d = {}
        for r in reads:
            ev = self.lastw.get(r)
            if ev:
                d[ev[0]] = max(d.get(ev[0], 0), ev[1])
        for w in writes:
            ev = self.lastw.get(w)
            if ev:
                d[ev[0]] = max(d.get(ev[0], 0), ev[1])
            for sn, v in self.readers.get(w, {}).items():
                d[sn] = max(d.get(sn, 0), v)
        return d

    def _record(self, ev, reads, writes):
        sn, v = ev
        for r in reads:
            self.readers.setdefault(r, {})[sn] = v
        for w in writes:
            self.lastw[w] = ev
            self.readers[w] = {}

    def op(self, e, fn, reads=(), writes=()):
        self._wait(e, self._deps(reads, writes))
        ins = fn(self.E[e])
        self.cnt[e] += 1
        ins.then_inc(self.sems['E_' + e], 1)
        self._record(('E_' + e, self.cnt[e]), reads, writes)

    def dma(self, q, out, in_, sem, reads=(), writes=()):
        self.dma_group(q, [(out, in_)], sem, reads, writes)

    def dma_group(self, q, pairs, sem, reads=(), writes=()):
        sn = 'D_' + sem
        s = self._sem(sn)
        self._wait(q, self._deps(reads, writes))
        for out, in_ in pairs:
            ins = self.E[q].dma_start(out=out, in_=in_)
            self.dcnt[sn] += 16
            ins.then_inc(s, 16)
        self._record((sn, self.dcnt[sn]), reads, writes)

    def collective(self, ins_ap, outs_ap, groups, sem, reads=(), writes=()):
        sn = 'C_' + sem
        s = self._sem(sn)
        self._wait('pool', self._deps(reads, writes))
        ins = self.nc.gpsimd.collective_compute("AllGather", ALU.bypass, replica_groups=groups,
                                                ins=[ins_ap], outs=[outs_ap])
        self.dcnt[sn] += 1
        ins.then_inc(s, 1)
        self._record((sn, self.dcnt[sn]), reads, writes)

    def barrier(self):
        allev = {('E_' + e): self.cnt[e] for e in self.E}
        allev.update(self.dcnt)
        for e in self.E:
            self._wait(e, allev)
        self.lastw, self.readers = {}, {}

    def rot(self, name, n):
        i = self.psi.get(name, 0)
        self.psi[name] = (i + 1) % n
        return i


def build_program(cfg):
    c = cfg
    D, T, KC, BS, NTB, NT = c.D, c.T, c.KC, c.BS, c.NTB, c.NT
    nc = bass.Bass("TRN2", target_bir_lowering=False)

    def din(name, shape, dt=F32):
        return nc.dram_tensor(name, list(shape), dt, kind="ExternalInput")

    xT_in = din("xT", [D, T])
    memT_in = din("memT", [D, c.MEM])
    w_in_gla = din("w_in_gla", [c.NG, D, c.GIN])
    w_gate_up = din("w_gate_up", [c.NG, c.R, c.GK])
    w_in_swa = din("w_in_swa", [max(c.NS, 1), D, c.SIN])
    w_mem_kv = din("w_mem_kv", [c.NL, D, 2 * c.XW])
    w_out = din("w_out", [c.NL, c.MIXW, D])
    w_up = din("w_up", [c.NL, D, c.DFF])
    w_down = din("w_down", [c.NL, c.DFF, D])
    pp_in = din("pp", [128, c.NPP])
    sinks_in = din("sinks_bc", [128, max(c.NS, 1) * c.SQH])
    ident_in = din("ident", [128, 128], BF16)
    onesb_in = din("onesb", [128, 128], BF16)
    onesf_in = din("onesf", [128, 128])
    maskbd_in = din("maskbd", [128, 128])
    cmask_in = din("cmask", [128, T])
    swab_in = din("swa_bias", [128, c.SQH, 256])
    premask_in = din("premask", [128, 256])
    yT_out = nc.dram_tensor("yT", [D, T], F32, kind="ExternalOutput")

    xs = nc.dram_tensor("xs", [D, T], F32)
    mixd = nc.dram_tensor("mixd", [c.MIXW, T], BF16)
    oTd = nc.dram_tensor("oTd", [c.GV, T], F32)
    qcd = nc.dram_tensor("qcd", [c.GK, T], BF16)
    cc_g_in = nc.dram_tensor("cc_g_in", [c.GK, c.DV], F32)
    cc_g_out = nc.dram_tensor("cc_g_out", [2 * c.GK, c.DV], F32)
    SXW = c.SKV * 128 + c.SKW
    cc_s_in = nc.dram_tensor("cc_s_in", [128, SXW], BF16)
    cc_s_out = nc.dram_tensor("cc_s_out", [256, SXW], BF16)
    groups = [[2 * i, 2 * i + 1] for i in range(c.n_cores // 2)]

    with ExitStack() as es:
        fw = Fw(nc, es)
        sb = lambda name, shape, dt=F32: es.enter_context(nc.sbuf_tensor(name, list(shape), dt))
        uid = [0]

        def mk_sbl(stack):
            def f(name, shape, dt=F32):
                uid[0] += 1
                return stack.enter_context(nc.sbuf_tensor(f"{name}_{uid[0]}", list(shape), dt))
            return f
        hT = sb("hT", [128, KC, T], BF16)
        wslot = [sb(f"wslot{i}", [128, 8192], BF16) for i in range(2)]
        pp = sb("pp_sb", [128, c.NPP])
        negbg = sb("negbg", [128, c.NG * (c.GK // 128)])
        ident = sb("ident_sb", [128, 128], BF16)
        onesb = sb("onesb_sb", [128, 128], BF16)
        onesf = sb("onesf_sb", [128, 128])
        maskbd = sb("maskbd_sb", [128, 128])
        sinks = sb("sinks_sb", [128, max(c.NS, 1) * c.SQH])
        memnT = sb("memnT", [128, KC, c.MEM], BF16)
        PS = es.enter_context(nc.psum_tensor("PS", [128, 6 * 512], F32))
        PT = es.enter_context(nc.psum_tensor("PT", [128, 2048], BF16))

        def bank(i):
            return PS[:, i * 512:(i + 1) * 512]

        def ptbank(i):
            return PT[:, i * 1024:(i + 1) * 1024]

        for dst, src, nm in ((pp, pp_in, 'c0'), (ident, ident_in, 'c1'), (onesb, onesb_in, 'c2'),
                             (onesf, onesf_in, 'c3'), (maskbd, maskbd_in, 'c4'), (sinks, sinks_in, 'c5')):
            fw.dma('sp', dst[:], src[:, :], nm, writes=[nm])
        nbg = c.NG * (c.GK // 128)
        fw.op('dve', lambda e: e.tensor_scalar_mul(out=negbg[:], in0=pp[:, c.c_bg:c.c_bg + nbg], scalar1=-1.0),
              reads=['c0'], writes=['negbg'])
        fw.barrier()

        wctr = [0]

        def wload(view, k, m):
            s = wctr[0] % 2
            wctr[0] += 1
            assert k * m <= 8192
            wv = wslot[s][:, 0:k * m].rearrange("p (k m) -> p k m", k=k)
            fw.dma('pool', wv, view, f'w{s}', writes=[('w', s)])
            return wv, ('w', s)

        def wcols(W2d, c0, ncols, kc=KC):
            return W2d[:, c0:c0 + ncols].rearrange("(k p) m -> p k m", p=128)

        def norm_phase(src, N, gcol, dst_tile=None, dst_dram=None, tokname='hT'):
            bs = min(512, N)
            with ExitStack() as ps_:
                sbl = mk_sbl(ps_)
                xblk = [sbl(f"n_xblk{i}", [128, KC, bs]) for i in range(2)]
                sq = [sbl(f"n_sq{i}", [128, 4, bs], BF16) for i in range(2)]
                rs = [sbl(f"n_rs{i}", [128, bs]) for i in range(2)]
                ot = [sbl(f"n_ot{i}", [128, 4, bs]) for i in range(2)] if dst_dram is not None else None
                for b in range(N // bs):
                    s = b % 2
                    fw.dma('sp', xblk[s][:], src[:, b * bs:(b + 1) * bs].rearrange("(k p) t -> p k t", p=128),
                           f'nx{s}', writes=[('nx', s)])
                    pb = 4 + fw.rot('nps', 2)
                    for g4 in range(KC // 4):
                        s2 = fw.rot('nsq', 2)
                        fw.op('act', lambda e, s=s, s2=s2, g4=g4: e.activation(
                            out=sq[s2][:], in_=xblk[s][:, g4 * 4:(g4 + 1) * 4, :], func=AF.Square),
                            reads=[('nx', s)], writes=[('nsq', s2)])
                        for k in range(4):
                            kk = g4 * 4 + k
                            fw.op('pe', lambda e, s2=s2, k=k, kk=kk, pb=pb: e.matmul(
                                bank(pb)[:, 0:bs], onesb[:], sq[s2][:, k, :], start=(kk == 0), stop=(kk == KC - 1)),
                                reads=[('nsq', s2)], writes=[('ps', pb)])
                    fw.op('act', lambda e, s=s, pb=pb: e.activation(out=rs[s][:], in_=bank(pb)[:, 0:bs], func=AF.Sqrt,
                                                                   scale=1.0 / D, bias=RMS_EPS),
                          reads=[('ps', pb)], writes=[('nrs', s)])
                    fw.op('dve', lambda e, s=s: e.reciprocal(out=rs[s][:], in_=rs[s][:]),
                          reads=[('nrs', s)], writes=[('nrs', s)])
                    if dst_tile is not None:
                        for kc in range(KC):
                            fw.op('dve', lambda e, s=s, kc=kc, b=b: e.scalar_tensor_tensor(
                                out=dst_tile[:, kc, b * bs:(b + 1) * bs], in0=xblk[s][:, kc, :],
                                scalar=pp[:, gcol + kc:gcol + kc + 1], in1=rs[s][:], op0=ALU.mult, op1=ALU.mult),
                                reads=[('nx', s), ('nrs', s)], writes=[(tokname, kc, b)])
                    else:
                        for g4 in range(KC // 4):
                            so = fw.rot('not', 2)
                            for k in range(4):
                                kc = g4 * 4 + k
                                fw.op('dve', lambda e, s=s, kc=kc, k=k, so=so: e.scalar_tensor_tensor(
                                    out=ot[so][:, k, :], in0=xblk[s][:, kc, :],
                                    scalar=pp[:, gcol + kc:gcol + kc + 1], in1=rs[s][:], op0=ALU.mult, op1=ALU.mult),
                                    reads=[('nx', s), ('nrs', s)], writes=[('not', so)])
                            fw.dma('sp', dst_dram[g4 * 512:(g4 + 1) * 512, b * bs:(b + 1) * bs].rearrange(
                                "(k p) t -> p k t", p=128), ot[so][:], f'not{so}', reads=[('not', so)],
                                writes=[('yT', g4, b)])
                fw.barrier()

        def hT_reads(b):
            return [('hT', kc, b) for kc in range(KC)]

        def lin_ws(W2d, c0, ncols, epi, banks=(0, 1, 2, 3), src=None, src_reads=None, kc_n=KC, wview=None,
                   tbs=None):
            src = hT if src is None else src
            src_reads = hT_reads if src_reads is None else src_reads
            tbs = range(NTB) if tbs is None else tbs
            gw = 8192 // kc_n
            gw = min(gw, 512)
            for g0 in range(0, ncols, gw):
                gsz = min(gw, ncols - g0)
                wv, wt = wload(wcols(W2d, c0 + g0, gsz) if wview is None else wview(g0, gsz), kc_n, gsz)
                for b in tbs:
                    for m0 in range(0, gsz, 128):
                        msz = min(128, gsz - m0)
                        pb = banks[fw.rot(('lin', banks), len(banks))]
                        for kc in range(kc_n):
                            fw.op('pe', lambda e, kc=kc, m0=m0, msz=msz, b=b, pb=pb, wv=wv: e.matmul(
                                bank(pb)[0:msz, 0:BS], wv[:, kc, m0:m0 + msz], src[:, kc, b * BS:(b + 1) * BS],
                                start=(kc == 0), stop=(kc == kc_n - 1)),
                                reads=[wt] + (src_reads(b) if kc == 0 else []), writes=[('ps', pb)])
                        epi(g0 + m0, msz, b, bank(pb)[0:msz, 0:BS], ('ps', pb))

        def mem_kv(li, mkT, mv_tm):
            W = w_mem_kv[li]
            MT = c.MEM // 128
            for g0 in range(0, c.XW, 512):
                gsz = min(512, c.XW - g0)
                wv, wt = wload(wcols(W, g0, gsz), KC, gsz)
                for m0 in range(0, gsz, 128):
                    h = (g0 + m0) // 128
                    pb = fw.rot('mk', 2)
                    for kc in range(KC):
                        fw.op('pe', lambda e, kc=kc, m0=m0, pb=pb, wv=wv: e.matmul(
                            bank(pb)[:, 0:c.MEM], wv[:, kc, m0:m0 + 128], memnT[:, kc, :],
                            start=(kc == 0), stop=(kc == KC - 1)), reads=[wt, 'memnT'], writes=[('ps', pb)])
                    fw.op('act', lambda e, h=h, pb=pb: e.copy(out=mkT[:, h, :], in_=bank(pb)[:, 0:c.MEM]),
                          reads=[('ps', pb)], writes=[('mkT', h)])
            for g0 in range(0, c.XW, 512):
                gsz = min(512, c.XW - g0)
                wv, wt = wload(wcols(W, c.XW + g0, gsz), KC, gsz)
                for mt in range(MT):
                    pb = fw.rot('mk', 2)
                    for kc in range(KC):
                        fw.op('pe', lambda e, kc=kc, mt=mt, pb=pb, wv=wv, gsz=gsz: e.matmul(
                            bank(pb)[:, 0:gsz], memnT[:, kc, mt * 128:(mt + 1) * 128], wv[:, kc, :],
                            start=(kc == 0), stop=(kc == KC - 1)), reads=[wt, 'memnT'], writes=[('ps', pb)])
                    fw.op('act', lambda e, mt=mt, pb=pb, g0=g0, gsz=gsz: e.copy(
                        out=mv_tm[:, mt, g0:g0 + gsz], in_=bank(pb)[:, 0:gsz]),
                        reads=[('ps', pb)], writes=[('mv', mt, g0)])

        def mem_attn(li, Win2d, cxq):
            MT = c.MEM // 128
            with ExitStack() as ps_:
                sbl = mk_sbl(ps_)
                mkT = sbl("ma_mkT", [128, c.XH, c.MEM], BF16)
                mv_tm = sbl("ma_mv", [128, MT, c.XW], BF16)
                xqT = sbl("ma_xqT", [128, c.XH, T], BF16)
                s_sb = [sbl(f"ma_s{i}", [128, c.MEM]) for i in range(4)]
                pn = [sbl(f"ma_pn{i}", [128, c.MEM], BF16) for i in range(4)]
                st = [sbl(f"ma_st{i}", [128, 4]) for i in range(4)]
                pT = [sbl(f"ma_pT{i}", [128, MT, 128], BF16) for i in range(4)]
                ob = [sbl(f"ma_ob{i}", [128, 128], BF16) for i in range(4)]
                mem_kv(li, mkT, mv_tm)

                def epi_xq(mg, msz, b, ps, pst):
                    h = mg // 128
                    fw.op('act', lambda e: e.activation(out=xqT[:, h, b * BS:(b + 1) * BS], in_=ps, func=AF.Copy,
                                                        scale=128.0 ** -0.5),
                          reads=[pst], writes=[('xq', h, b)])
                lin_ws(Win2d, cxq, c.XW, epi_xq, banks=(4, 5))
                mv_reads = [('mv', mt, g0) for mt in range(MT) for g0 in range(0, c.XW, 512)]
                units = [(tt, h) for tt in range(NT) for h in range(c.XH)]
                NU = 4

                def stA(n):
                    tt, h = units[n]
                    b = (tt * 128) // BS
                    u = n % NU
                    pb = fw.rot('ma_ps', 3)
                    fw.op('pe', lambda e: e.matmul(
                        bank(pb)[:, 0:c.MEM], xqT[:, h, tt * 128:(tt + 1) * 128], mkT[:, h, :], start=True, stop=True),
                        reads=[('xq', h, b), ('mkT', h)], writes=[('ps', pb)])
                    fw.op('dve', lambda e: e.reduce_max(out=st[u][:, 0:1], in_=bank(pb)[:, 0:c.MEM], axis=AX.X),
                          reads=[('ps', pb)], writes=[('ma_st', u)])
                    fw.op('dve', lambda e: e.tensor_scalar_mul(out=st[u][:, 1:2], in0=st[u][:, 0:1], scalar1=-1.0),
                          reads=[('ma_st', u)], writes=[('ma_st', u)])
                    fw.op('act', lambda e: e.activation(out=s_sb[u][:], in_=bank(pb)[:, 0:c.MEM], func=AF.Exp,
                                                        bias=st[u][:, 1:2], accum_out=st[u][:, 2:3]),
                          reads=[('ps', pb), ('ma_st', u)], writes=[('ma_s', u), ('ma_st2', u)])
                    fw.op('dve', lambda e: e.reciprocal(out=st[u][:, 3:4], in_=st[u][:, 2:3]),
                          reads=[('ma_st2', u)], writes=[('ma_st3', u)])
                    fw.op('dve', lambda e: e.tensor_scalar_mul(out=pn[u][:], in0=s_sb[u][:], scalar1=st[u][:, 3:4]),
                          reads=[('ma_s', u), ('ma_st3', u)], writes=[('ma_pn', u)])

                def stB(n):
                    u = n % NU
                    tb_ = fw.rot('pt', 2)
                    for mt in range(MT):
                        fw.op('pe', lambda e, mt=mt: e.transpose(
                            ptbank(tb_)[:, mt * 128:(mt + 1) * 128], pn[u][:, mt * 128:(mt + 1) * 128], ident[:]),
                            reads=[('ma_pn', u)], writes=[('pt', tb_)])
                    fw.op('act', lambda e: e.copy(
                        out=pT[u][:], in_=ptbank(tb_)[:, 0:MT * 128].rearrange("p (m t) -> p m t", m=MT)),
                        reads=[('pt', tb_)], writes=[('ma_pT', u)])

                def stC(n):
                    tt, h = units[n]
                    u = n % NU
                    pb2 = 3 + fw.rot('ma_ps2', 3)
                    for mt in range(MT):
                        fw.op('pe', lambda e, mt=mt: e.matmul(
                            bank(pb2)[:, 0:128], mv_tm[:, mt, h * 128:(h + 1) * 128], pT[u][:, mt, :],
                            start=(mt == 0), stop=(mt == MT - 1)),
                            reads=[('ma_pT', u)] + mv_reads, writes=[('ps', pb2)])
                    fw.op('act', lambda e: e.copy(out=ob[u][:], in_=bank(pb2)[:, 0:128]),
                          reads=[('ps', pb2)], writes=[('ma_ob', u)])
                    r0 = c.MIXW - c.XW + h * 128
                    fw.dma('sp', mixd[r0:r0 + 128, tt * 128:(tt + 1) * 128], ob[u][:], f'ma_ob{u}',
                           reads=[('ma_ob', u)], writes=[('mixd', r0, tt)])

                for n in range(len(units) + 2):
                    if n < len(units):
                        stA(n)
                    if 0 <= n - 1 < len(units):
                        stB(n - 1)
                    if 0 <= n - 2 < len(units):
                        stC(n - 2)
                fw.barrier()

        def gla_layer(li, j):
            W = w_in_gla[j]
            cq0, ck0, cv0, cgo0 = 0, c.GK, 2 * c.GK, 2 * c.GK + c.GV
            cglr = 2 * c.GK + 2 * c.GV
            cxq = cglr + c.R
            DKC, DVC, DK, DV, NCH = c.DKC, c.DVC, c.DK, c.DV, c.NCH
            i16 = 1.0 / GLA_TAU
            with ExitStack() as ps_:
                sbl = mk_sbl(ps_)
                glr = sbl("g_glr", [c.R, T], BF16)
                wgu = sbl("g_wgu", [c.R, c.GK], BF16)
                cmask = sbl("g_cmask", [128, BS])
                lbuf = [sbl(f"g_l{i}", [128, BS]) for i in range(2)]
                cs = [sbl(f"g_cs{i}", [128, T]) for i in range(DKC)]
                etmp = [sbl(f"g_et{i}", [128, BS]) for i in range(1)]
                ebt = [sbl(f"g_eb{i}", [128, BS]) for i in range(1)]
                enbt = [sbl(f"g_enb{i}", [128, BS]) for i in range(1)]
                ekt = [sbl(f"g_ek{i}", [128, BS]) for i in range(1)]
                kot = [sbl(f"g_kot{i}", [128, BS], BF16) for i in range(2)]
                qct = [sbl(f"g_qct{i}", [128, BS], BF16) for i in range(2)]
                dec = sbl("g_dec", [128, DKC, NCH])
                ncl = sbl("g_ncl", [128, DKC, NCH])
                pn_ = sbl("g_pn", [128, DKC, NCH])
                sc1 = sbl("g_sc1", [128, NCH])
                sc2 = sbl("g_sc2", [128, NCH])
                q_in = sbl("g_qin", [128, DKC, T], BF16)
                k_in = sbl("g_kin", [128, DKC, T], BF16)
                k_out = sbl("g_kout", [128, NT, DK], BF16)
                v_tm = sbl("g_v", [128, NT, DV], BF16)
                S_f = [sbl(f"g_Sf{i}", [128, DKC, DV]) for i in range(2)]
                S_b = [sbl(f"g_Sb{i}", [128, DKC, DV], BF16) for i in range(2)]
                AT = [sbl(f"g_AT{i}", [128, 128], BF16) for i in range(2)]
                o_sb = [sbl(f"g_o{i}", [128, DVC, 128]) for i in range(2)]

                fw.dma('sp', cmask[:], cmask_in[:, 0:BS], 'g_cm', writes=['cmask'])
                fw.dma('pool', wgu[:], w_gate_up[j], 'g_wgu', writes=['wgu'])

                def epi_glr(mg, msz, b, ps, pst):
                    fw.op('act', lambda e: e.copy(out=glr[0:c.R, b * BS:(b + 1) * BS], in_=ps),
                          reads=[pst], writes=[('glr', b)])
                lin_ws(W, cglr, c.R, epi_glr)

                for h in range(c.GH):
                    for jj in range(DKC):
                        cg = h * DKC + jj
                        for b in range(NTB):
                            pb = fw.rot('g_ps', 4)
                            fw.op('pe', lambda e, cg=cg, b=b, pb=pb: e.matmul(
                                bank(pb)[:, 0:BS], wgu[0:c.R, cg * 128:(cg + 1) * 128], glr[0:c.R, b * BS:(b + 1) * BS],
                                start=True, stop=True), reads=['wgu', ('glr', b)], writes=[('ps', pb)])
                            u = 0
                            lb_ = fw.rot('g_l', 2)
                            bcol = j * (c.GK // 128) + cg
                            fw.op('act', lambda e, u=u, pb=pb, bcol=bcol: e.activation(
                                out=etmp[u][:], in_=bank(pb)[:, 0:BS], func=AF.Exp, scale=-1.0,
                                bias=negbg[:, bcol:bcol + 1]), reads=[('ps', pb), 'negbg'], writes=[('g_et', u)])
                            fw.op('act', lambda e, u=u, lb_=lb_: e.activation(
                                out=lbuf[lb_][:], in_=etmp[u][:], func=AF.Ln, bias=1.0),
                                reads=[('g_et', u)], writes=[('g_l', lb_)])
                            fw.op('dve', lambda e, jj=jj, b=b, lb_=lb_: e.tensor_tensor_scan(
                                out=cs[jj][:, b * BS:(b + 1) * BS], data0=cmask[:], data1=lbuf[lb_][:], initial=0.0,
                                op0=ALU.mult, op1=ALU.add),
                                reads=['cmask', ('g_l', lb_)], writes=[('g_cs', jj)])
                        csl = cs[jj][:, :].rearrange("p (n q) -> p n q", q=64)[:, :, 63]
                        fw.op('act', lambda e, jj=jj, csl=csl: e.activation(out=dec[:, jj, :], in_=csl, func=AF.Exp, scale=-i16),
                              reads=[('g_cs', jj)], writes=[('g_dec', jj)])
                        fw.op('dve', lambda e, jj=jj, csl=csl: e.tensor_scalar_mul(out=ncl[:, jj, :], in0=csl, scalar1=-i16),
                              reads=[('g_cs', jj)], writes=[('g_ncl', jj)])
                        fw.op('dve', lambda e, csl=csl: e.tensor_tensor_scan(
                            out=sc1[:], data0=onesf[:, 0:NCH], data1=csl, initial=0.0, op0=ALU.mult, op1=ALU.add),
                            reads=[('g_cs', jj), 'c3'], writes=['g_sc1'])
                        fw.op('dve', lambda e, csl=csl: e.tensor_sub(out=sc2[:], in0=sc1[:], in1=csl),
                              reads=['g_sc1', ('g_cs', jj)], writes=['g_sc2'])
                        fw.op('act', lambda e, jj=jj: e.activation(out=pn_[:, jj, :], in_=sc2[:], func=AF.Exp, scale=-i16),
                              reads=['g_sc2'], writes=[('g_pn', jj)])

                    def decay_blocks(jj, b):
                        u = 0
                        blk = cs[jj][:, b * BS:(b + 1) * BS]
                        fw.op('act', lambda e: e.activation(out=ebt[u][:], in_=blk, func=AF.Exp, scale=-i16),
                              reads=[('g_cs', jj)], writes=[('g_eb', u)])
                        fw.op('act', lambda e: e.activation(out=enbt[u][:], in_=blk, func=AF.Exp, scale=i16),
                              reads=[('g_cs', jj)], writes=[('g_enb', u)])
                        for n in range(BS // 64):
                            ng = b * (BS // 64) + n
                            fw.op('act', lambda e, n=n, ng=ng: e.activation(
                                out=ekt[u][:, n * 64:(n + 1) * 64], in_=cs[jj][:, ng * 64:(ng + 1) * 64], func=AF.Exp,
                                scale=i16, bias=ncl[:, jj, ng:ng + 1]),
                                reads=[('g_cs', jj), ('g_ncl', jj)], writes=[('g_ek', u, n)])
                        return u

                    def epi_q(mg, msz, b, ps, pst):
                        jj = mg // 128
                        u = decay_blocks(jj, b)
                        fw.op('dve', lambda e: e.scalar_tensor_tensor(
                            out=q_in[:, jj, b * BS:(b + 1) * BS], in0=ps, scalar=float(DK) ** -0.5, in1=ebt[u][:],
                            op0=ALU.mult, op1=ALU.mult), reads=[pst, ('g_eb', u)], writes=[('g_qin', jj, b)])
                        nb = BS // 64
                        v = fw.rot('g_qct', 2)
                        fw.op('dve', lambda e: e.tensor_tensor(
                            out=qct[v][:, :].rearrange("p (n q) -> p n q", q=64),
                            in0=q_in[:, jj, b * BS:(b + 1) * BS].rearrange("p (n q) -> p n q", q=64),
                            in1=pn_[:, jj, b * nb:(b + 1) * nb].unsqueeze(2).to_broadcast([128, nb, 64]), op=ALU.mult),
                            reads=[('g_qin', jj, b), ('g_pn', jj)], writes=[('g_qct', v)])
                        r0 = (h * DKC + jj) * 128
                        fw.dma('sp', qcd[r0:r0 + 128, b * BS:(b + 1) * BS], qct[v][:], f'g_qct{v}',
                               reads=[('g_qct', v)], writes=[('qcd', r0, b)])
                    lin_ws(W, cq0 + h * DK, DK, epi_q)

                    def epi_k(mg, msz, b, ps, pst):
                        jj = mg // 128
                        u = decay_blocks(jj, b)
                        fw.op('dve', lambda e: e.tensor_tensor(out=k_in[:, jj, b * BS:(b + 1) * BS], in0=ps, in1=enbt[u][:],
                                                               op=ALU.mult),
                              reads=[pst, ('g_enb', u)], writes=[('g_kin', jj, b)])
                        v = fw.rot('g_kot', 2)
                        fw.op('dve', lambda e: e.tensor_tensor(out=kot[v][:], in0=ps, in1=ekt[u][:], op=ALU.mult),
                              reads=[pst] + [('g_ek', u, n) for n in range(BS // 64)], writes=[('g_kot', v)])
                        for t4 in range(BS // 128):
                            tt = b * (BS // 128) + t4
                            tb_ = fw.rot('pt', 2)
                            fw.op('pe', lambda e, t4=t4, tb_=tb_: e.transpose(
                                ptbank(tb_)[:, 0:128], kot[v][:, t4 * 128:(t4 + 1) * 128], ident[:]),
                                reads=[('g_kot', v)], writes=[('pt', tb_)])
                            fw.op('act', lambda e, tt=tt, tb_=tb_: e.copy(out=k_out[:, tt, jj * 128:(jj + 1) * 128],
                                                                         in_=ptbank(tb_)[:, 0:128]),
                                  reads=[('pt', tb_)], writes=[('g_kout', tt, jj)])
                    lin_ws(W, ck0 + h * DK, DK, epi_k)

                    wv, wt = wload(wcols(W, cv0 + h * DV, DV), KC, DV)
                    for tt in range(NT):
                        b = (tt * 128) // BS
                        pb = fw.rot('g_ps', 4)
                        for kc in range(KC):
                            fw.op('pe', lambda e, kc=kc, tt=tt, pb=pb: e.matmul(
                                bank(pb)[:, 0:DV], hT[:, kc, tt * 128:(tt + 1) * 128], wv[:, kc, :],
                                start=(kc == 0), stop=(kc == KC - 1)),
                                reads=[wt] + (hT_reads(b) if kc == 0 else []), writes=[('ps', pb)])
                        fw.op('act', lambda e, tt=tt, pb=pb: e.copy(out=v_tm[:, tt, :], in_=bank(pb)[:, 0:DV]),
                              reads=[('ps', pb)], writes=[('g_v', tt)])

                    fw.op('dve', lambda e: e.memset(S_f[0][:], 0.0), writes=[('g_Sf', 0, jj) for jj in range(DKC)])
                    fw.op('dve', lambda e: e.memset(S_b[0][:], 0.0), writes=[('g_Sb', 0, jj) for jj in range(DKC)])
                    def emit_AT(tt):
                        b = (tt * 128) // BS
                        tsl = slice(tt * 128, (tt + 1) * 128)
                        pa = fw.rot('g_ps', 4)
                        for jj in range(DKC):
                            fw.op('pe', lambda e, jj=jj: e.matmul(
                                bank(pa)[:, 0:128], k_in[:, jj, tsl], q_in[:, jj, tsl], start=(jj == 0), stop=(jj == DKC - 1)),
                                reads=[('g_kin', jj, b), ('g_qin', jj, b)], writes=[('ps', pa)])
                        a = tt % 2
                        fw.op('dve', lambda e: e.tensor_tensor(out=AT[a][:], in0=bank(pa)[:, 0:128], in1=maskbd[:], op=ALU.mult),
                              reads=[('ps', pa), 'c4'], writes=[('g_AT', a)])

                    emit_AT(0)
                    for tt in range(NT):
                        b = (tt * 128) // BS
                        tsl = slice(tt * 128, (tt + 1) * 128)
                        if tt + 1 < NT:
                            emit_AT(tt + 1)
                        a = tt % 2
                        ob_ = fw.rot('g_o', 2)
                        po = 4 + fw.rot('g_po', 2)
                        for cc in range(2):
                            n = tt * 2 + cc
                            cur, nxt = n % 2, (n + 1) % 2
                            rows = slice(cc * 64, (cc + 1) * 64)
                            csl_ = slice(tt * 128 + cc * 64, tt * 128 + (cc + 1) * 64)
                            for jj in range(DKC):
                                pu = fw.rot('g_ps', 4)
                                fw.op('pe', lambda e, jj=jj: e.matmul(
                                    bank(pu)[:, 0:DV], k_out[rows, tt, jj * 128:(jj + 1) * 128], v_tm[rows, tt, :],
                                    start=True, stop=True),
                                    reads=[('g_kout', tt, jj), ('g_v', tt)], writes=[('ps', pu)])
                                fw.op('dve', lambda e, jj=jj: e.scalar_tensor_tensor(
                                    out=S_f[nxt][:, jj, :], in0=S_f[cur][:, jj, :], scalar=dec[:, jj, n:n + 1], in1=bank(pu)[:, 0:DV],
                                    op0=ALU.mult, op1=ALU.add),
                                    reads=[('g_Sf', cur, jj), ('g_dec', jj), ('ps', pu)], writes=[('g_Sf', nxt, jj)])
                                fw.op('act', lambda e, jj=jj: e.copy(out=S_b[nxt][:, jj, :], in_=S_f[nxt][:, jj, :]),
                                      reads=[('g_Sf', nxt, jj)], writes=[('g_Sb', nxt, jj)])
                            for dvc in range(DVC):
                                ocol = slice(dvc * 128 + cc * 64, dvc * 128 + (cc + 1) * 64)
                                for jj in range(DKC):
                                    fw.op('pe', lambda e, jj=jj: e.matmul(
                                        bank(po)[:, ocol], S_b[cur][:, jj, dvc * 128:(dvc + 1) * 128], q_in[:, jj, csl_],
                                        start=(jj == 0), stop=False),
                                        reads=[('g_Sb', cur, jj), ('g_qin', jj, b)], writes=[('ps', po)])
                                fw.op('pe', lambda e: e.matmul(
                                    bank(po)[:, ocol], v_tm[rows, tt, dvc * 128:(dvc + 1) * 128],
                                    AT[a][rows, cc * 64:(cc + 1) * 64], start=False, stop=True),
                                    reads=[('g_v', tt), ('g_AT', a)], writes=[('ps', po)])
                        fw.op('act', lambda e: e.copy(
                            out=o_sb[ob_][:], in_=bank(po)[:, 0:DVC * 128].rearrange("p (d t) -> p d t", d=DVC)),
                            reads=[('ps', po)], writes=[('g_o', ob_)])
                        fw.dma('sp', oTd[h * DV:(h + 1) * DV, tsl].rearrange("(d p) t -> p d t", p=128), o_sb[ob_][:],
                               f'g_o{ob_}', reads=[('g_o', ob_)], writes=[('oTd', h, tt)])
                    fin = NCH % 2
                    fw.dma('sp', cc_g_in[h * DK:(h + 1) * DK, :].rearrange("(j p) v -> p j v", p=128), S_f[fin][:],
                           'g_Sst', reads=[('g_Sf', fin, jj) for jj in range(DKC)], writes=[('ccg', h)])
                fw.collective(cc_g_in.ap().opt(), cc_g_out.ap().opt(), groups, 'g',
                              reads=[('ccg', h) for h in range(c.GH)], writes=['ccg_out'])
                fw.barrier()
            return W, cxq, cgo0

        def gla_finish(li, j, W, cgo0):
            DKC, DVC, DK, DV = c.DKC, c.DVC, c.DK, c.DV
            with ExitStack() as ps_:
                sbl = mk_sbl(ps_)
                S0f = sbl("f_S0f", [128, DKC, DV])
                S0b = sbl("f_S0b", [128, DKC, DV], BF16)
                qc = [sbl(f"f_qc{i}", [128, DKC, BS], BF16) for i in range(2)]
                ob = [sbl(f"f_ob{i}", [128, DVC, BS]) for i in range(2)]
                sq = [sbl(f"f_sq{i}", [128, DVC, BS], BF16) for i in range(2)]
                rs = [sbl(f"f_rs{i}", [128, BS]) for i in range(2)]
                sg = [sbl(f"f_sg{i}", [128, BS]) for i in range(2)]
                tm = [sbl(f"f_tm{i}", [128, BS]) for i in range(2)]
                mx = [sbl(f"f_mx{i}", [128, BS], BF16) for i in range(2)]
                units = [(h, b) for h in range(c.GH) for b in range(NTB)]
                wts = {}

                def stA(n):
                    h, b = units[n]
                    u = n % 2
                    bsl = slice(b * BS, (b + 1) * BS)
                    if b == 0:
                        fw.dma('sp', S0f[:], cc_g_out[h * DK:(h + 1) * DK, :].rearrange("(j p) v -> p j v", p=128), 'f_S0',
                               writes=['f_S0f'])
                        fw.op('dve', lambda e: e.tensor_scalar_mul(out=S0b[:], in0=S0f[:], scalar1=pp[:, c.c_flag:c.c_flag + 1]),
                              reads=['f_S0f'], writes=['f_S0b'])
                    fw.dma('sp', qc[u][:], qcd[h * DK:(h + 1) * DK, bsl].rearrange("(j p) t -> p j t", p=128), f'f_qc{u}',
                           writes=[('f_qc', u)])
                    fw.dma('sp', ob[u][:], oTd[h * DV:(h + 1) * DV, bsl].rearrange("(d p) t -> p d t", p=128), f'f_ob{u}',
                           writes=[('f_ob', u)])
                    for dvc in range(DVC):
                        pb = fw.rot('f_ps', 2)
                        for jj in range(DKC):
                            fw.op('pe', lambda e, jj=jj: e.matmul(
                                bank(pb)[:, 0:BS], S0b[:, jj, dvc * 128:(dvc + 1) * 128], qc[u][:, jj, :],
                                start=(jj == 0), stop=(jj == DKC - 1)),
                                reads=['f_S0b', ('f_qc', u)], writes=[('ps', pb)])
                        fw.op('dve', lambda e: e.tensor_tensor(
                            out=ob[u][:, dvc, :], in0=bank(pb)[:, 0:BS], in1=ob[u][:, dvc, :], op=ALU.add),
                            reads=[('ps', pb), ('f_ob', u)], writes=[('f_ob', u)])
                    fw.op('act', lambda e: e.activation(out=sq[u][:], in_=ob[u][:], func=AF.Square),
                          reads=[('f_ob', u)], writes=[('f_sq', u)])
                    pn = 2 + fw.rot('f_pn', 2)
                    for dvc in range(DVC):
                        fw.op('pe', lambda e: e.matmul(
                            bank(pn)[:, 0:BS], onesb[:], sq[u][:, dvc, :], start=(dvc == 0), stop=(dvc == DVC - 1)),
                            reads=[('f_sq', u)], writes=[('ps', pn)])
                    fw.op('act', lambda e: e.activation(out=rs[u][:], in_=bank(pn)[:, 0:BS], func=AF.Sqrt,
                                                        scale=1.0 / DV, bias=RMS_EPS),
                          reads=[('ps', pn)], writes=[('f_rs', u)])
                    fw.op('dve', lambda e: e.reciprocal(out=rs[u][:], in_=rs[u][:]),
                          reads=[('f_rs', u)], writes=[('f_rs', u)])

                def stB(n):
                    h, b = units[n]
                    u = n % 2
                    bsl = slice(b * BS, (b + 1) * BS)
                    if b == 0 and h + 1 < c.GH:
                        wts[h + 1] = wload(wcols(W, cgo0 + (h + 1) * DV, DV), KC, DV)
                    wv, wt = wts[h]
                    for dvc in range(DVC):
                        pg = 4 + fw.rot('f_pg', 2)
                        for kc in range(KC):
                            fw.op('pe', lambda e, kc=kc: e.matmul(
                                bank(pg)[:, 0:BS], wv[:, kc, dvc * 128:(dvc + 1) * 128], hT[:, kc, b * BS:(b + 1) * BS],
                                start=(kc == 0), stop=(kc == KC - 1)),
                                reads=[wt] + (hT_reads(b) if kc == 0 else []), writes=[('ps', pg)])
                        v = fw.rot('f_sg', 2)
                        fw.op('act', lambda e: e.activation(out=sg[v][:], in_=bank(pg)[:, 0:BS], func=AF.Silu),
                              reads=[('ps', pg)], writes=[('f_sg', v)])
                        gcol = c.c_gon + j * DVC + dvc
                        fw.op('dve', lambda e: e.scalar_tensor_tensor(
                            out=tm[v][:], in0=ob[u][:, dvc, :], scalar=pp[:, gcol:gcol + 1], in1=rs[u][:],
                            op0=ALU.mult, op1=ALU.mult), reads=[('f_ob', u), ('f_rs', u)], writes=[('f_tm', v)])
                        fw.op('dve', lambda e: e.tensor_tensor(out=mx[v][:], in0=tm[v][:], in1=sg[v][:], op=ALU.mult),
                              reads=[('f_tm', v), ('f_sg', v)], writes=[('f_mx', v)])
                        r0 = h * DV + dvc * 128
                        fw.dma('sp', mixd[r0:r0 + 128, bsl], mx[v][:], f'f_mx{v}', reads=[('f_mx', v)],
                               writes=[('mixd', r0, b)])

                wts[0] = wload(wcols(W, cgo0, DV), KC, DV)
                stA(0)
                for n in range(len(units)):
                    if n + 1 < len(units):
                        stA(n + 1)
                    stB(n)
                fw.barrier()

        def swa_layer(li, j):
            W = w_in_swa[j]
            cq0, ck0, cv0, cxq = 0, c.SQW, c.SQW + c.SKW, c.SQW + 2 * c.SKW
            SG, SKV, SKW = c.SG, c.SKV, c.SKW
            NP = SG // 2
            with ExitStack() as ps_:
                sbl = mk_sbl(ps_)
                kT2 = sbl("s_kT2", [128, SKV, 128 + T], BF16)
                v_tm = sbl("s_v", [128, NT + 1, SKW], BF16)
                q2T = sbl("s_q2T", [128, NP, T], BF16)
                bias = sbl("s_bias", [128, SG, 256])
                prem = sbl("s_prem", [128, 256])
                s_sb = [sbl(f"s_s{i}", [128, SG, 256]) for i in range(2)]
                pnb = [sbl(f"s_pn{i}", [128, SG, 256], BF16) for i in range(2)]
                st = [sbl(f"s_st{i}", [128, 6, SG]) for i in range(2)]
                pT = [sbl(f"s_pT{i}", [128, SG * 2, 128], BF16) for i in range(2)]
                ob = [sbl(f"s_ob{i}", [128, NP, 128], BF16) for i in range(2)]
                fw.dma('sp', prem[:], premask_in[:, :], 's_prem', writes=['s_prem'])

                for kh in range(SKV):
                    s = wctr[0] % 2
                    wctr[0] += 1
                    wv = wslot[s][:, 0:KC * 128].rearrange("p (k m) -> p k m", k=KC)
                    src_ = wcols(W, ck0 + kh * 64, 64)
                    fw.dma_group('pool', [(wv[:, :, 0:64], src_), (wv[:, :, 64:128], src_)], f'w{s}', writes=[('w', s)])
                    for b in range(NTB):
                        pb = fw.rot('s_ps', 2)
                        for kc in range(KC):
                            fw.op('pe', lambda e, kc=kc, b=b, pb=pb, wv=wv: e.matmul(
                                bank(pb)[:, 0:BS], wv[:, kc, :], hT[:, kc, b * BS:(b + 1) * BS],
                                start=(kc == 0), stop=(kc == KC - 1)),
                                reads=[('w', s)] + (hT_reads(b) if kc == 0 else []), writes=[('ps', pb)])
                        fw.op('act', lambda e, kh=kh, b=b, pb=pb: e.copy(out=kT2[:, kh, 128 + b * BS:128 + (b + 1) * BS],
                                                                        in_=bank(pb)[:, 0:BS]),
                              reads=[('ps', pb)], writes=[('s_k', kh, b)])
                wv, wt = wload(wcols(W, cv0, SKW), KC, SKW)
                for tt in range(NT):
                    b = (tt * 128) // BS
                    pb = fw.rot('s_ps', 2)
                    for kc in range(KC):
                        fw.op('pe', lambda e, kc=kc, tt=tt, pb=pb: e.matmul(
                            bank(pb)[:, 0:SKW], hT[:, kc, tt * 128:(tt + 1) * 128], wv[:, kc, :],
                            start=(kc == 0), stop=(kc == KC - 1)),
                            reads=[wt] + (hT_reads(b) if kc == 0 else []), writes=[('ps', pb)])
                    fw.op('act', lambda e, tt=tt, pb=pb: e.copy(out=v_tm[:, tt + 1, :], in_=bank(pb)[:, 0:SKW]),
                          reads=[('ps', pb)], writes=[('s_v', tt + 1)])
                lb = NTB - 1
                fw.dma('sp', cc_s_in[:, 0:SKV * 128].rearrange("p (k t) -> p k t", k=SKV), kT2[:, :, T:T + 128], 's_x0',
                       reads=[('s_k', kh, lb) for kh in range(SKV)], writes=['ccs0'])
                fw.dma('sp', cc_s_in[:, SKV * 128:SKV * 128 + SKW], v_tm[:, NT, :], 's_x1',
                       reads=[('s_v', NT)], writes=['ccs1'])
                fw.collective(cc_s_in.ap().opt(), cc_s_out.ap().opt(), groups, 's', reads=['ccs0', 'ccs1'],
                              writes=['ccs_out'])
                fw.dma('sp', kT2[:, :, 0:128], cc_s_out[0:128, 0:SKV * 128].rearrange("p (k t) -> p k t", k=SKV), 's_x2',
                       reads=['ccs_out'], writes=[('s_k', kh, -1) for kh in range(SKV)])
                fw.dma('sp', v_tm[:, 0, :], cc_s_out[0:128, SKV * 128:SKV * 128 + SKW], 's_x3',
                       reads=['ccs_out'], writes=[('s_v', 0)])

                for kh in range(SKV):
                    fw.dma('sp', bias[:], swab_in[:, kh * SG:(kh + 1) * SG, :], 's_bias', writes=['s_bias'])

                    def epi_q(mg, msz, b, ps, pst):
                        p = mg // 128
                        fw.op('act', lambda e: e.activation(out=q2T[:, p, b * BS:(b + 1) * BS], in_=ps, func=AF.Copy,
                                                            scale=64.0 ** -0.5), reads=[pst], writes=[('s_q', p, b)])
                    lin_ws(W, cq0 + kh * SG * 64, SG * 64, epi_q, banks=(0, 1))

                    gs = 256 if SG >= 4 else 512
                    slot = lambda g_: (g_ % 2) * (SG // 2) + g_ // 2
                    half_major = [g_ for hf_ in (0, 1) for g_ in range(hf_, SG, 2)]
                    sk = sinks[:, j * c.SQH + kh * SG:j * c.SQH + (kh + 1) * SG]
                    sc_ps = PS[:, 2 * 512:2 * 512 + SG * gs].rearrange("p (g k) -> p g k", g=SG)[:, :, 0:256]
                    psr = [('ps', 2 + i) for i in range((SG * gs + 511) // 512)]

                    def stA(tt):
                        b = (tt * 128) // BS
                        u = tt % 2
                        kreads = [('s_k', kh, -1 if tt == 0 else ((tt - 1) * 128) // BS), ('s_k', kh, b)]
                        for g in half_major:
                            p, half = g // 2, g % 2
                            hs = slice(half * 64, (half + 1) * 64)
                            pb = 2 + (slot(g) * gs) // 512
                            off = (slot(g) * gs) % 512
                            fw.op('pe', lambda e: e.matmul(
                                bank(pb)[:, off:off + 256], q2T[hs, p, tt * 128:(tt + 1) * 128],
                                kT2[hs, kh, tt * 128:tt * 128 + 256], start=True, stop=True),
                                reads=[('s_q', p, b)] + kreads, writes=[('ps', pb)])
                        fw.op('dve', lambda e: e.tensor_tensor(out=s_sb[u][:], in0=sc_ps, in1=bias[:], op=ALU.add),
                              reads=psr + ['s_bias'], writes=[('s_s', u)])
                        if tt == 0:
                            fw.op('dve', lambda e: e.tensor_tensor(
                                out=s_sb[u][:], in0=s_sb[u][:], in1=prem[:, :].unsqueeze(1).to_broadcast([128, SG, 256]),
                                op=ALU.add), reads=[('s_s', u), 's_prem'], writes=[('s_s', u)])
                        fw.op('dve', lambda e: e.tensor_reduce(out=st[u][:, 0, :], in_=s_sb[u][:], axis=AX.X, op=ALU.max),
                              reads=[('s_s', u)], writes=[('s_st', u)])
                        fw.op('dve', lambda e: e.tensor_tensor(out=st[u][:, 0, :], in0=st[u][:, 0, :], in1=sk, op=ALU.max),
                              reads=[('s_st', u), 'c5'], writes=[('s_st', u)])
                        fw.op('dve', lambda e: e.tensor_scalar_mul(out=st[u][:, 1, :], in0=st[u][:, 0, :], scalar1=-1.0),
                              reads=[('s_st', u)], writes=[('s_st', u)])
                        fw.op('dve', lambda e: e.tensor_sub(out=st[u][:, 2, :], in0=sk, in1=st[u][:, 0, :]),
                              reads=[('s_st', u)], writes=[('s_st', u)])
                        fw.op('act', lambda e: e.activation(out=st[u][:, 3, :], in_=st[u][:, 2, :], func=AF.Exp),
                              reads=[('s_st', u)], writes=[('s_st', u)])
                        for g in range(SG):
                            fw.op('act', lambda e, g=g: e.activation(
                                out=s_sb[u][:, g, :], in_=s_sb[u][:, g, :], func=AF.Exp, bias=st[u][:, 1, g:g + 1],
                                accum_out=st[u][:, 4, g:g + 1]), reads=[('s_s', u), ('s_st', u)], writes=[('s_s', u), ('s_st', u)])

                    def stA2(tt):
                        u = tt % 2
                        fw.op('dve', lambda e: e.tensor_add(out=st[u][:, 5, :], in0=st[u][:, 4, :], in1=st[u][:, 3, :]),
                              reads=[('s_st', u)], writes=[('s_st', u)])
                        fw.op('dve', lambda e: e.reciprocal(out=st[u][:, 5, :], in_=st[u][:, 5, :]),
                              reads=[('s_st', u)], writes=[('s_st', u)])
                        fw.op('dve', lambda e: e.tensor_tensor(
                            out=pnb[u][:], in0=s_sb[u][:], in1=st[u][:, 5, :].unsqueeze(2).to_broadcast([128, SG, 256]),
                            op=ALU.mult), reads=[('s_s', u), ('s_st', u)], writes=[('s_pn', u)])

                    def stB(tt):
                        u = tt % 2
                        for i in range(SG * 2):
                            g, kt = i // 2, i % 2
                            tb_ = (i * 128) // 1024
                            off = (i * 128) % 1024
                            fw.op('pe', lambda e: e.transpose(
                                ptbank(tb_)[:, off:off + 128], pnb[u][:, g, kt * 128:(kt + 1) * 128], ident[:]),
                                reads=[('s_pn', u)], writes=[('pt', tb_)])
                        ntb_ = (SG * 2 * 128 + 1023) // 1024
                        for tb_ in range(ntb_):
                            n_i = min(8, SG * 2 - tb_ * 8)
                            fw.op('act', lambda e: e.copy(
                                out=pT[u][:, tb_ * 8:tb_ * 8 + n_i, :],
                                in_=ptbank(tb_)[:, 0:n_i * 128].rearrange("p (i t) -> p i t", i=n_i)),
                                reads=[('pt', tb_)], writes=[('s_pT', u, tb_)])
                        for g in half_major:
                            p, half = g // 2, g % 2
                            for kt in range(2):
                                i = slot(g) * 2 + kt
                                fw.op('pe', lambda e: e.matmul(
                                    bank(half)[half * 64:(half + 1) * 64, p * 128:(p + 1) * 128],
                                    v_tm[:, tt + kt, kh * 64:(kh + 1) * 64], pT[u][:, i, :], start=(kt == 0), stop=(kt == 1)),
                                    reads=[('s_pT', u, i // 8), ('s_v', tt + kt)], writes=[('ps', half)])
                        for half in range(2):
                            hs = slice(half * 64, (half + 1) * 64)
                            fw.op('act', lambda e: e.copy(
                                out=ob[u][hs, :, :], in_=bank(half)[hs, 0:NP * 128].rearrange("p (n t) -> p n t", n=NP)),
                                reads=[('ps', half)], writes=[('s_ob', u, half)])
                        r0 = kh * SG * 64
                        fw.dma('sp', mixd[r0:r0 + NP * 128, tt * 128:(tt + 1) * 128].rearrange("(n p) t -> p n t", p=128),
                               ob[u][:], f's_ob{u}', reads=[('s_ob', u, 0), ('s_ob', u, 1)], writes=[('mixd', r0, tt)])

                    stA(0)
                    for tt in range(NT):
                        if tt + 1 < NT:
                            stA(tt + 1)
                        stA2(tt)
                        stB(tt)
                fw.barrier()
            return W, cxq, 0

        def out_proj(li, xsrc):
            GW = min(512, D)
            MC = GW // 128
            with ExitStack() as ps_:
                sbl = mk_sbl(ps_)
                xt = [sbl(f"o_x{i}", [128, MC, BS]) for i in range(3)]
                for b in range(NTB):
                    fw.dma('sp', hT[:, :, b * BS:(b + 1) * BS], mixd[:, b * BS:(b + 1) * BS].rearrange("(k p) t -> p k t", p=128),
                           f'o_mix{b}', writes=[('mix', b)])
                items = [(g, b) for g in range(D // GW) for b in range(NTB)]

                def xload(idx):
                    g, b = items[idx]
                    u = idx % 3
                    fw.dma('sp', xt[u][:], xsrc[g * GW:(g + 1) * GW, b * BS:(b + 1) * BS].rearrange("(k p) t -> p k t", p=128),
                           f'o_x{u}', writes=[('o_x', u, m) for m in range(MC)])
                xload(0)
                if len(items) > 1:
                    xload(1)
                W2d = w_out[li]
                wts = {0: wload(wcols(W2d, 0, GW), KC, GW)}
                for idx, (g, b) in enumerate(items):
                    u = idx % 3
                    if b == 0 and (g + 1) * GW < D:
                        wts[g + 1] = wload(wcols(W2d, (g + 1) * GW, GW), KC, GW)
                    wv, wt = wts[g]
                    for m in range(MC):
                        pb = fw.rot('o_ps', 4)
                        for kc in range(KC):
                            fw.op('pe', lambda e, kc=kc: e.matmul(
                                bank(pb)[:, 0:BS], wv[:, kc, m * 128:(m + 1) * 128], hT[:, kc, b * BS:(b + 1) * BS],
                                start=(kc == 0), stop=(kc == KC - 1)),
                                reads=[wt] + ([('mix', b)] if kc == 0 else []), writes=[('ps', pb)])
                        fw.op('dve', lambda e: e.tensor_tensor(out=xt[u][:, m, :], in0=bank(pb)[:, 0:BS], in1=xt[u][:, m, :],
                                                               op=ALU.add),
                              reads=[('ps', pb), ('o_x', u, m)], writes=[('o_x', u, m)])
                    fw.dma('sp', xs[g * GW:(g + 1) * GW, b * BS:(b + 1) * BS].rearrange("(k p) t -> p k t", p=128), xt[u][:],
                           f'o_x{u}', reads=[('o_x', u, m) for m in range(MC)], writes=[('o_x', u, m) for m in range(MC)])
                    if idx + 2 < len(items):
                        xload(idx + 2)
                fw.barrier()

        def mlp(li):
            TH, FG = c.TH, c.FG
            NBH = TH // BS
            FC = FG // 128
            with ExitStack() as ps_:
                sbl = mk_sbl(ps_)
                acc = sbl("m_acc", [128, KC, TH])
                uT = [sbl(f"m_uT{i}", [128, FC, TH], BF16) for i in range(2)]
                rt = [sbl(f"m_rt{i}", [128, BS]) for i in range(2)]
                for hf in range(T // TH):
                    t0 = hf * TH
                    fw.dma_group('sp', [(acc[:, kc, :], xs[kc * 128:(kc + 1) * 128, t0:t0 + TH]) for kc in range(KC)], 'm_acc',
                                 writes=[('m_acc', kc, bb) for kc in range(KC) for bb in range(NBH)])
                    for g in range(c.DFF // FG):
                        uu = g % 2
                        wv, wt = wload(wcols(w_up[li], g * FG, FG), KC, FG)
                        wv2, wt2 = wload(w_down[li][g * FG:(g + 1) * FG, :].rearrange("(k p) m -> p k m", p=128), FC, D)
                        for bb in range(NBH):
                            b = (t0 // BS) + bb
                            for fc in range(FC):
                                pb = fw.rot('m_pu', 3)
                                for kc in range(KC):
                                    fw.op('pe', lambda e, kc=kc, fc=fc, b=b, pb=pb, wv=wv: e.matmul(
                                        bank(pb)[:, 0:BS], wv[:, kc, fc * 128:(fc + 1) * 128], hT[:, kc, b * BS:(b + 1) * BS],
                                        start=(kc == 0), stop=(kc == KC - 1)),
                                        reads=[wt] + (hT_reads(b) if kc == 0 else []), writes=[('ps', pb)])
                                r = fw.rot('m_rt', 2)
                                fw.op('act', lambda e, r=r, pb=pb: e.activation(out=rt[r][:], in_=bank(pb)[:, 0:BS], func=AF.Relu),
                                      reads=[('ps', pb)], writes=[('m_rt', r)])
                                fw.op('act', lambda e, r=r, fc=fc, bb=bb, uu=uu: e.activation(
                                    out=uT[uu][:, fc, bb * BS:(bb + 1) * BS], in_=rt[r][:], func=AF.Square),
                                    reads=[('m_rt', r)], writes=[('m_uT', uu, fc, bb)])
                        for bb in range(NBH):
                            for dc in range(KC):
                                pb = 3 + fw.rot('m_pd', 3)
                                for fc in range(FC):
                                    fw.op('pe', lambda e, fc=fc, dc=dc, bb=bb, pb=pb, wv2=wv2, uu=uu: e.matmul(
                                        bank(pb)[:, 0:BS], wv2[:, fc, dc * 128:(dc + 1) * 128], uT[uu][:, fc, bb * BS:(bb + 1) * BS],
                                        start=(fc == 0), stop=(fc == FC - 1)),
                                        reads=[wt2, ('m_uT', uu, fc, bb)], writes=[('ps', pb)])
                                fw.op('dve', lambda e, dc=dc, bb=bb, pb=pb: e.tensor_tensor(
                                    out=acc[:, dc, bb * BS:(bb + 1) * BS], in0=bank(pb)[:, 0:BS], in1=acc[:, dc, bb * BS:(bb + 1) * BS],
                                    op=ALU.add), reads=[('ps', pb), ('m_acc', dc, bb)], writes=[('m_acc', dc, bb)])
                    fw.dma_group('sp', [(xs[kc * 128:(kc + 1) * 128, t0:t0 + TH], acc[:, kc, :]) for kc in range(KC)], 'm_acc',
                                 reads=[('m_acc', kc, bb) for kc in range(KC) for bb in range(NBH)],
                                 writes=[('m_acc', kc, bb) for kc in range(KC) for bb in range(NBH)])
                fw.barrier()

        stop = getattr(cfg, 'stop_after', None)
        pc = [0]

        def gate(fn):
            def g(*a, **k):
                pc[0] += 1
                if stop is not None and pc[0] > stop:
                    return (None, 0, 0)
                if getattr(cfg, 'scopes', False):
                    with nc.named_scope(f"P{pc[0]:02d}_{fn.__name__}"):
                        return fn(*a, **k)
                return fn(*a, **k)
            return g
        norm_phase, gla_layer, mem_attn, gla_finish, swa_layer, out_proj, mlp = map(
            gate, (norm_phase, gla_layer, mem_attn, gla_finish, swa_layer, out_proj, mlp))
        norm_phase(memT_in, c.MEM, c.c_memg, dst_tile=memnT, tokname='memn')
        for li in range(c.NL):
            xsrc = xT_in if li == 0 else xs
            norm_phase(xsrc, T, c.c_attng + li * KC, dst_tile=hT)
            j = li // 2
            if li % 2 == 0:
                W, cxq, cgo0 = gla_layer(li, j)
                mem_attn(li, W, cxq)
                gla_finish(li, j, W, cgo0)
            else:
                W, cxq = swa_layer(li, j)[:2]
                mem_attn(li, W, cxq)
            out_proj(li, xsrc)
            norm_phase(xs, T, c.c_mlpg + li * KC, dst_tile=hT)
            mlp(li)
        norm_phase(xs, T, c.c_fing, dst_dram=yT_out)
        fw.barrier()
    return nc


def _cols(v):
    return np.ascontiguousarray(np.asarray(v, np.float32).reshape(-1, 128).T)


def make_in_maps(cfg, inputs, seq_halves=2):
    c = cfg
    x = np.asarray(inputs['x'], np.float32)
    mem = np.asarray(inputs['mem'], np.float32)
    B, S = x.shape[0], x.shape[1]
    f = lambda k: np.ascontiguousarray(np.asarray(inputs[k], np.float32))
    bf = ml_dtypes.bfloat16
    ident = np.eye(128, dtype=np.float32).astype(bf)
    onesb = np.ones((128, 128), np.float32).astype(bf)
    onesf = np.ones((128, 128), np.float32)
    jj, ii = np.meshgrid(np.arange(128), np.arange(128), indexing='ij')
    maskbd = ((jj // 64 == ii // 64) & (jj <= ii)).astype(np.float32)
    cmask = np.broadcast_to((np.arange(c.T) % 64 != 0).astype(np.float32)[None, :], (128, c.T)).copy()
    slopes = np.exp2(-8.0 * np.arange(1, c.SQH + 1, dtype=np.float32) / c.SQH).astype(np.float32)
    q = np.arange(128)[:, None]
    kk = np.arange(256)[None, :]
    dist = (q + 128 - kk).astype(np.float32)
    valid = (dist >= 0) & (dist < 128)
    swab = np.where(valid[:, None, :], -slopes[None, :, None] * dist[:, None, :], np.float32(NEG_INF)).astype(np.float32)
    perm = np.array([kh * c.SG + (s_ % (c.SG // 2)) * 2 + s_ // (c.SG // 2) for kh in range(c.SKV) for s_ in range(c.SG)])
    swab = swab[:, perm, :]
    sinks_bc = np.ascontiguousarray(np.broadcast_to(f('sinks')[:, perm].reshape(1, -1), (128, f('sinks').size))) \
        if c.NS > 0 else np.zeros((128, c.SQH), np.float32)
    w_in_swa = f('w_in_swa') if c.NS > 0 else np.zeros((1, c.D, c.SIN), np.float32)
    shared = dict(w_in_gla=f('w_in_gla'), w_gate_up=f('w_gate_up'), w_in_swa=w_in_swa, w_mem_kv=f('w_mem_kv'),
                  w_out=f('w_out'), w_up=f('w_up'), w_down=f('w_down'), sinks_bc=sinks_bc, ident=ident, onesb=onesb,
                  onesf=onesf, maskbd=maskbd, cmask=cmask, swa_bias=np.ascontiguousarray(swab))
    ppb = np.zeros((128, c.NPP), np.float32)
    ppb[:, c.c_memg:c.c_memg + c.KC] = _cols(inputs['mem_norm_g'])
    for l in range(c.NL):
        ppb[:, c.c_attng + l * c.KC:c.c_attng + (l + 1) * c.KC] = _cols(inputs['attn_norm_g'][l])
        ppb[:, c.c_mlpg + l * c.KC:c.c_mlpg + (l + 1) * c.KC] = _cols(inputs['mlp_norm_g'][l])
    ppb[:, c.c_fing:c.c_fing + c.KC] = _cols(inputs['final_norm_g'])
    nb = c.GK // 128
    for g in range(c.NG):
        ppb[:, c.c_bg + g * nb:c.c_bg + (g + 1) * nb] = _cols(inputs['b_gate'][g])
        ppb[:, c.c_gon + g * c.DVC:c.c_gon + (g + 1) * c.DVC] = _cols(inputs['gla_out_norm_g'][g])
    in_maps = []
    for core in range(c.n_cores):
        b, half = core // seq_halves, core % seq_halves
        p = ppb.copy()
        p[:, c.c_flag] = float(half)
        prem = np.zeros((128, 256), np.float32)
        if half == 0:
            prem[:, 0:128] = NEG_INF
        m = dict(shared)
        m['xT'] = np.ascontiguousarray(x[b, half * c.T:(half + 1) * c.T, :].T)
        m['memT'] = np.ascontiguousarray(mem[b].T)
        m['pp'] = p
        m['premask'] = prem
        in_maps.append(m)
    return in_maps


_NC_CACHE = {}


def kernel(**inputs):
    cfg = Cfg()
    if 'nc' not in _NC_CACHE:
        _NC_CACHE['nc'] = build_program(cfg)
    nc = _NC_CACHE['nc']
    in_maps = make_in_maps(cfg, inputs)
    res = run_bass_kernel_spmd(nc, in_maps, core_ids=list(range(cfg.n_cores)))
    B, S = inputs['x'].shape[0], inputs['x'].shape[1]
    out = np.empty((B, S, cfg.D), np.float32)
    for core in range(cfg.n_cores):
        b, half = core // 2, core % 2
        out[b, half * cfg.T:(half + 1) * cfg.T, :] = res.results[core]['yT'].T
    return out
```
